# Optimizing a Trainium2 kernel written in Bass

```python
import math
import jax, jax.numpy as jnp
from jax import lax
import numpy as np

D_MODEL = 1024
BATCH = 4
SEQ = 4096
DEPTH = 2

H_A = 8
DQK_A = D_MODEL // 16
DV_A = D_MODEL // 8
QK_A = H_A * DQK_A
V_A = H_A * DV_A
CHUNK = 64
CONV_K = 4
H_B = 8
HD_B = D_MODEL // 8
W_B = H_B * HD_B
QBLK = 128
D_FF = 4 * D_MODEL
EPS = 1e-6
IN_SIZES = (QK_A, QK_A, V_A, V_A, H_A, H_A, W_B, W_B, W_B, D_MODEL, D_MODEL)
IN_COLS = sum(IN_SIZES)

kernel_name = "hybrid_mlstm_stickbreaking_block"


def rms_norm(x, g):
    xf = x.astype(jnp.float32)
    y = xf * lax.rsqrt(jnp.mean(xf * xf, axis=-1, keepdims=True) + EPS)
    return (y * g.astype(jnp.float32)).astype(x.dtype)


def head_rms_norm(x, g):
    xf = x.astype(jnp.float32)
    y = xf * lax.rsqrt(jnp.mean(xf * xf, axis=-1, keepdims=True) + EPS)
    return y * g.astype(jnp.float32)


def causal_conv(x, w):
    S = x.shape[1]
    xp = jnp.pad(x, ((0, 0), (CONV_K - 1, 0), (0, 0)))
    y = xp[:, 0:S] * w[0]
    for j in range(1, CONV_K):
        y = y + xp[:, j:j + S] * w[j]
    return y


def mlstm_chunkwise(q, k, v, i_pre, f_pre):
    B, S, H, _ = q.shape
    NC = S // CHUNK
    to_chunks = lambda a: jnp.moveaxis(a.astype(jnp.float32).reshape(B, NC, CHUNK, H, -1), 3, 1)
    q = to_chunks(q)
    k = to_chunks(k) * (DQK_A ** -0.5)
    v = to_chunks(v)
    gate = lambda a: jnp.moveaxis(a.astype(jnp.float32).reshape(B, NC, CHUNK, H), 3, 1)
    logf = jax.nn.log_sigmoid(gate(f_pre))
    ig = gate(i_pre)
    b = jnp.cumsum(logf, axis=-1)
    b_last = b[..., -1]

    a = b_last[..., None] - b + ig
    m_loc = jnp.max(a, axis=-1)
    w = jnp.exp(a - m_loc[..., None])
    S_loc = jnp.einsum('bhcs,bhcsk,bhcsv->bhckv', w, k, v)
    n_loc = jnp.einsum('bhcs,bhcsk->bhck', w, k)

    def step(carry, inp):
        S_prev, n_prev, m_prev = carry
        S_l, n_l, m_l, bl = inp
        m_new = jnp.maximum(bl + m_prev, m_l)
        sp = jnp.exp(bl + m_prev - m_new)
        sl = jnp.exp(m_l - m_new)
        S_new = sp[..., None, None] * S_prev + sl[..., None, None] * S_l
        n_new = sp[..., None] * n_prev + sl[..., None] * n_l
        return (S_new, n_new, m_new), (S_prev, n_prev, m_prev)

    init = (jnp.zeros_like(S_loc[:, :, 0]), jnp.zeros_like(n_loc[:, :, 0]), jnp.zeros_like(m_loc[:, :, 0]))
    xs = (jnp.moveaxis(S_loc, 2, 0), jnp.moveaxis(n_loc, 2, 0), jnp.moveaxis(m_loc, 2, 0), jnp.moveaxis(b_last, 2, 0))
    _, (S0, n0, m0) = lax.scan(step, init, xs)
    S0 = jnp.moveaxis(S0, 0, 2)
    n0 = jnp.moveaxis(n0, 0, 2)
    m0 = jnp.moveaxis(m0, 0, 2)

    causal = jnp.tril(jnp.ones((CHUNK, CHUNK), dtype=bool))
    D = jnp.where(causal, b[..., :, None] - b[..., None, :] + ig[..., None, :], -jnp.inf)
    b_inter = b + m0[..., None]
    m_t = jnp.maximum(b_inter, jnp.max(D, axis=-1))
    P = jnp.exp(D - m_t[..., None]) * jnp.einsum('bhctk,bhcsk->bhcts', q, k)
    inter = jnp.exp(b_inter - m_t)
    num = jnp.einsum('bhcts,bhcsv->bhctv', P, v) + inter[..., None] * jnp.einsum('bhctk,bhckv->bhctv', q, S0)
    den = jnp.sum(P, axis=-1) + inter * jnp.einsum('bhctk,bhck->bhct', q, n0)
    h = num / jnp.maximum(jnp.abs(den), jnp.exp(-m_t))[..., None]
    return jnp.moveaxis(h, 1, 3).reshape(B, S, H, DV_A)


def stick_breaking(q, k, v):
    S = q.shape[1]
    scale = HD_B ** -0.5
    outs = []
    for blk in range(S // QBLK):
        t0, t1 = blk * QBLK, (blk + 1) * QBLK
        z = jnp.einsum('bqhd,bshd->bhqs', q[:, t0:t1], k[:, :t1]) * scale
        t_idx = t0 + jnp.arange(QBLK)
        s_idx = jnp.arange(t1)
        strict = s_idx[None, :] < t_idx[:, None]
        log_keep = jnp.where(strict, jax.nn.log_sigmoid(-z), 0.0)
        suffix = lax.cumsum(log_keep, axis=3, reverse=True) - log_keep
        A = jnp.where(strict, jnp.exp(jax.nn.log_sigmoid(z) + suffix), 0.0)
        outs.append(jnp.einsum('bhqs,bshd->bqhd', A, v[:, :t1]))
    return jnp.concatenate(outs, axis=1)


def setup_inputs(seed: int = 0) -> dict:
    key = jax.random.key(seed)
    ks = jax.random.split(key, 15)
    n = jax.random.normal
    f32 = jnp.float32
    x = n(ks[0], (BATCH, SEQ, D_MODEL), f32)
    norm_mix_g = 1.0 + 0.1 * n(ks[1], (DEPTH, D_MODEL), f32)
    w_in = n(ks[2], (DEPTH, D_MODEL, IN_COLS), f32) * D_MODEL ** -0.5
    b_if = jnp.concatenate([0.5 * n(ks[3], (DEPTH, H_A), f32),
                            3.0 + 0.5 * n(ks[4], (DEPTH, H_A), f32)], axis=-1)
    b_gate = 0.02 * n(ks[5], (DEPTH, 2 * D_MODEL), f32)
    conv_w = n(ks[6], (DEPTH, CONV_K, 2 * QK_A), f32) * CONV_K ** -0.5
    mlstm_norm_g = 1.0 + 0.1 * n(ks[7], (DEPTH, V_A), f32)
    sb_q_norm_g = 1.0 + 0.1 * n(ks[8], (DEPTH, HD_B), f32)
    sb_k_norm_g = 1.0 + 0.1 * n(ks[9], (DEPTH, HD_B), f32)
    w_out = n(ks[10], (DEPTH, D_MODEL, D_MODEL), f32) * D_MODEL ** -0.5
    norm_mlp_g = 1.0 + 0.1 * n(ks[11], (DEPTH, D_MODEL), f32)
    w_up = n(ks[12], (DEPTH, D_MODEL, D_FF), f32) * D_MODEL ** -0.5
    w_down = n(ks[13], (DEPTH, D_FF, D_MODEL), f32) * D_FF ** -0.5
    return {"x": x, "norm_mix_g": norm_mix_g, "w_in": w_in, "b_if": b_if, "b_gate": b_gate,
            "conv_w": conv_w, "mlstm_norm_g": mlstm_norm_g, "sb_q_norm_g": sb_q_norm_g,
            "sb_k_norm_g": sb_k_norm_g, "w_out": w_out, "norm_mlp_g": norm_mlp_g,
            "w_up": w_up, "w_down": w_down}


def reference(x, norm_mix_g, w_in, b_if, b_gate, conv_w, mlstm_norm_g, sb_q_norm_g, sb_k_norm_g,
              w_out, norm_mlp_g, w_up, w_down):
    B, S, _ = x.shape
    split_idx = np.cumsum(IN_SIZES)[:-1].tolist()
    for l in range(DEPTH):
        h = rms_norm(x, norm_mix_g[l])
        p = h @ w_in[l]
        q_a, k_a, v_a, o_a, i_a, f_a, q_b, k_b, v_b, g_a, g_b = jnp.split(p, split_idx, axis=-1)
        qk_a = jax.nn.silu(causal_conv(jnp.concatenate([q_a, k_a], axis=-1), conv_w[l]))
        q_a, k_a = qk_a[..., :QK_A], qk_a[..., QK_A:]
        i_pre = i_a.astype(jnp.float32) + b_if[l, :H_A].astype(jnp.float32)
        f_pre = f_a.astype(jnp.float32) + b_if[l, H_A:].astype(jnp.float32)
        h_a = mlstm_chunkwise(q_a.reshape(B, S, H_A, DQK_A), k_a.reshape(B, S, H_A, DQK_A),
                              v_a.reshape(B, S, H_A, DV_A), i_pre, f_pre)
        h_a = head_rms_norm(h_a, mlstm_norm_g[l].reshape(H_A, DV_A)).reshape(B, S, V_A)
        y_a = jax.nn.sigmoid(o_a.astype(jnp.float32)) * h_a
        qn = head_rms_norm(q_b.reshape(B, S, H_B, HD_B), sb_q_norm_g[l])
        kn = head_rms_norm(k_b.reshape(B, S, H_B, HD_B), sb_k_norm_g[l])
        y_b = stick_breaking(qn, kn, v_b.reshape(B, S, H_B, HD_B).astype(jnp.float32)).reshape(B, S, W_B)
        gates = jax.nn.sigmoid(jnp.concatenate([g_a, g_b], axis=-1).astype(jnp.float32)
                               + b_gate[l].astype(jnp.float32))
        y = gates[..., :D_MODEL] * y_a + gates[..., D_MODEL:] * y_b
        x = x + y.astype(x.dtype) @ w_out[l]
        h2 = rms_norm(x, norm_mlp_g[l])
        x = x + jnp.square(jax.nn.relu(h2 @ w_up[l])) @ w_down[l]
    return x
```

```python
import math
from contextlib import ExitStack

import numpy as np
import concourse.bass as bass
import concourse.mybir as mybir
from concourse.bass_utils import run_bass_kernel_spmd

F32 = mybir.dt.float32
BF16 = mybir.dt.bfloat16
AF = mybir.ActivationFunctionType
ALU = mybir.AluOpType
AX = mybir.AxisListType

D = 1024
NH = 8
EPS = 1e-6
QA, KA, VA, OA, IA, FA, QB, KB, VB, GA, GB = 0, 512, 1024, 2048, 3072, 3080, 3088, 4112, 5136, 6160, 7184
INC = 8208
DFF = 4096
NEG = -30000.0
GATE_GROUP = -1
import os
PC_STAGES = os.environ.get('PC_STAGES', 'fmb')


class Sched:
    def __init__(self, nc, st, n_dma_sems=22):
        self.nc = nc
        self.st = st
        self.eng = {'pe': nc.tensor, 'act': nc.scalar, 'dve': nc.vector, 'pool': nc.gpsimd, 'sp': nc.sync}
        self.sem = {}
        self.cnt = {}
        self.nphase = 0
        for e in ['pe', 'act', 'dve', 'pool']:
            self.sem[e] = st.enter_context(nc.semaphore('s_%s_0' % e))
            self.cnt[e] = 0
        self.dsem = [st.enter_context(nc.semaphore('d%d' % i)) for i in range(n_dma_sems)]
        self.bsem = [st.enter_context(nc.semaphore('b%d' % i)) for i in range(6)]
        self.bcnt = [0] * 6
        self.bnext = 0
        self.bg = {}
        self.dcnt = [0] * n_dma_sems
        self.dnext = 0
        self.npool0 = n_dma_sems - 6
        self.dnext_pool = self.npool0
        self.known = {e: {} for e in self.eng}
        self.last_w = {}
        self.readers = {}
        self.log = None
        self._cur = None

    def _wait(self, e, tk):
        key, val, src = tk
        if self.known[e].get(key, 0) >= val:
            return
        if self.log is not None and self._cur is not None:
            self._cur.append((key, val))
        if isinstance(key, str):
            sem = self.sem[key]
        elif isinstance(key, tuple):
            sem = self.bsem[key[1]]
        else:
            sem = self.dsem[key]
        self.eng[e].wait_ge(sem, val)
        self.known[e][key] = val

    def _deps(self, e, reads, writes):
        for r in reads:
            w = self.last_w.get(r)
            if w is not None:
                self._wait(e, w)
            w = self.bg.get(r)
            if w is not None:
                self._wait(e, w)
        for r in writes:
            w = self.last_w.get(r)
            if w is not None and (w[2] != e or e != 'pe'):
                self._wait(e, w)
            for rd in self.readers.get(r, {}).values():
                if rd[2] != e or e != 'pe':
                    self._wait(e, rd)

    def _commit(self, tk, reads, writes):
        for r in reads:
            self.readers.setdefault(r, {})[tk[2]] = tk
        for r in writes:
            self.last_w[r] = tk
            self.readers[r] = {}

    def op(self, e, fn, reads=(), writes=()):
        if self.log is not None:
            self._cur = []
        self._deps(e, reads, writes)
        inst = fn(self.eng[e])
        self.cnt[e] += 1
        if self.log is not None:
            self.log.append((e, self.cnt[e], list(reads), list(writes), self._cur))
            self._cur = None
        inst.then_inc(self.sem[e], 1)
        tk = (e, self.cnt[e], e)
        self._commit(tk, reads, writes)
        return tk

    def dma(self, q, out, in_, reads=(), writes=()):
        if q == 'pool':
            j = self.dnext_pool
            self.dnext_pool = self.npool0 + (self.dnext_pool + 1 - self.npool0) % (len(self.dsem) - self.npool0)
        else:
            j = self.dnext
            self.dnext = (self.dnext + 1) % self.npool0
        if self.dcnt[j] > 0:
            self._wait(q, (j, self.dcnt[j], 'dma'))
        self._deps(q, reads, writes)
        inst = self.eng[q].dma_start(out=out, in_=in_)
        self.dcnt[j] += 16
        inst.then_inc(self.dsem[j], 16)
        tk = (j, self.dcnt[j], 'dma%d' % j)
        self._commit(tk, reads, writes)
        return tk

    def dma_bg(self, q, out, in_, key):
        j = self.bnext
        self.bnext = (self.bnext + 1) % len(self.bsem)
        if self.bcnt[j] > 0:
            self._wait(q, (('b', j), self.bcnt[j], 'bg'))
        inst = self.eng[q].dma_start(out=out, in_=in_)
        self.bcnt[j] += 16
        inst.then_inc(self.bsem[j], 16)
        self.bg[key] = (('b', j), self.bcnt[j], 'bg')

    def _all_tickets(self):
        tks = [(e, self.cnt[e], e) for e in self.cnt if self.cnt[e] > 0]
        tks += [(j, self.dcnt[j], 'dma') for j in range(len(self.dsem)) if self.dcnt[j] > 0]
        return tks

    def barrier(self):
        tks = self._all_tickets()
        for e in self.eng:
            for tk in tks:
                if tk[2] == e:
                    continue
                self._wait(e, tk)
        self.last_w = {}
        self.readers = {}

    def new_phase(self):
        self.barrier()
        self.nphase += 1
        for e in ['pe', 'act', 'dve', 'pool']:
            self.sem[e] = self.st.enter_context(self.nc.semaphore('s_%s_%d' % (e, self.nphase)))
            self.cnt[e] = 0
        for e in self.known:
            self.known[e] = {k: v for k, v in self.known[e].items() if not isinstance(k, str)}

    def finish(self, e='sp'):
        for tk in self._all_tickets():
            self._wait(e, tk)
        for j in range(len(self.bsem)):
            if self.bcnt[j] > 0:
                self._wait(e, (('b', j), self.bcnt[j], 'bg'))


def build(S, L, dbg=False, stop_after=None):
    NT = S // 128
    NB = S // 512
    nc = bass.Bass('TRN2', target_bir_lowering=False)
    dt = nc.dram_tensor
    x_d = dt('x', [S, D], F32, kind='ExternalInput').ap()
    g1_d = dt('norm_mix_g', [L, D], F32, kind='ExternalInput').ap()
    win_d = dt('w_in', [L, D, INC], F32, kind='ExternalInput').ap()
    bif_d = dt('b_if', [L, 16], F32, kind='ExternalInput').ap()
    bg_d = dt('b_gate', [L, 2 * D], F32, kind='ExternalInput').ap()
    cw_d = dt('conv_w', [L, 4, D], F32, kind='ExternalInput').ap()
    gm_d = dt('mlstm_norm_g', [L, D], F32, kind='ExternalInput').ap()
    gq_d = dt('sb_q_norm_g', [L, 128], F32, kind='ExternalInput').ap()
    gk_d = dt('sb_k_norm_g', [L, 128], F32, kind='ExternalInput').ap()
    wout_d = dt('w_out', [L, D, D], F32, kind='ExternalInput').ap()
    g2_d = dt('norm_mlp_g', [L, D], F32, kind='ExternalInput').ap()
    wup_d = dt('w_up', [L, D, DFF], F32, kind='ExternalInput').ap()
    wdn_d = dt('w_down', [L, DFF, D], F32, kind='ExternalInput').ap()
    out_d = dt('out', [S, D], F32, kind='ExternalOutput').ap()
    sk = 'ExternalOutput' if dbg else 'Internal'
    xmid_d = dt('xmid', [S, D], F32, kind=sk).ap()
    xl_d = dt('xl', [S, D], F32, kind=sk).ap()
    va_d = dt('va', [S, D], BF16, kind=sk).ap()
    vb_d = dt('vb', [S, D], BF16, kind=sk).ap()
    qlo_d = dt('qlo', [4, 128, S], BF16, kind=sk).ap()
    qhi_d = dt('qhi', [4, 128, S], BF16, kind=sk).ap()
    ka_d = dt('ka', [4, 128, S], BF16, kind=sk).ap()
    gaT_d = dt('gaT', [8, 128, S], BF16, kind=sk).ap()
    gbT_d = dt('gbT', [8, 128, S], BF16, kind=sk).ap()
    qbT_d = dt('qbT', [8, 128, S], BF16, kind=sk).ap()
    kbT_d = dt('kbT', [8, 128, S], BF16, kind=sk).ap()
    yaT_d = dt('yaT', [8, 128, S], BF16, kind=sk).ap()
    winb_d = dt('winb', [D, INC], BF16, kind='Internal').ap()
    woutb_d = dt('woutb', [D, D], BF16, kind='Internal').ap()
    wupb_d = dt('wupb', [D, DFF], BF16, kind='Internal').ap()
    wdnb_d = dt('wdnb', [DFF, D], BF16, kind='Internal').ap()

    with ExitStack() as st:
        S_ = Sched(nc, st)
        op, dma = S_.op, S_.dma

        uid = [0]

        def sb(stack, name, shape, dtp):
            uid[0] += 1
            return stack.enter_context(nc.sbuf_tensor('%s_%d' % (name, uid[0]), shape, dtp))

        def ps(stack, name, shape, dtp):
            uid[0] += 1
            return stack.enter_context(nc.psum_tensor('%s_%d' % (name, uid[0]), shape, dtp))

        R1 = sb(st, 'R1', [128, 8, S], BF16)
        R2w = max(8 * S, 32 * 1024)
        R2 = sb(st, 'R2', [128, R2w], BF16)
        yT = R2[:, 0:8 * S].rearrange('p (c t) -> p c t', c=8)
        wdn = R2[:, 0:32 * 1024].rearrange('p (f n) -> p f n', f=32)
        hT = R1
        ident = sb(st, 'ident', [128, 128], BF16)
        cf = sb(st, 'cf', [128, 128], F32)
        triU = sb(st, 'triU', [128, 128], F32)
        onesf = sb(st, 'onesf', [128, 128], F32)
        triNeg = sb(st, 'triNeg', [128, 128], BF16)
        ones_bf = sb(st, 'ones_bf', [128, 128], BF16)
        negones = sb(st, 'negones', [128, 128], BF16)
        sel = sb(st, 'sel', [128, 4, 128], BF16)
        maskneg = sb(st, 'maskneg', [128, 4, 512], BF16)
        gqk = sb(st, 'gqk', [128, 2], F32)
        bgT = sb(st, 'bgT', [128, 16], F32)
        gmT = sb(st, 'gmT', [128, 8], F32)
        cwT = sb(st, 'cwT', [128, 8, 4], F32)
        bifB = sb(st, 'bifB', [128, 16], F32)
        gates = sb(st, 'gates', [128, NT, 2, 8], F32)
        decB = sb(st, 'decB', [128, NT, 4], F32)

        op('pool', lambda e: e.memset(cf[:], 1.0), writes=['cf'])
        op('pool', lambda e: e.affine_select(out=cf[:], in_=cf[:], pattern=[[1, 128]], compare_op=ALU.is_equal,
                                            fill=0.0, base=0, channel_multiplier=-1), reads=['cf'], writes=['cf'])
        op('dve', lambda e: e.tensor_copy(ident[:], cf[:]), reads=['cf'], writes=['ident'])
        op('pool', lambda e: e.memset(onesf[:], 1.0), writes=['onesf'])
        op('pool', lambda e: e.affine_select(out=triU[:], in_=onesf[:], pattern=[[1, 128]], compare_op=ALU.is_ge,
                                            fill=0.0, base=0, channel_multiplier=-1), reads=['onesf'], writes=['triU'])
        op('dve', lambda e: e.memset(cf[:], -1.0), reads=['ident'], writes=['cf'])
        op('pool', lambda e: e.affine_select(out=cf[:], in_=cf[:], pattern=[[-1, 128]], compare_op=ALU.is_ge,
                                            fill=0.0, base=0, channel_multiplier=1), reads=['cf'], writes=['cf'])
        op('dve', lambda e: e.tensor_copy(triNeg[:], cf[:]), reads=['cf'], writes=['triNeg'])
        op('dve', lambda e: e.memset(ones_bf[:], 1.0), writes=['ones_bf'])
        op('dve', lambda e: e.memset(negones[:], -1.0), writes=['negones'])
        op('dve', lambda e: e.memset(sel[:], 0.0), writes=['sel'])
        for (k, rows, val) in ((0, (0, 32), -1.0), (1, (64, 96), -1.0), (2, (0, 32), 1.0), (3, (64, 96), 1.0)):
            for r in rows:
                op('dve', lambda e: e.memset(sel[r:r + 1, k, :], val), writes=['sel'])
        mst = ExitStack()
        mtmp = sb(mst, 'mtmp', [128, 512], F32)
        for k in range(4):
            op('pool', lambda e: e.memset(mtmp[:], 0.0), reads=['mtmp'], writes=['mtmp'])
            op('pool', lambda e: e.affine_select(out=mtmp[:], in_=mtmp[:], pattern=[[1, 512]], compare_op=ALU.is_gt,
                                                fill=NEG, base=-128 * k, channel_multiplier=-1),
               reads=['mtmp'], writes=['mtmp'])
            op('dve', lambda e: e.tensor_copy(maskneg[:, k, :], mtmp[:]), reads=['mtmp'], writes=['maskneg'])
        S_.barrier()
        mst.close()

        for l in range(L):
            src_d = x_d if l == 0 else xl_d
            dst_d = out_d if l == L - 1 else xl_d
            S_.new_phase()
            dma('sp', bifB[:], bif_d[l:l + 1, :].broadcast_to([128, 16]), writes=['bifB'])
            with nc.allow_non_contiguous_dma(reason='tiny param transposes'):
                dma('sp', bgT[:], bg_d[l, :].rearrange('(c p) -> p c', p=128), writes=['bgT'])
                dma('sp', gqk[:, 0:1], gq_d[l, :].rearrange('(p o) -> p o', o=1), writes=['gqk'])
                dma('sp', gqk[:, 1:2], gk_d[l, :].rearrange('(p o) -> p o', o=1), writes=['gqk'])
                dma('sp', gmT[:], gm_d[l, :].rearrange('(c p) -> p c', p=128), writes=['gmT'])
                for jj in range(4):
                    dma('sp', cwT[:, :, jj], cw_d[l, jj, :].rearrange('(c p) -> p c', p=128), writes=['cwT'])
            op('dve', lambda e: e.tensor_scalar(out=gqk[:, 0:1], in0=gqk[:, 0:1], scalar1=128.0 ** -0.5, scalar2=None, op0=ALU.mult),
               reads=['gqk'], writes=['gqk'])

            def run_skewed(make_gen, ntiles):
                active = []
                nxt = 0
                while nxt < ntiles or active:
                    for gen in list(active):
                        try:
                            next(gen)
                        except StopIteration:
                            active.remove(gen)
                    if nxt < ntiles:
                        gen = make_gen(nxt)
                        nxt += 1
                        try:
                            next(gen)
                            active.append(gen)
                        except StopIteration:
                            pass

            with ExitStack() as ph:
                gB = sb(ph, 'gB', [128, D], F32)
                dma('sp', gB[:], g1_d[l:l + 1, :].broadcast_to([128, D]), writes=['gB'])
                xt = [sb(ph, 'xt%d' % i, [128, D], F32) for i in range(4)]
                junk = sb(ph, 'junk', [128, D], BF16)
                hn = [sb(ph, 'hn%d' % i, [128, D], BF16) for i in range(2)]
                stt = [sb(ph, 'stt%d' % i, [128, 4], F32) for i in range(4)]
                tp = [ps(ph, 'tpA%d' % i, [128, 8, 128], BF16) for i in range(2)]

                def a_tile(i):
                    b = i % 2
                    b4 = i % 4
                    dma('sp', xt[b4][:], src_d[i * 128:(i + 1) * 128, :], writes=[('xt', b4)])
                    op('dve', lambda e: e.scalar_tensor_tensor(out=junk[:], in0=xt[b4][:], scalar=1.0, in1=xt[b4][:],
                                                               op0=ALU.mult, op1=ALU.mult, accum_out=stt[b4][:, 0:1]),
                       reads=[('xt', b4)], writes=['junk', ('stt', b4)])
                    yield
                    op('act', lambda e: e.activation(out=stt[b4][:, 1:2], in_=stt[b4][:, 0:1], func=AF.Sqrt,
                                                     scale=1.0 / D, bias=EPS), reads=[('stt', b4)], writes=[('stt', b4)])
                    yield
                    op('dve', lambda e: e.reciprocal(stt[b4][:, 2:3], stt[b4][:, 1:2]), reads=[('stt', b4)], writes=[('stt', b4)])
                    op('dve', lambda e: e.scalar_tensor_tensor(out=hn[b][:], in0=xt[b4][:], scalar=stt[b4][:, 2:3], in1=gB[:],
                                                               op0=ALU.mult, op1=ALU.mult),
                       reads=[('xt', b4), ('stt', b4), 'gB'], writes=[('hn', b)])
                    yield
                    for k in range(8):
                        op('pe', lambda e: e.transpose(tp[b][:, k, :], hn[b][:, k * 128:(k + 1) * 128], ident[:]),
                           reads=[('hn', b), 'ident'], writes=[('tp', b)])
                    op('act', lambda e: e.activation(out=hT[:, :, i * 128:(i + 1) * 128], in_=tp[b][:], func=AF.Copy),
                       reads=[('tp', b)], writes=[('hT', i // 4)])

                run_skewed(a_tile, NT)
            if stop_after == 'A':
                break

            S_.new_phase()
            winb_keys = [('winb', r) for r in range(8)]
            with ExitStack() as ph:
                wb = [sb(ph, 'wb%d' % i, [128, 8, 512], BF16) for i in range(2)]
                pB = [ps(ph, 'pB%d' % i, [128, 512], F32) for i in range(3)]
                tpq = ps(ph, 'tpq', [128, 4, 128], BF16)
                pF = [ps(ph, 'pF%d' % i, [128, 512], F32) for i in range(2)]
                pI = ps(ph, 'pI', [128, 512], F32)
                pI2 = ps(ph, 'pI2', [128, 512], F32)
                vst = [sb(ph, 'vst%d' % i, [128, 512], BF16) for i in range(2)]
                sq = sb(ph, 'sq', [128, 512], F32)
                st4 = sb(ph, 'st4', [128, 12], F32)
                qnb = [sb(ph, 'qnb%d' % i, [128, 4, 128], BF16) for i in range(3)]
                qTs = [sb(ph, 'qTs%d' % i, [128, 4, 128], BF16) for i in range(2)]
                wf = [sb(ph, 'wf%d' % i, [128, 8, 128], BF16) for i in range(6)]
                pre2 = [sb(ph, 'pre2_%d' % i, [128, 3 + 512], F32) for i in range(2)]
                acc2 = [sb(ph, 'acc2_%d' % i, [128, 512], F32) for i in range(2)]
                qlo_s = [sb(ph, 'qlo_s%d' % i, [128, 512], BF16) for i in range(2)]
                qhi_s = [sb(ph, 'qhi_s%d' % i, [128, 512], BF16) for i in range(2)]
                so = sb(ph, 'so', [128, 512], F32)
                sg = sb(ph, 'sg', [128, 512], F32)
                gas = [sb(ph, 'gas%d' % i, [128, 512], BF16) for i in range(2)]
                wif = sb(ph, 'wif', [128, 8, 16], BF16)
                pif = [sb(ph, 'pif%d' % i, [128, 16], F32) for i in range(2)]
                t8 = [sb(ph, 't8%d' % i, [128, 40], F32) for i in range(2)]
                cum = [sb(ph, 'cum%d' % i, [128, 16], F32) for i in range(2)]

                if l == 0:
                    with nc.allow_non_contiguous_dma(reason='small gate weight rows'):
                        dma('pool', wif[:], win_d[l][:, IA:IA + 16].rearrange('(kc p) n -> p kc n', p=128), writes=['wif'])
                else:
                    with nc.allow_non_contiguous_dma(reason='small gate weight rows'):
                        dma('sp', wif[:], winb_d[:, IA:IA + 16].rearrange('(kc p) n -> p kc n', p=128), reads=winb_keys, writes=['wif'])

                def gate_s1(i):
                    j = i % 2
                    for kc in range(8):
                        op('pe', lambda e: e.matmul(pI[:, 0:16], lhsT=hT[:, kc, i * 128:(i + 1) * 128], rhs=wif[:, kc, :],
                                                    start=(kc == 0), stop=(kc == 7)),
                           reads=[('hT', i // 4), 'wif'], writes=['pI'])
                    op('dve', lambda e: e.tensor_tensor(out=pif[j][:], in0=pI[:, 0:16], in1=bifB[:], op=ALU.add),
                       reads=['pI', 'bifB'], writes=[('pif', j)])
                    op('act', lambda e: e.activation(out=t8[j][:, 0:8], in_=pif[j][:, 8:16], func=AF.Exp, scale=-1.0),
                       reads=[('pif', j)], writes=[('t8a', j)])
                    op('act', lambda e: e.activation(out=t8[j][:, 8:16], in_=t8[j][:, 0:8], func=AF.Ln, bias=1.0),
                       reads=[('t8a', j)], writes=[('t8b', j)])

                def gate_s2(i):
                    j = i % 2
                    op('pe', lambda e: e.matmul(pI2[:, 0:8], lhsT=triU[:], rhs=t8[j][:, 8:16], start=True, stop=True),
                       reads=[('t8b', j), 'triU'], writes=['pI2'])
                    op('pe', lambda e: e.matmul(pI2[:, 8:16], lhsT=onesf[:], rhs=t8[j][:, 8:16], start=True, stop=True),
                       reads=[('t8b', j), 'onesf'], writes=['pI2'])
                    op('act', lambda e: e.activation(out=cum[j][:], in_=pI2[:, 0:16], func=AF.Copy), reads=['pI2'], writes=[('cum', j)])
                    op('dve', lambda e: e.tensor_tensor(out=t8[j][:, 16:24], in0=cum[j][:, 0:8], in1=cum[j][:, 8:16], op=ALU.subtract),
                       reads=[('cum', j)], writes=[('t8c', j)])
                    op('dve', lambda e: e.tensor_tensor(out=t8[j][:, 24:32], in0=t8[j][:, 16:24], in1=pif[j][:, 0:8], op=ALU.add),
                       reads=[('t8c', j), ('pif', j)], writes=[('t8d', j)])
                    op('act', lambda e: e.activation(out=gates[:, i, 0, :], in_=t8[j][:, 24:32], func=AF.Exp, bias=-math.log(8.0)),
                       reads=[('t8d', j)], writes=[('gates', i, 0)])
                    op('act', lambda e: e.activation(out=gates[:, i, 1, :], in_=t8[j][:, 16:24], func=AF.Exp, scale=-1.0),
                       reads=[('t8c', j)], writes=[('gates', i, 1)])
                    op('act', lambda e: e.activation(out=t8[j][:, 32:40], in_=cum[j][:, 8:16], func=AF.Exp, scale=-1.0),
                       reads=[('cum', j)], writes=[('t8e', j)])
                    op('dve', lambda e: e.tensor_copy(decB[0:64, i, :], t8[j][0:64, 32:40:2]), reads=[('t8e', j)], writes=[('decB', i, 0)])
                    op('dve', lambda e: e.tensor_copy(decB[64:128, i, :], t8[j][64:128, 33:40:2]), reads=[('t8e', j)], writes=[('decB', i, 1)])

                groups = []
                for kind, c0 in (('va', VA), ('qb', QB), ('kb', KB), ('vb', VB)):
                    groups.append((kind, c0, 0))
                    groups.append((kind, c0 + 512, 1))
                n = 0
                nqk = 0
                nfin = 0
                pend = []
                def load_wb(gi):
                    c0g = groups[gi][1]
                    wbb_ = gi % 2
                    if l == 0:
                        dma('pool', wb[wbb_][:], win_d[l][:, c0g:c0g + 512].rearrange('(kc p) n -> p kc n', p=128),
                            writes=[('wb', wbb_)])
                    else:
                        dma('sp', wb[wbb_][:], winb_d[:, c0g:c0g + 512].rearrange('(kc p) n -> p kc n', p=128),
                            reads=winb_keys, writes=[('wb', wbb_)])

                load_wb(0)
                for gi, (kind, c0, half) in enumerate(groups):
                    wbb = gi % 2
                    if gi + 1 < len(groups):
                        load_wb(gi + 1)
                    for i in range(NT):
                        pb = n % 3
                        j = n % 2
                        n += 1
                        for kc in range(8):
                            op('pe', lambda e: e.matmul(pB[pb][:], lhsT=hT[:, kc, i * 128:(i + 1) * 128], rhs=wb[wbb][:, kc, :],
                                                        start=(kc == 0), stop=(kc == 7)),
                               reads=[('hT', i // 4), ('wb', wbb)], writes=[('pB', pb)])
                        if kind in ('va', 'vb'):
                            dstv = va_d if kind == 'va' else vb_d
                            op('act', lambda e: e.activation(out=vst[j][:], in_=pB[pb][:], func=AF.Copy),
                               reads=[('pB', pb)], writes=[('vst', j)])
                            dma('sp', dstv[i * 128:(i + 1) * 128, half * 512:(half + 1) * 512], vst[j][:], reads=[('vst', j)])
                        else:
                            gcol = 0 if kind == 'qb' else 1
                            dstT = qbT_d if kind == 'qb' else kbT_d
                            j3 = nqk % 3
                            nqk += 1
                            op('act', lambda e: e.activation(out=sq[:], in_=pB[pb][:], func=AF.Square),
                               reads=[('pB', pb)], writes=['sq'])
                            op('dve', lambda e: e.tensor_reduce(out=st4[:, 0:4], in_=sq[:].rearrange('p (h d) -> p h d', d=128),
                                                                axis=AX.X, op=ALU.add), reads=['sq'], writes=['st4'])
                            op('act', lambda e: e.activation(out=st4[:, 4:8], in_=st4[:, 0:4], func=AF.Sqrt, scale=1.0 / 128, bias=EPS),
                               reads=['st4'], writes=['st4'])
                            op('dve', lambda e: e.reciprocal(st4[:, 8:12], st4[:, 4:8]), reads=['st4'], writes=['st4'])
                            op('dve', lambda e: e.tensor_tensor(out=qnb[j3][:], in0=pB[pb][:].rearrange('p (h d) -> p h d', d=128),
                                                                in1=st4[:, 8:12].unsqueeze(2).broadcast_to([128, 4, 128]), op=ALU.mult),
                               reads=[('pB', pb), 'st4'], writes=[('qnb', j3)])

                            def fin(j3=j3, gcol=gcol, dstT=dstT, half=half, i=i):
                                nonlocal nfin
                                jj = nfin % 2
                                nfin += 1
                                for hh in range(4):
                                    op('pe', lambda e: e.transpose(tpq[:, hh, :], qnb[j3][:, hh, :], ident[:]),
                                       reads=[('qnb', j3), 'ident'], writes=['tpq'])
                                op('act', lambda e: e.activation(out=qTs[jj][:], in_=tpq[:], func=AF.Copy, scale=gqk[:, gcol:gcol + 1]),
                                   reads=['tpq', 'gqk'], writes=[('qTs', jj)])
                                dma('sp', dstT[half * 4:(half + 1) * 4, :, i * 128:(i + 1) * 128].rearrange('h d t -> d h t'),
                                    qTs[jj][:], reads=[('qTs', jj)])
                            pend.append(fin)
                            if len(pend) > 2:
                                pend.pop(0)()
                        if gi == GATE_GROUP:
                            gate_s1(i)
                            if i > 0:
                                gate_s2(i - 1)
                    if gi == GATE_GROUP:
                        gate_s2(NT - 1)
                while pend:
                    pend.pop(0)()
                if GATE_GROUP < 0:
                    for i in range(NT + 1):
                        if i < NT:
                            gate_s1(i)
                        if i > 0:
                            gate_s2(i - 1)

                nf = 0
                nw = 0

                wf_cols = [cc * 128 for cc in range(8)]
                for cc in range(8):
                    wf_cols += [OA + cc * 128, GA + cc * 128, GB + cc * 128]
                wf_issued = [0]

                def issue_wf():
                    idx = wf_issued[0]
                    if idx >= len(wf_cols):
                        return
                    wf_issued[0] += 1
                    c0w = wf_cols[idx]
                    kk = idx % 6
                    with nc.allow_non_contiguous_dma(reason='512B weight rows'):
                        if l == 0:
                            dma('pool', wf[kk][:], win_d[l][:, c0w:c0w + 128].rearrange('(kc p) n -> p kc n', p=128), writes=[('wf', kk)])
                        else:
                            dma('sp', wf[kk][:], winb_d[:, c0w:c0w + 128].rearrange('(kc p) n -> p kc n', p=128),
                                reads=winb_keys, writes=[('wf', kk)])

                def load_wf(c0):
                    nonlocal nw
                    assert wf_cols[nw] == c0
                    k = nw % 6
                    nw += 1
                    while wf_issued[0] < min(nw + 2, len(wf_cols)):
                        issue_wf()
                    return k

                issue_wf()
                issue_wf()

                def fm_matmul(k, tb):
                    nonlocal nf
                    pb = nf % 2
                    nf += 1
                    for kc in range(8):
                        op('pe', lambda e: e.matmul(pF[pb][:], lhsT=wf[k][:, kc, :], rhs=hT[:, kc, tb * 512:(tb + 1) * 512],
                                                    start=(kc == 0), stop=(kc == 7)),
                           reads=[('hT', tb), ('wf', k)], writes=[('pF', pb)])
                    return pb

                for i in range(2):
                    op('pool', lambda e: e.memset(qlo_s[i][:], 0.0), writes=[('qlo_s', i)])
                    op('pool', lambda e: e.memset(qhi_s[i][:], 0.0), writes=[('qhi_s', i)])
                nq = 0
                nblk = 0
                pendc = []
                for cc in range(8):
                    k = load_wf(cc * 128)
                    for tb in range(NB):
                        pb = fm_matmul(k, tb)
                        sl = nblk % 2
                        nblk += 1
                        P = pre2[sl]
                        Pp = pre2[1 - sl]
                        A = acc2[sl]
                        if tb == 0:
                            op('pool', lambda e: e.memset(P[:, 0:3], 0.0), writes=[('pre', sl)])
                        else:
                            op('pool', lambda e: e.tensor_copy(P[:, 0:3], Pp[:, 512:515]), reads=[('pre', 1 - sl)], writes=[('pre', sl)])
                        op('act', lambda e: e.activation(out=P[:, 3:515], in_=pF[pb][:], func=AF.Copy),
                           reads=[('pF', pb)], writes=[('pre', sl)])
                        op('act', lambda e: e.activation(out=A[:], in_=pF[pb][:], func=AF.Copy, scale=cwT[:, cc, 3:4]),
                           reads=[('pF', pb), 'cwT'], writes=[('acc', sl)])
                        for jj in range(0, 3):
                            op('dve', lambda e: e.scalar_tensor_tensor(out=A[:], in0=P[:, jj:jj + 512],
                                                                       scalar=cwT[:, cc, jj:jj + 1], in1=A[:],
                                                                       op0=ALU.mult, op1=ALU.add),
                               reads=[('pre', sl), 'cwT', ('acc', sl)], writes=[('acc', sl)])
                        def fin_conv(cc=cc, tb=tb, A=A, sl=sl):
                            nonlocal nq
                            j = nq % 2
                            nq += 1
                            if cc < 4:
                                op('act', lambda e: e.activation(out=qlo_s[j][0:64, :], in_=A[0:64, :], func=AF.Silu),
                                   reads=[('acc', sl)], writes=[('qlo_s', j)])
                                op('act', lambda e: e.activation(out=qhi_s[j][64:128, :], in_=A[64:128, :], func=AF.Silu),
                                   reads=[('acc', sl)], writes=[('qhi_s', j)])
                                dma('sp', qlo_d[cc, :, tb * 512:(tb + 1) * 512], qlo_s[j][:], reads=[('qlo_s', j)])
                                dma('sp', qhi_d[cc, :, tb * 512:(tb + 1) * 512], qhi_s[j][:], reads=[('qhi_s', j)])
                            else:
                                op('act', lambda e: e.activation(out=gas[j][:], in_=A[:], func=AF.Silu),
                                   reads=[('acc', sl)], writes=[('gas', j)])
                                dma('sp', ka_d[cc - 4, :, tb * 512:(tb + 1) * 512], gas[j][:], reads=[('gas', j)])
                        pendc.append(fin_conv)
                        if len(pendc) > 1:
                            pendc.pop(0)()
                while pendc:
                    pendc.pop(0)()
                for cc in range(8):
                    ko = load_wf(OA + cc * 128)
                    kg = load_wf(GA + cc * 128)
                    kb = load_wf(GB + cc * 128)
                    for tb in range(NB):
                        pb = fm_matmul(ko, tb)
                        op('act', lambda e: e.activation(out=so[:], in_=pF[pb][:], func=AF.Sigmoid),
                           reads=[('pF', pb)], writes=['so'])
                        pb = fm_matmul(kg, tb)
                        op('act', lambda e: e.activation(out=sg[:], in_=pF[pb][:], func=AF.Sigmoid, bias=bgT[:, cc:cc + 1]),
                           reads=[('pF', pb), 'bgT'], writes=['sg'])
                        j = nq % 2
                        nq += 1
                        op('dve', lambda e: e.scalar_tensor_tensor(out=gas[j][:], in0=sg[:], scalar=gmT[:, cc:cc + 1], in1=so[:],
                                                                   op0=ALU.mult, op1=ALU.mult),
                           reads=['sg', 'so', 'gmT'], writes=[('gas', j)])
                        dma('sp', gaT_d[cc, :, tb * 512:(tb + 1) * 512], gas[j][:], reads=[('gas', j)])
                        pb = fm_matmul(kb, tb)
                        j = nq % 2
                        nq += 1
                        op('act', lambda e: e.activation(out=gas[j][:], in_=pF[pb][:], func=AF.Sigmoid, bias=bgT[:, 8 + cc:9 + cc]),
                           reads=[('pF', pb), 'bgT'], writes=[('gas', j)])
                        dma('sp', gbT_d[cc, :, tb * 512:(tb + 1) * 512], gas[j][:], reads=[('gas', j)])
            if stop_after == 'B':
                break

            S_.new_phase()
            for r in range(8):
                S_.dma_bg('pool', woutb_d[r * 128:(r + 1) * 128, :], wout_d[l][r * 128:(r + 1) * 128, :], ('woutb', r))
            for r in range(32):
                S_.dma_bg('pool', wdnb_d[r * 128:(r + 1) * 128, :], wdn_d[l][r * 128:(r + 1) * 128, :], ('wdnb', r))
            for r in range(8):
                S_.dma_bg('pool', wupb_d[r * 128:(r + 1) * 128, :], wup_d[l][r * 128:(r + 1) * 128, :], ('wupb', r))
            if l + 1 < L:
                for r in range(8):
                    S_.dma_bg('pool', winb_d[r * 128:(r + 1) * 128, :], win_d[l + 1][r * 128:(r + 1) * 128, :], ('winb', r))
            with ExitStack() as ph:
                qlo_c = [sb(ph, 'qlo_c%d' % i, [128, 4, 128], BF16) for i in range(4)]
                qhi_c = [sb(ph, 'qhi_c%d' % i, [128, 4, 128], BF16) for i in range(4)]
                k_c = [sb(ph, 'k_c%d' % i, [128, 4, 128], BF16) for i in range(4)]
                v_c = [sb(ph, 'v_c%d' % i, [128, D], BF16) for i in range(4)]
                ga_c = [sb(ph, 'ga_c%d' % i, [128, 8, 128], BF16) for i in range(4)]
                PT = [sb(ph, 'PT%d' % i, [128, 8, 128], BF16) for i in range(2)]
                kw = [sb(ph, 'kw%d' % i, [128, 8, 64], BF16) for i in range(2)]
                C_all = sb(ph, 'C_all', [128, 4, 128], F32)
                n_all = sb(ph, 'n_all', [128, 4], F32)
                Cd = sb(ph, 'Cd', [128, 4, 128], F32)
                nd = sb(ph, 'nd', [128, 4], F32)
                Cd_bf = sb(ph, 'Cd_bf', [128, 4, 128], BF16)
                nd_bf = sb(ph, 'nd_bf', [128, 4], BF16)
                sm = sb(ph, 'sm', [128, 96], F32)
                den_sb = [sb(ph, 'den_sb%d' % i, [128, 8], F32) for i in range(2)]
                sqb = sb(ph, 'sqb', [128, D], F32)
                ya = sb(ph, 'ya', [128, D], BF16)
                yag = [sb(ph, 'yag%d' % i, [128, 8, 128], BF16) for i in range(2)]
                scp = ps(ph, 'scp', [128, 4, 128], F32)
                nump = [[ps(ph, 'nump%d_%d' % (i, j), [128, 4, 128], F32) for j in range(2)] for i in range(2)]
                dCp = ps(ph, 'dCp', [128, 2, 256], F32)
                misc = ps(ph, 'misc', [128, 512], F32)
                tp = ps(ph, 'tpC', [128, 8, 128], BF16)
                op('dve', lambda e: e.memset(C_all[:], 0.0), writes=['C_all'])
                op('dve', lambda e: e.memset(n_all[:], 0.0), writes=['n_all'])

                def qm(c, h):
                    b3 = c % 4
                    return (qlo_c[b3] if h % 2 == 0 else qhi_c[b3])[:, h // 2, :]

                def qmk(c, h):
                    return ('qlo_c', c % 4) if h % 2 == 0 else ('qhi_c', c % 4)

                def loads(c):
                    b3 = c % 4
                    tsl = slice(c * 128, (c + 1) * 128)
                    with nc.allow_non_contiguous_dma(reason='256B rows'):
                        dma('sp', qlo_c[b3][:], qlo_d[:, :, tsl].rearrange('c p t -> p c t'), writes=[('qlo_c', b3)])
                        dma('sp', qhi_c[b3][:], qhi_d[:, :, tsl].rearrange('c p t -> p c t'), writes=[('qhi_c', b3)])
                        dma('sp', k_c[b3][:], ka_d[:, :, tsl].rearrange('c p t -> p c t'), writes=[('k_c', b3)])
                        dma('sp', ga_c[b3][:], gaT_d[:, :, tsl].rearrange('c p t -> p c t'), writes=[('ga_c', b3)])
                    dma('sp', v_c[b3][:], va_d[tsl, :], writes=[('v_c', b3)])

                def front(c):
                    b3 = c % 4
                    p2 = c % 2
                    for hg in range(2):
                        for h in range(4 * hg, 4 * hg + 4):
                            op('pe', lambda e: e.matmul(scp[:, h % 4, :], lhsT=k_c[b3][:, h // 2, :], rhs=qm(c, h), start=True, stop=True),
                               reads=[('k_c', b3), qmk(c, h)], writes=['scp'])
                        for h in range(4 * hg, 4 * hg + 4):
                            op('dve', lambda e: e.scalar_tensor_tensor(out=PT[p2][:, h, :], in0=scp[:, h % 4, :],
                                                                       scalar=gates[:, c, 0, h:h + 1], in1=triU[:],
                                                                       op0=ALU.mult, op1=ALU.mult),
                               reads=['scp', ('gates', c, 0), 'triU'], writes=[('PT', p2, h)])
                        yield
                    for cc in range(4):
                        op('pe', lambda e: e.transpose(tp[:, cc, :], k_c[b3][:, cc, :], ident[:]),
                           reads=[('k_c', b3), 'ident'], writes=['tp'])
                    op('dve', lambda e: e.tensor_tensor(out=kw[p2][:], in0=tp[:, 0:4, :].rearrange('p c (j d) -> p (c j) d', j=2),
                                                        in1=gates[:, c, 0, :].unsqueeze(2).broadcast_to([128, 8, 64]), op=ALU.mult),
                       reads=['tp', ('gates', c, 0)], writes=[('kw', p2)])

                def mid(c):
                    b3 = c % 4
                    p2 = c % 2
                    op('dve', lambda e: e.tensor_tensor(out=Cd[:], in0=C_all[:], in1=decB[:, c, :].unsqueeze(2).broadcast_to([128, 4, 128]),
                                                        op=ALU.mult), reads=['C_all', ('decB', c, 0), ('decB', c, 1)], writes=['Cd'])
                    op('dve', lambda e: e.tensor_tensor(out=nd[:], in0=n_all[:], in1=decB[:, c, :], op=ALU.mult),
                       reads=['n_all', ('decB', c, 0), ('decB', c, 1)], writes=['nd'])
                    op('act', lambda e: e.activation(out=Cd_bf[:], in_=Cd[:], func=AF.Copy), reads=['Cd'], writes=['Cd_bf'])
                    op('act', lambda e: e.activation(out=nd_bf[:], in_=nd[:], func=AF.Copy), reads=['nd'], writes=['nd_bf'])
                    yield
                    for h in range(NH):
                        if h == 4:
                            yield
                        npk = ('nump', p2, h // 4)
                        nt_ = nump[p2][h // 4]
                        op('pe', lambda e: e.matmul(nt_[:, h % 4, :], lhsT=PT[p2][:, h, :], rhs=v_c[b3][:, h * 128:(h + 1) * 128],
                                                    start=True, stop=False),
                           reads=[('PT', p2, h), ('v_c', b3)], writes=[npk])
                        op('pe', lambda e: e.matmul(nt_[:, h % 4, :], lhsT=qm(c, h), rhs=Cd_bf[:, h // 2, :],
                                                    start=False, stop=True),
                           reads=[qmk(c, h), 'Cd_bf'], writes=[npk])
                        dcol = 16 * p2 + h
                        op('pe', lambda e: e.matmul(misc[:, dcol:dcol + 1], lhsT=PT[p2][:, h, :], rhs=ones_bf[:, 0:1], start=True, stop=False),
                           reads=[('PT', p2, h), 'ones_bf'], writes=['misc'])
                        op('pe', lambda e: e.matmul(misc[:, dcol:dcol + 1], lhsT=qm(c, h), rhs=nd_bf[:, h // 2:h // 2 + 1], start=False, stop=True),
                           reads=[qmk(c, h), 'nd_bf'], writes=['misc'])
                    yield
                    op('dve', lambda e: e.tensor_copy(den_sb[p2][:], misc[:, 16 * p2:16 * p2 + 8]), reads=['misc'], writes=[('den_sb', p2)])
                    for g2 in range(2):
                        if g2 == 1:
                            yield
                        for cc in (2 * g2, 2 * g2 + 1):
                            op('pe', lambda e: e.matmul(dCp[:, cc % 2, :], lhsT=kw[p2][:, 2 * cc:2 * cc + 2, :].rearrange('p j d -> p (j d)'),
                                                        rhs=v_c[b3][:, cc * 256:(cc + 1) * 256], start=True, stop=True),
                               reads=[('kw', p2), ('v_c', b3)], writes=['dCp'])
                        op('dve', lambda e: e.tensor_tensor(out=C_all[0:64, 2 * g2:2 * g2 + 2, :], in0=Cd[0:64, 2 * g2:2 * g2 + 2, :],
                                                            in1=dCp[0:64, :, 0:128], op=ALU.add),
                           reads=['Cd', 'dCp'], writes=['C_all'])
                        op('dve', lambda e: e.tensor_tensor(out=C_all[64:128, 2 * g2:2 * g2 + 2, :], in0=Cd[64:128, 2 * g2:2 * g2 + 2, :],
                                                            in1=dCp[64:128, :, 128:256], op=ALU.add),
                           reads=['Cd', 'dCp'], writes=['C_all'])
                    for cc in range(4):
                        op('pe', lambda e: e.matmul(misc[:, 32 + cc:33 + cc], lhsT=kw[p2][:, 2 * cc:2 * cc + 2, :].rearrange('p j d -> p (j d)'),
                                                    rhs=ones_bf[:, 0:1], start=True, stop=True),
                           reads=[('kw', p2), 'ones_bf'], writes=['misc'])
                    op('dve', lambda e: e.tensor_tensor(out=n_all[:], in0=nd[:], in1=misc[:, 32:36], op=ALU.add),
                       reads=['nd', 'misc'], writes=['n_all'])

                def back(c):
                    b3 = c % 4
                    p2 = c % 2
                    tsl = slice(c * 128, (c + 1) * 128)
                    ebp = gates[:, c, 1, :]
                    op('dve', lambda e: e.tensor_tensor(out=sm[:, 0:8], in0=den_sb[p2][:], in1=ebp, op=ALU.mult),
                       reads=[('den_sb', p2), ('gates', c, 1)], writes=['sm0'])
                    op('dve', lambda e: e.scalar_tensor_tensor(out=sm[:, 8:16], in0=sm[:, 0:8], scalar=-1.0, in1=sm[:, 0:8],
                                                               op0=ALU.mult, op1=ALU.max), reads=['sm0'], writes=['sm1'])
                    op('dve', lambda e: e.tensor_scalar(out=sm[:, 16:24], in0=sm[:, 8:16], scalar1=1.0, scalar2=None, op0=ALU.max),
                       reads=['sm1'], writes=['sm2'])
                    op('dve', lambda e: e.reciprocal(sm[:, 24:32], sm[:, 16:24]), reads=['sm2'], writes=['sm3'])
                    op('dve', lambda e: e.tensor_tensor(out=sm[:, 32:40], in0=ebp, in1=sm[:, 24:32], op=ALU.mult),
                       reads=['sm3', ('gates', c, 1)], writes=['sm4'])
                    for g2 in range(2):
                        op('act', lambda e: e.activation(out=sqb[:, g2 * 512:(g2 + 1) * 512],
                                                         in_=nump[p2][g2][:].rearrange('p h d -> p (h d)'), func=AF.Square),
                           reads=[('nump', p2, g2)], writes=[('sqb', g2)])
                    yield
                    op('dve', lambda e: e.tensor_reduce(out=sm[:, 40:48], in_=sqb[:].rearrange('p (h d) -> p h d', d=128),
                                                        axis=AX.X, op=ALU.add), reads=[('sqb', 0), ('sqb', 1)], writes=['sm5'])
                    op('dve', lambda e: e.tensor_tensor(out=sm[:, 48:56], in0=sm[:, 32:40], in1=sm[:, 32:40], op=ALU.mult),
                       reads=['sm4'], writes=['sm6'])
                    op('dve', lambda e: e.tensor_tensor(out=sm[:, 56:64], in0=sm[:, 48:56], in1=sm[:, 40:48], op=ALU.mult),
                       reads=['sm6', 'sm5'], writes=['sm7'])
                    op('act', lambda e: e.activation(out=sm[:, 64:72], in_=sm[:, 56:64], func=AF.Sqrt, scale=1.0 / 128, bias=EPS),
                       reads=['sm7'], writes=['sm8'])
                    op('dve', lambda e: e.reciprocal(sm[:, 72:80], sm[:, 64:72]), reads=['sm8'], writes=['sm9'])
                    op('dve', lambda e: e.tensor_tensor(out=sm[:, 80:88], in0=sm[:, 32:40], in1=sm[:, 72:80], op=ALU.mult),
                       reads=['sm9', 'sm4'], writes=['sm10'])
                    yield
                    for h in range(NH):
                        op('act', lambda e: e.activation(out=ya[:, h * 128:(h + 1) * 128], in_=nump[p2][h // 4][:, h % 4, :],
                                                         func=AF.Copy, scale=sm[:, 80 + h:81 + h]),
                           reads=[('nump', p2, h // 4), 'sm10'], writes=[('ya', h)])
                    yield
                    for h in range(NH):
                        op('pe', lambda e: e.transpose(tp[:, h, :], ya[:, h * 128:(h + 1) * 128], ident[:]),
                           reads=[('ya', h), 'ident'], writes=['tp'])
                    op('dve', lambda e: e.tensor_tensor(out=yag[p2][:], in0=tp[:], in1=ga_c[b3][:], op=ALU.mult),
                       reads=['tp', ('ga_c', b3)], writes=[('yag', p2)])
                    with nc.allow_non_contiguous_dma(reason='256B rows'):
                        dma('sp', yaT_d[:, :, tsl].rearrange('c p t -> p c t'), yag[p2][:], reads=[('yag', p2)])

                loads(0)
                for it in range(NT + 2):
                    if it + 1 < NT:
                        loads(it + 1)
                    gens = []
                    if 0 <= it - 2 < NT:
                        gens.append(back(it - 2))
                    if 0 <= it - 1 < NT:
                        gens.append(mid(it - 1))
                    if it < NT:
                        gens.append(front(it))
                    while gens:
                        for gen in list(gens):
                            try:
                                next(gen)
                            except StopIteration:
                                gens.remove(gen)
            if stop_after == 'C':
                break

            S_.new_phase()
            with ExitStack() as ph:
                knT = [sb(ph, 'knT%d' % i, [128, S], BF16) for i in range(2)]
                qnT = [sb(ph, 'qnT%d' % i, [128, S], BF16) for i in range(2)]
                Vh = [sb(ph, 'Vh%d' % i, [128, NT, 128], BF16) for i in range(2)]
                Eb = [sb(ph, 'Eb', [128, 2, 512], F32)] * 2
                Lp = [sb(ph, 'Lp%d' % i, [128, 2, 512], BF16) for i in range(2)]
                AT = [sb(ph, 'AT%d' % i, [128, 2, 512], BF16) for i in range(2)]
                gb_b = [sb(ph, 'gb_b%d' % i, [128, 512], BF16) for i in range(2)]
                ya_b = [sb(ph, 'ya_b%d' % i, [128, 512], BF16) for i in range(2)]
                c2l = [sb(ph, 'c2_%d' % i, [64, 512], BF16) for i in range(2)]
                ytmp = sb(ph, 'ytmp', [128, 512], F32)
                zA = [ps(ph, 'zA%d' % i, [128, 2, 512], F32) for i in range(3)]
                yacc = ps(ph, 'yacc', [128, 512], F32)
                csp = ps(ph, 'csp', [128, 512], F32)

                def load_head(h):
                    hp = h % 2
                    dma('sp', knT[hp][:], kbT_d[h, :, :], writes=[('knT', hp)])
                    dma('sp', qnT[hp][:], qbT_d[h, :, :], writes=[('qnT', hp)])
                    with nc.allow_non_contiguous_dma(reason='256B rows'):
                        dma('sp', Vh[hp][:], vb_d[:, h * 128:(h + 1) * 128].rearrange('(n p) d -> p n d', p=128),
                            writes=[('Vh', hp)])

                blocks = [(h, qb) for h in range(NH) for qb in range(NB)]

                def load_block(bi):
                    h, qb = blocks[bi]
                    yb = bi % 2
                    qsl = slice(qb * 512, (qb + 1) * 512)
                    dma('sp', gb_b[yb][:], gbT_d[h, :, qsl], writes=[('gb_b', yb)])
                    dma('sp', ya_b[yb][:], yaT_d[h, :, qsl], writes=[('ya_b', yb)])

                tiles = []
                for bi, (h, qb) in enumerate(blocks):
                    ktmax = 4 * qb + 3
                    for n, kt in enumerate(range(ktmax, -1, -2)):
                        tiles.append((bi, h, qb, n, kt))
                G = len(tiles)

                def emit_zA(g):
                    bi, h, qb, n, kt = tiles[g]
                    hp = h % 2
                    k3 = g % 3
                    for u in range(2):
                        ktu = kt - u
                        diag = ktu >= 4 * qb
                        op('pe', lambda e: e.matmul(zA[k3][:, u, :], lhsT=knT[hp][:, ktu * 128:(ktu + 1) * 128],
                                                    rhs=qnT[hp][:, qb * 512:(qb + 1) * 512], start=True, stop=(not diag)),
                           reads=[('knT', hp), ('qnT', hp)], writes=[('zA', k3, u)])
                        if diag:
                            op('pe', lambda e: e.matmul(zA[k3][:, u, :], lhsT=ident[:], rhs=maskneg[:, ktu - 4 * qb, :],
                                                        start=False, stop=True),
                               reads=['ident', 'maskneg'], writes=[('zA', k3, u)])

                def emit_tail(g):
                    bi, h, qb, n, kt = tiles[g]
                    hp = h % 2
                    yb = bi % 2
                    k3 = g % 3
                    op('act', lambda e: e.activation(out=AT[g % 2][:], in_=zA[k3][:], func=AF.Exp),
                       reads=[('zA', k3, 0), ('zA', k3, 1)], writes=[('AT', g % 2)])
                    for u in range(2):
                        op('pe', lambda e: e.matmul(yacc[:], lhsT=Vh[hp][:, kt - u, :], rhs=AT[g % 2][:, u, :],
                                                    start=(n == 0 and u == 0), stop=(kt - u == 0)),
                           reads=[('Vh', hp), ('AT', g % 2)], writes=['yacc'])
                    if kt - 1 == 0:
                        qsl = slice(qb * 512, (qb + 1) * 512)
                        op('dve', lambda e: e.tensor_tensor(out=ytmp[:], in0=yacc[:], in1=gb_b[yb][:], op=ALU.mult),
                           reads=['yacc', ('gb_b', yb)], writes=['ytmp'])
                        op('dve', lambda e: e.tensor_tensor(out=yT[:, h, qsl], in0=ytmp[:], in1=ya_b[yb][:], op=ALU.add),
                           reads=['ytmp', ('ya_b', yb)], writes=[('yT', qb)])

                for i in range(2):
                    op('dve', lambda e: e.memset(c2l[i][:], 0.0), writes=[('c2', i)])
                load_head(0)
                if NH > 1:
                    load_head(1)
                load_block(0)
                emit_zA(0)
                for g in range(G):
                    bi, h, qb, n, kt = tiles[g]
                    k3 = g % 3
                    last = (kt - 1 == 0)
                    if g + 1 < G:
                        emit_zA(g + 1)
                    op('act', lambda e: e.activation(out=Eb[g % 2][:], in_=zA[k3][:], func=AF.Exp),
                       reads=[('zA', k3, 0), ('zA', k3, 1)], writes=['Eb'])
                    op('act', lambda e: e.activation(out=Lp[g % 2][:], in_=Eb[g % 2][:], func=AF.Ln, bias=1.0),
                       reads=['Eb'], writes=[('Lp', g % 2)])
                    Lg = Lp[g % 2]
                    c2p, c2pk = c2l[(g - 1) % 2], ('c2', (g - 1) % 2)
                    c2n, c2nk = c2l[g % 2], ('c2', g % 2)
                    lk = ('Lp', g % 2)
                    if not last:
                        op('pe', lambda e: e.matmul(csp[0:64, :], lhsT=ones_bf[:, 0:64], rhs=Lg[:, 0, :], start=True, stop=False),
                           reads=['ones_bf', lk], writes=['csp'])
                        op('pe', lambda e: e.matmul(csp[0:64, :], lhsT=ones_bf[:, 0:64], rhs=Lg[:, 1, :], start=False, stop=(n == 0)),
                           reads=['ones_bf', lk], writes=['csp'])
                        if n > 0:
                            op('pe', lambda e: e.matmul(csp[0:64, :], lhsT=sel[0:64, 2, 0:64], rhs=c2p[:], start=False, stop=True),
                               reads=['sel', c2pk], writes=['csp'])
                    op('pe', lambda e: e.matmul(zA[k3][:, 0, :], lhsT=triNeg[:], rhs=Lg[:, 0, :], start=False, stop=(n == 0),
                                                skip_group_check=True),
                       reads=['triNeg', lk], writes=[('zA', k3, 0)])
                    op('pe', lambda e: e.matmul(zA[k3][:, 1, :], lhsT=triNeg[:], rhs=Lg[:, 1, :], start=False, stop=False,
                                                skip_group_check=True),
                       reads=['triNeg', lk], writes=[('zA', k3, 1)])
                    op('pe', lambda e: e.matmul(zA[k3][:, 1, :], lhsT=negones[:], rhs=Lg[:, 0, :], start=False, stop=(n == 0),
                                                skip_group_check=True),
                       reads=['negones', lk], writes=[('zA', k3, 1)])
                    if n > 0:
                        for u in range(2):
                            op('pe', lambda e: e.matmul(zA[k3][:, u, :], lhsT=sel[0:64, 0, :], rhs=c2p[:], start=False, stop=True,
                                                        skip_group_check=True),
                               reads=['sel', c2pk], writes=[('zA', k3, u)])
                    if not last:
                        op('dve', lambda e: e.tensor_copy(c2n[:], csp[0:64, :]), reads=['csp'], writes=[c2nk])
                        op('dve', lambda e: e.tensor_tensor(out=c2n[32:64, :], in0=csp[32:64, :], in1=c2n[32:64, :], op=ALU.subtract),
                           reads=['csp', c2nk], writes=[c2nk])
                    if g > 0:
                        emit_tail(g - 1)
                    if n == 0:
                        if bi + 1 < len(blocks):
                            load_block(bi + 1)
                        if qb == 0 and h >= 1 and h + 1 < NH:
                            load_head(h + 1)
                emit_tail(G - 1)
            if stop_after == 'D':
                break

            S_.new_phase()
            with ExitStack() as ph:
                gB = sb(ph, 'gB', [128, D], F32)
                dma('sp', gB[:], g2_d[l:l + 1, :].broadcast_to([128, D]), writes=['gB'])
                wo = sb(ph, 'wo', [128, 8, D], BF16)
                xt = [sb(ph, 'xt%d' % i, [128, D], F32) for i in range(2)]
                x1 = [sb(ph, 'x1%d' % i, [128, D], F32) for i in range(4)]
                junk = sb(ph, 'junk', [128, D], BF16)
                hn = [sb(ph, 'hn%d' % i, [128, D], BF16) for i in range(2)]
                stt = [sb(ph, 'stt%d' % i, [128, 4], F32) for i in range(4)]
                pO = [ps(ph, 'pO%d' % i, [128, 512], F32) for i in range(4)]
                tp = [ps(ph, 'tpE%d' % i, [128, 8, 128], BF16) for i in range(2)]
                dma('sp', wo[:], woutb_d.rearrange('(c p) n -> p c n', p=128), reads=[('woutb', r) for r in range(8)], writes=['wo'])

                def e_tile(i):
                    b = i % 2
                    b4 = i % 4
                    tsl = slice(i * 128, (i + 1) * 128)
                    dma('sp', xt[b][:], src_d[tsl, :], writes=[('xt', b)])
                    for half in range(2):
                        pk = 2 * b + half
                        for cc in range(8):
                            op('pe', lambda e: e.matmul(pO[pk][:], lhsT=yT[:, cc, tsl], rhs=wo[:, cc, half * 512:(half + 1) * 512],
                                                        start=(cc == 0), stop=(cc == 7)),
                               reads=[('yT', i // 4), 'wo'], writes=[('pO', pk)])
                        op('dve', lambda e: e.tensor_tensor(out=x1[b4][:, half * 512:(half + 1) * 512], in0=xt[b][:, half * 512:(half + 1) * 512],
                                                            in1=pO[pk][:], op=ALU.add),
                           reads=[('xt', b), ('pO', pk)], writes=[('x1', b4)])
                    dma('sp', xmid_d[tsl, :], x1[b4][:], reads=[('x1', b4)])
                    op('dve', lambda e: e.scalar_tensor_tensor(out=junk[:], in0=x1[b4][:], scalar=1.0, in1=x1[b4][:],
                                                               op0=ALU.mult, op1=ALU.mult, accum_out=stt[b4][:, 0:1]),
                       reads=[('x1', b4)], writes=['junk', ('stt', b4)])
                    yield
                    op('act', lambda e: e.activation(out=stt[b4][:, 1:2], in_=stt[b4][:, 0:1], func=AF.Sqrt,
                                                     scale=1.0 / D, bias=EPS), reads=[('stt', b4)], writes=[('stt', b4)])
                    yield
                    op('dve', lambda e: e.reciprocal(stt[b4][:, 2:3], stt[b4][:, 1:2]), reads=[('stt', b4)], writes=[('stt', b4)])
                    op('dve', lambda e: e.scalar_tensor_tensor(out=hn[b][:], in0=x1[b4][:], scalar=stt[b4][:, 2:3], in1=gB[:],
                                                               op0=ALU.mult, op1=ALU.mult),
                       reads=[('x1', b4), ('stt', b4), 'gB'], writes=[('hn', b)])
                    yield
                    for k in range(8):
                        op('pe', lambda e: e.transpose(tp[b][:, k, :], hn[b][:, k * 128:(k + 1) * 128], ident[:]),
                           reads=[('hn', b), 'ident'], writes=[('tp', b)])
                    op('act', lambda e: e.activation(out=hT[:, :, tsl], in_=tp[b][:], func=AF.Copy),
                       reads=[('tp', b)], writes=[('hT', i // 4)])

                run_skewed(e_tile, NT)
            if stop_after == 'E1':
                break

            S_.new_phase()
            with ExitStack() as ph:
                wu = [sb(ph, 'wu%d' % i, [128, 8, 512], BF16) for i in range(2)]
                rr = [sb(ph, 'rr%d' % i, [128, 512], F32) for i in range(2)]
                aT = sb(ph, 'aT', [128, 32, 512], BF16)
                x1t = [sb(ph, 'x1t%d' % i, [128, D], F32) for i in range(2)]
                pU = [ps(ph, 'pU%d' % i, [128, 512], F32) for i in range(2)]
                pD = [ps(ph, 'pD%d' % i, [128, 512], F32) for i in range(4)]
                for q4 in range(4):
                    dma('sp', wdn[:, q4 * 8:(q4 + 1) * 8, :], wdnb_d[q4 * 1024:(q4 + 1) * 1024, :].rearrange('(f p) n -> p f n', p=128),
                        reads=[('wdnb', r) for r in range(q4 * 8, q4 * 8 + 8)], writes=[('wdn', q4)])
                nu = 0
                nx = 0
                for tb in range(NB):
                    for g in range(8):
                        wbb = (tb * 8 + g) % 2
                        dma('sp', wu[wbb][:], wupb_d[:, g * 512:(g + 1) * 512].rearrange('(kc p) n -> p kc n', p=128),
                            reads=[('wupb', r) for r in range(8)], writes=[('wu', wbb)])
                        for f in range(4):
                            fc = 4 * g + f
                            pb = nu % 2
                            nu += 1
                            for kc in range(8):
                                op('pe', lambda e: e.matmul(pU[pb][:], lhsT=wu[wbb][:, kc, f * 128:(f + 1) * 128],
                                                            rhs=hT[:, kc, tb * 512:(tb + 1) * 512], start=(kc == 0), stop=(kc == 7)),
                                   reads=[('wu', wbb), ('hT', tb)], writes=[('pU', pb)])
                            op('act', lambda e: e.activation(out=rr[pb][:], in_=pU[pb][:], func=AF.Relu),
                               reads=[('pU', pb)], writes=[('rr', pb)])
                            op('pool', lambda e: e.tensor_tensor(out=aT[:, fc, :], in0=rr[pb][:], in1=rr[pb][:], op=ALU.mult),
                               reads=[('rr', pb)], writes=[('aT', fc)])
                    for ts in range(4):
                        i = tb * 4 + ts
                        xb = nx % 2
                        nx += 1
                        tsl = slice(i * 128, (i + 1) * 128)
                        dma('sp', x1t[xb][:], xmid_d[tsl, :], writes=[('x1t', xb)])
                        for half in range(2):
                            pk = 2 * xb + half
                            for fc in range(32):
                                op('pe', lambda e: e.matmul(pD[pk][:], lhsT=aT[:, fc, ts * 128:(ts + 1) * 128],
                                                            rhs=wdn[:, fc, half * 512:(half + 1) * 512], start=(fc == 0), stop=(fc == 31)),
                                   reads=[('aT', fc), ('wdn', fc // 8)], writes=[('pD', pk)])
                            op('dve', lambda e: e.tensor_tensor(out=x1t[xb][:, half * 512:(half + 1) * 512],
                                                                in0=x1t[xb][:, half * 512:(half + 1) * 512], in1=pD[pk][:], op=ALU.add),
                               reads=[('x1t', xb), ('pD', pk)], writes=[('x1t', xb)])
                        dma('sp', dst_d[tsl, :], x1t[xb][:], reads=[('x1t', xb)])
        S_.finish('sp')
    return nc


_CACHE = {}


def kernel(**inputs):
    x = np.ascontiguousarray(np.asarray(inputs['x'], dtype=np.float32))
    B, S, _ = x.shape
    L = int(np.asarray(inputs['w_in']).shape[0])
    key = (S, L)
    if key not in _CACHE:
        _CACHE[key] = build(S, L)
    nc = _CACHE[key]
    names = ['norm_mix_g', 'w_in', 'b_if', 'b_gate', 'conv_w', 'mlstm_norm_g', 'sb_q_norm_g', 'sb_k_norm_g',
             'w_out', 'norm_mlp_g', 'w_up', 'w_down']
    shared = {k: np.ascontiguousarray(np.asarray(inputs[k], dtype=np.float32)) for k in names}
    n = 8
    in_maps = []
    for c in range(n):
        m = dict(shared)
        m['x'] = x[c % B]
        in_maps.append(m)
    res = run_bass_kernel_spmd(nc, in_maps, core_ids=list(range(n)))
    return np.stack([res.results[b]['out'] for b in range(B)], axis=0).astype(np.float32)
```

```python
import math
from contextlib import ExitStack

import numpy as np
import concourse.bass as bass
import concourse.mybir as mybir
from concourse.bass_utils import run_bass_kernel_spmd

F32 = mybir.dt.float32
BF16 = mybir.dt.bfloat16
AF = mybir.ActivationFunctionType
ALU = mybir.AluOpType
AX = mybir.AxisListType

D = 1024
NH = 8
EPS = 1e-6
QA, KA, VA, OA, IA, FA, QB, KB, VB, GA, GB = 0, 512, 1024, 2048, 3072, 3080, 3088, 4112, 5136, 6160, 7184
INC = 8208
DFF = 4096
NEG = -30000.0
GATE_GROUP = -1
import os
PC_STAGES = os.environ.get('PC_STAGES', 'fmb')


class Sched:
    def __init__(self, nc, st, n_dma_sems=22):
        self.nc = nc
        self.st = st
        self.eng = {'pe': nc.tensor, 'act': nc.scalar, 'dve': nc.vector, 'pool': nc.gpsimd, 'sp': nc.sync}
        self.sem = {}
        self.cnt = {}
        self.nphase = 0
        for e in ['pe', 'act', 'dve', 'pool']:
            self.sem[e] = st.enter_context(nc.semaphore('s_%s_0' % e))
            self.cnt[e] = 0
        self.dsem = [st.enter_context(nc.semaphore('d%d' % i)) for i in range(n_dma_sems)]
        self.bsem = [st.enter_context(nc.semaphore('b%d' % i)) for i in range(6)]
        self.bcnt = [0] * 6
        self.bnext = 0
        self.bg = {}
        self.dcnt = [0] * n_dma_sems
        self.dnext = 0
        self.npool0 = n_dma_sems - 6
        self.dnext_pool = self.npool0
        self.known = {e: {} for e in self.eng}
        self.last_w = {}
        self.readers = {}
        self.log = None
        self._cur = None

    def _wait(self, e, tk):
        key, val, src = tk
        if self.known[e].get(key, 0) >= val:
            return
        if self.log is not None and self._cur is not None:
            self._cur.append((key, val))
        if isinstance(key, str):
            sem = self.sem[key]
        elif isinstance(key, tuple):
            sem = self.bsem[key[1]]
        else:
            sem = self.dsem[key]
        self.eng[e].wait_ge(sem, val)
        self.known[e][key] = val

    def _deps(self, e, reads, writes):
        for r in reads:
            w = self.last_w.get(r)
            if w is not None:
                self._wait(e, w)
            w = self.bg.get(r)
            if w is not None:
                self._wait(e, w)
        for r in writes:
            w = self.last_w.get(r)
            if w is not None and (w[2] != e or e != 'pe'):
                self._wait(e, w)
            for rd in self.readers.get(r, {}).values():
                if rd[2] != e or e != 'pe':
                    self._wait(e, rd)

    def _commit(self, tk, reads, writes):
        for r in reads:
            self.readers.setdefault(r, {})[tk[2]] = tk
        for r in writes:
            self.last_w[r] = tk
            self.readers[r] = {}

    def op(self, e, fn, reads=(), writes=()):
        if self.log is not None:
            self._cur = []
        self._deps(e, reads, writes)
        inst = fn(self.eng[e])
        self.cnt[e] += 1
        if self.log is not None:
            self.log.append((e, self.cnt[e], list(reads), list(writes), self._cur))
            self._cur = None
        inst.then_inc(self.sem[e], 1)
        tk = (e, self.cnt[e], e)
        self._commit(tk, reads, writes)
        return tk

    def dma(self, q, out, in_, reads=(), writes=()):
        if q == 'pool':
            j = self.dnext_pool
            self.dnext_pool = self.npool0 + (self.dnext_pool + 1 - self.npool0) % (len(self.dsem) - self.npool0)
        else:
            j = self.dnext
            self.dnext = (self.dnext + 1) % self.npool0
        if self.dcnt[j] > 0:
            self._wait(q, (j, self.dcnt[j], 'dma'))
        self._deps(q, reads, writes)
        inst = self.eng[q].dma_start(out=out, in_=in_)
        self.dcnt[j] += 16
        inst.then_inc(self.dsem[j], 16)
        tk = (j, self.dcnt[j], 'dma%d' % j)
        self._commit(tk, reads, writes)
        return tk

    def dma_bg(self, q, out, in_, key):
        j = self.bnext
        self.bnext = (self.bnext + 1) % len(self.bsem)
        if self.bcnt[j] > 0:
            self._wait(q, (('b', j), self.bcnt[j], 'bg'))
        inst = self.eng[q].dma_start(out=out, in_=in_)
        self.bcnt[j] += 16
        inst.then_inc(self.bsem[j], 16)
        self.bg[key] = (('b', j), self.bcnt[j], 'bg')

    def _all_tickets(self):
        tks = [(e, self.cnt[e], e) for e in self.cnt if self.cnt[e] > 0]
        tks += [(j, self.dcnt[j], 'dma') for j in range(len(self.dsem)) if self.dcnt[j] > 0]
        return tks

    def barrier(self):
        tks = self._all_tickets()
        for e in self.eng:
            for tk in tks:
                if tk[2] == e:
                    continue
                self._wait(e, tk)
        self.last_w = {}
        self.readers = {}

    def new_phase(self):
        self.barrier()
        self.nphase += 1
        for e in ['pe', 'act', 'dve', 'pool']:
            self.sem[e] = self.st.enter_context(self.nc.semaphore('s_%s_%d' % (e, self.nphase)))
            self.cnt[e] = 0
        for e in self.known:
            self.known[e] = {k: v for k, v in self.known[e].items() if not isinstance(k, str)}

    def finish(self, e='sp'):
        for tk in self._all_tickets():
            self._wait(e, tk)
        for j in range(len(self.bsem)):
            if self.bcnt[j] > 0:
                self._wait(e, (('b', j), self.bcnt[j], 'bg'))


def build(S, L, dbg=False, stop_after=None):
    NT = S // 128
    NB = S // 512
    nc = bass.Bass('TRN2', target_bir_lowering=False)
    dt = nc.dram_tensor
    x_d = dt('x', [S, D], F32, kind='ExternalInput').ap()
    g1_d = dt('norm_mix_g', [L, D], F32, kind='ExternalInput').ap()
    win_d = dt('w_in', [L, D, INC], F32, kind='ExternalInput').ap()
    bif_d = dt('b_if', [L, 16], F32, kind='ExternalInput').ap()
    bg_d = dt('b_gate', [L, 2 * D], F32, kind='ExternalInput').ap()
    cw_d = dt('conv_w', [L, 4, D], F32, kind='ExternalInput').ap()
    gm_d = dt('mlstm_norm_g', [L, D], F32, kind='ExternalInput').ap()
    gq_d = dt('sb_q_norm_g', [L, 128], F32, kind='ExternalInput').ap()
    gk_d = dt('sb_k_norm_g', [L, 128], F32, kind='ExternalInput').ap()
    wout_d = dt('w_out', [L, D, D], F32, kind='ExternalInput').ap()
    g2_d = dt('norm_mlp_g', [L, D], F32, kind='ExternalInput').ap()
    wup_d = dt('w_up', [L, D, DFF], F32, kind='ExternalInput').ap()
    wdn_d = dt('w_down', [L, DFF, D], F32, kind='ExternalInput').ap()
    out_d = dt('out', [S, D], F32, kind='ExternalOutput').ap()
    sk = 'ExternalOutput' if dbg else 'Internal'
    xmid_d = dt('xmid', [S, D], F32, kind=sk).ap()
    xl_d = dt('xl', [S, D], F32, kind=sk).ap()
    va_d = dt('va', [S, D], BF16, kind=sk).ap()
    vb_d = dt('vb', [S, D], BF16, kind=sk).ap()
    qlo_d = dt('qlo', [4, 128, S], BF16, kind=sk).ap()
    qhi_d = dt('qhi', [4, 128, S], BF16, kind=sk).ap()
    ka_d = dt('ka', [4, 128, S], BF16, kind=sk).ap()
    gaT_d = dt('gaT', [8, 128, S], BF16, kind=sk).ap()
    gbT_d = dt('gbT', [8, 128, S], BF16, kind=sk).ap()
    qbT_d = dt('qbT', [8, 128, S], BF16, kind=sk).ap()
    kbT_d = dt('kbT', [8, 128, S], BF16, kind=sk).ap()
    yaT_d = dt('yaT', [8, 128, S], BF16, kind=sk).ap()
    winb_d = dt('winb', [D, INC], BF16, kind='Internal').ap()
    woutb_d = dt('woutb', [D, D], BF16, kind='Internal').ap()
    wupb_d = dt('wupb', [D, DFF], BF16, kind='Internal').ap()
    wdnb_d = dt('wdnb', [DFF, D], BF16, kind='Internal').ap()

    with ExitStack() as st:
        S_ = Sched(nc, st)
        op, dma = S_.op, S_.dma

        uid = [0]

        def sb(stack, name, shape, dtp):
            uid[0] += 1
            return stack.enter_context(nc.sbuf_tensor('%s_%d' % (name, uid[0]), shape, dtp))

        def ps(stack, name, shape, dtp):
            uid[0] += 1
            return stack.enter_context(nc.psum_tensor('%s_%d' % (name, uid[0]), shape, dtp))

        R1 = sb(st, 'R1', [128, 8, S], BF16)
        R2w = max(8 * S, 32 * 1024)
        R2 = sb(st, 'R2', [128, R2w], BF16)
        yT = R2[:, 0:8 * S].rearrange('p (c t) -> p c t', c=8)
        wdn = R2[:, 0:32 * 1024].rearrange('p (f n) -> p f n', f=32)
        hT = R1
        ident = sb(st, 'ident', [128, 128], BF16)
        cf = sb(st, 'cf', [128, 128], F32)
        triU = sb(st, 'triU', [128, 128], F32)
        onesf = sb(st, 'onesf', [128, 128], F32)
        triNeg = sb(st, 'triNeg', [128, 128], BF16)
        ones_bf = sb(st, 'ones_bf', [128, 128], BF16)
        negones = sb(st, 'negones', [128, 128], BF16)
        sel = sb(st, 'sel', [128, 4, 128], BF16)
        maskneg = sb(st, 'maskneg', [128, 4, 512], BF16)
        gqk = sb(st, 'gqk', [128, 2], F32)
        bgT = sb(st, 'bgT', [128, 16], F32)
        gmT = sb(st, 'gmT', [128, 8], F32)
        cwT = sb(st, 'cwT', [128, 8, 4], F32)
        bifB = sb(st, 'bifB', [128, 16], F32)
        gates = sb(st, 'gates', [128, NT, 2, 8], F32)
        decB = sb(st, 'decB', [128, NT, 4], F32)

        op('pool', lambda e: e.memset(cf[:], 1.0), writes=['cf'])
        op('pool', lambda e: e.affine_select(out=cf[:], in_=cf[:], pattern=[[1, 128]], compare_op=ALU.is_equal,
                                            fill=0.0, base=0, channel_multiplier=-1), reads=['cf'], writes=['cf'])
        op('dve', lambda e: e.tensor_copy(ident[:], cf[:]), reads=['cf'], writes=['ident'])
        op('pool', lambda e: e.memset(onesf[:], 1.0), writes=['onesf'])
        op('pool', lambda e: e.affine_select(out=triU[:], in_=onesf[:], pattern=[[1, 128]], compare_op=ALU.is_ge,
                                            fill=0.0, base=0, channel_multiplier=-1), reads=['onesf'], writes=['triU'])
        op('dve', lambda e: e.memset(cf[:], -1.0), reads=['ident'], writes=['cf'])
        op('pool', lambda e: e.affine_select(out=cf[:], in_=cf[:], pattern=[[-1, 128]], compare_op=ALU.is_ge,
                                            fill=0.0, base=0, channel_multiplier=1), reads=['cf'], writes=['cf'])
        op('dve', lambda e: e.tensor_copy(triNeg[:], cf[:]), reads=['cf'], writes=['triNeg'])
        op('dve', lambda e: e.memset(ones_bf[:], 1.0), writes=['ones_bf'])
        op('dve', lambda e: e.memset(negones[:], -1.0), writes=['negones'])
        op('dve', lambda e: e.memset(sel[:], 0.0), writes=['sel'])
        for (k, rows, val) in ((0, (0, 32), -1.0), (1, (64, 96), -1.0), (2, (0, 32), 1.0), (3, (64, 96), 1.0)):
            for r in rows:
                op('dve', lambda e: e.memset(sel[r:r + 1, k, :], val), writes=['sel'])
        mst = ExitStack()
        mtmp = sb(mst, 'mtmp', [128, 512], F32)
        for k in range(4):
            op('pool', lambda e: e.memset(mtmp[:], 0.0), reads=['mtmp'], writes=['mtmp'])
            op('pool', lambda e: e.affine_select(out=mtmp[:], in_=mtmp[:], pattern=[[1, 512]], compare_op=ALU.is_gt,
                                                fill=NEG, base=-128 * k, channel_multiplier=-1),
               reads=['mtmp'], writes=['mtmp'])
            op('dve', lambda e: e.tensor_copy(maskneg[:, k, :], mtmp[:]), reads=['mtmp'], writes=['maskneg'])
        S_.barrier()
        mst.close()

        for l in range(L):
            src_d = x_d if l == 0 else xl_d
            dst_d = out_d if l == L - 1 else xl_d
            S_.new_phase()
            dma('sp', bifB[:], bif_d[l:l + 1, :].broadcast_to([128, 16]), writes=['bifB'])
            with nc.allow_non_contiguous_dma(reason='tiny param transposes'):
                dma('sp', bgT[:], bg_d[l, :].rearrange('(c p) -> p c', p=128), writes=['bgT'])
                dma('sp', gqk[:, 0:1], gq_d[l, :].rearrange('(p o) -> p o', o=1), writes=['gqk'])
                dma('sp', gqk[:, 1:2], gk_d[l, :].rearrange('(p o) -> p o', o=1), writes=['gqk'])
                dma('sp', gmT[:], gm_d[l, :].rearrange('(c p) -> p c', p=128), writes=['gmT'])
                for jj in range(4):
                    dma('sp', cwT[:, :, jj], cw_d[l, jj, :].rearrange('(c p) -> p c', p=128), writes=['cwT'])
            op('dve', lambda e: e.tensor_scalar(out=gqk[:, 0:1], in0=gqk[:, 0:1], scalar1=128.0 ** -0.5, scalar2=None, op0=ALU.mult),
               reads=['gqk'], writes=['gqk'])

            def run_skewed(make_gen, ntiles):
                active = []
                nxt = 0
                while nxt < ntiles or active:
                    for gen in list(active):
                        try:
                            next(gen)
                        except StopIteration:
                            active.remove(gen)
                    if nxt < ntiles:
                        gen = make_gen(nxt)
                        nxt += 1
                        try:
                            next(gen)
                            active.append(gen)
                        except StopIteration:
                            pass

            with ExitStack() as ph:
                gB = sb(ph, 'gB', [128, D], F32)
                dma('sp', gB[:], g1_d[l:l + 1, :].broadcast_to([128, D]), writes=['gB'])
                xt = [sb(ph, 'xt%d' % i, [128, D], F32) for i in range(4)]
                junk = sb(ph, 'junk', [128, D], BF16)
                hn = [sb(ph, 'hn%d' % i, [128, D], BF16) for i in range(2)]
                stt = [sb(ph, 'stt%d' % i, [128, 4], F32) for i in range(4)]
                tp = [ps(ph, 'tpA%d' % i, [128, 8, 128], BF16) for i in range(2)]

                def a_tile(i):
                    b = i % 2
                    b4 = i % 4
                    dma('sp', xt[b4][:], src_d[i * 128:(i + 1) * 128, :], writes=[('xt', b4)])
                    op('dve', lambda e: e.scalar_tensor_tensor(out=junk[:], in0=xt[b4][:], scalar=1.0, in1=xt[b4][:],
                                                               op0=ALU.mult, op1=ALU.mult, accum_out=stt[b4][:, 0:1]),
                       reads=[('xt', b4)], writes=['junk', ('stt', b4)])
                    yield
                    op('act', lambda e: e.activation(out=stt[b4][:, 1:2], in_=stt[b4][:, 0:1], func=AF.Sqrt,
                                                     scale=1.0 / D, bias=EPS), reads=[('stt', b4)], writes=[('stt', b4)])
                    yield
                    op('dve', lambda e: e.reciprocal(stt[b4][:, 2:3], stt[b4][:, 1:2]), reads=[('stt', b4)], writes=[('stt', b4)])
                    op('dve', lambda e: e.scalar_tensor_tensor(out=hn[b][:], in0=xt[b4][:], scalar=stt[b4][:, 2:3], in1=gB[:],
                                                               op0=ALU.mult, op1=ALU.mult),
                       reads=[('xt', b4), ('stt', b4), 'gB'], writes=[('hn', b)])
                    yield
                    for k in range(8):
                        op('pe', lambda e: e.transpose(tp[b][:, k, :], hn[b][:, k * 128:(k + 1) * 128], ident[:]),
                           reads=[('hn', b), 'ident'], writes=[('tp', b)])
                    op('act', lambda e: e.activation(out=hT[:, :, i * 128:(i + 1) * 128], in_=tp[b][:], func=AF.Copy),
                       reads=[('tp', b)], writes=[('hT', i // 4)])

                run_skewed(a_tile, NT)
            if stop_after == 'A':
                break

            S_.new_phase()
            winb_keys = [('winb', r) for r in range(8)]
            with ExitStack() as ph:
                wb = [sb(ph, 'wb%d' % i, [128, 8, 512], BF16) for i in range(2)]
                pB = [ps(ph, 'pB%d' % i, [128, 512], F32) for i in range(3)]
                tpq = ps(ph, 'tpq', [128, 4, 128], BF16)
                pF = [ps(ph, 'pF%d' % i, [128, 512], F32) for i in range(2)]
                pI = ps(ph, 'pI', [128, 512], F32)
                pI2 = ps(ph, 'pI2', [128, 512], F32)
                vst = [sb(ph, 'vst%d' % i, [128, 512], BF16) for i in range(2)]
                sq = sb(ph, 'sq', [128, 512], F32)
                st4 = sb(ph, 'st4', [128, 12], F32)
                qnb = [sb(ph, 'qnb%d' % i, [128, 4, 128], BF16) for i in range(3)]
                qTs = [sb(ph, 'qTs%d' % i, [128, 4, 128], BF16) for i in range(2)]
                wf = [sb(ph, 'wf%d' % i, [128, 8, 128], BF16) for i in range(6)]
                pre2 = [sb(ph, 'pre2_%d' % i, [128, 3 + 512], F32) for i in range(2)]
                acc2 = [sb(ph, 'acc2_%d' % i, [128, 512], F32) for i in range(2)]
                qlo_s = [sb(ph, 'qlo_s%d' % i, [128, 512], BF16) for i in range(2)]
                qhi_s = [sb(ph, 'qhi_s%d' % i, [128, 512], BF16) for i in range(2)]
                so = sb(ph, 'so', [128, 512], F32)
                sg = sb(ph, 'sg', [128, 512], F32)
                gas = [sb(ph, 'gas%d' % i, [128, 512], BF16) for i in range(2)]
                wif = sb(ph, 'wif', [128, 8, 16], BF16)
                pif = [sb(ph, 'pif%d' % i, [128, 16], F32) for i in range(2)]
                t8 = [sb(ph, 't8%d' % i, [128, 40], F32) for i in range(2)]
                cum = [sb(ph, 'cum%d' % i, [128, 16], F32) for i in range(2)]

                if l == 0:
                    with nc.allow_non_contiguous_dma(reason='small gate weight rows'):
                        dma('pool', wif[:], win_d[l][:, IA:IA + 16].rearrange('(kc p) n -> p kc n', p=128), writes=['wif'])
                else:
                    with nc.allow_non_contiguous_dma(reason='small gate weight rows'):
                        dma('sp', wif[:], winb_d[:, IA:IA + 16].rearrange('(kc p) n -> p kc n', p=128), reads=winb_keys, writes=['wif'])

                def gate_s1(i):
                    j = i % 2
                    for kc in range(8):
                        op('pe', lambda e: e.matmul(pI[:, 0:16], lhsT=hT[:, kc, i * 128:(i + 1) * 128], rhs=wif[:, kc, :],
                                                    start=(kc == 0), stop=(kc == 7)),
                           reads=[('hT', i // 4), 'wif'], writes=['pI'])
                    op('dve', lambda e: e.tensor_tensor(out=pif[j][:], in0=pI[:, 0:16], in1=bifB[:], op=ALU.add),
                       reads=['pI', 'bifB'], writes=[('pif', j)])
                    op('act', lambda e: e.activation(out=t8[j][:, 0:8], in_=pif[j][:, 8:16], func=AF.Exp, scale=-1.0),
                       reads=[('pif', j)], writes=[('t8a', j)])
                    op('act', lambda e: e.activation(out=t8[j][:, 8:16], in_=t8[j][:, 0:8], func=AF.Ln, bias=1.0),
                       reads=[('t8a', j)], writes=[('t8b', j)])

                def gate_s2(i):
                    j = i % 2
                    op('pe', lambda e: e.matmul(pI2[:, 0:8], lhsT=triU[:], rhs=t8[j][:, 8:16], start=True, stop=True),
                       reads=[('t8b', j), 'triU'], writes=['pI2'])
                    op('pe', lambda e: e.matmul(pI2[:, 8:16], lhsT=onesf[:], rhs=t8[j][:, 8:16], start=True, stop=True),
                       reads=[('t8b', j), 'onesf'], writes=['pI2'])
                    op('act', lambda e: e.activation(out=cum[j][:], in_=pI2[:, 0:16], func=AF.Copy), reads=['pI2'], writes=[('cum', j)])
                    op('dve', lambda e: e.tensor_tensor(out=t8[j][:, 16:24], in0=cum[j][:, 0:8], in1=cum[j][:, 8:16], op=ALU.subtract),
                       reads=[('cum', j)], writes=[('t8c', j)])
                    op('dve', lambda e: e.tensor_tensor(out=t8[j][:, 24:32], in0=t8[j][:, 16:24], in1=pif[j][:, 0:8], op=ALU.add),
                       reads=[('t8c', j), ('pif', j)], writes=[('t8d', j)])
                    op('act', lambda e: e.activation(out=gates[:, i, 0, :], in_=t8[j][:, 24:32], func=AF.Exp, bias=-math.log(8.0)),
                       reads=[('t8d', j)], writes=[('gates', i, 0)])
                    op('act', lambda e: e.activation(out=gates[:, i, 1, :], in_=t8[j][:, 16:24], func=AF.Exp, scale=-1.0),
                       reads=[('t8c', j)], writes=[('gates', i, 1)])
                    op('act', lambda e: e.activation(out=t8[j][:, 32:40], in_=cum[j][:, 8:16], func=AF.Exp, scale=-1.0),
                       reads=[('cum', j)], writes=[('t8e', j)])
                    op('dve', lambda e: e.tensor_copy(decB[0:64, i, :], t8[j][0:64, 32:40:2]), reads=[('t8e', j)], writes=[('decB', i, 0)])
                    op('dve', lambda e: e.tensor_copy(decB[64:128, i, :], t8[j][64:128, 33:40:2]), reads=[('t8e', j)], writes=[('decB', i, 1)])

                groups = []
                for kind, c0 in (('va', VA), ('qb', QB), ('kb', KB), ('vb', VB)):
                    groups.append((kind, c0, 0))
                    groups.append((kind, c0 + 512, 1))
                n = 0
                nqk = 0
                nfin = 0
                pend = []
                def load_wb(gi):
                    c0g = groups[gi][1]
                    wbb_ = gi % 2
                    if l == 0:
                        dma('pool', wb[wbb_][:], win_d[l][:, c0g:c0g + 512].rearrange('(kc p) n -> p kc n', p=128),
                            writes=[('wb', wbb_)])
                    else:
                        dma('sp', wb[wbb_][:], winb_d[:, c0g:c0g + 512].rearrange('(kc p) n -> p kc n', p=128),
                            reads=winb_keys, writes=[('wb', wbb_)])

                load_wb(0)
                for gi, (kind, c0, half) in enumerate(groups):
                    wbb = gi % 2
                    if gi + 1 < len(groups):
                        load_wb(gi + 1)
                    for i in range(NT):
                        pb = n % 3
                        j = n % 2
                        n += 1
                        for kc in range(8):
                            op('pe', lambda e: e.matmul(pB[pb][:], lhsT=hT[:, kc, i * 128:(i + 1) * 128], rhs=wb[wbb][:, kc, :],
                                                        start=(kc == 0), stop=(kc == 7)),
                               reads=[('hT', i // 4), ('wb', wbb)], writes=[('pB', pb)])
                        if kind in ('va', 'vb'):
                            dstv = va_d if kind == 'va' else vb_d
                            op('act', lambda e: e.activation(out=vst[j][:], in_=pB[pb][:], func=AF.Copy),
                               reads=[('pB', pb)], writes=[('vst', j)])
                            dma('sp', dstv[i * 128:(i + 1) * 128, half * 512:(half + 1) * 512], vst[j][:], reads=[('vst', j)])
                        else:
                            gcol = 0 if kind == 'qb' else 1
                            dstT = qbT_d if kind == 'qb' else kbT_d
                            j3 = nqk % 3
                            nqk += 1
                            op('act', lambda e: e.activation(out=sq[:], in_=pB[pb][:], func=AF.Square),
                               reads=[('pB', pb)], writes=['sq'])
                            op('dve', lambda e: e.tensor_reduce(out=st4[:, 0:4], in_=sq[:].rearrange('p (h d) -> p h d', d=128),
                                                                axis=AX.X, op=ALU.add), reads=['sq'], writes=['st4'])
                            op('act', lambda e: e.activation(out=st4[:, 4:8], in_=st4[:, 0:4], func=AF.Sqrt, scale=1.0 / 128, bias=EPS),
                               reads=['st4'], writes=['st4'])
                            op('dve', lambda e: e.reciprocal(st4[:, 8:12], st4[:, 4:8]), reads=['st4'], writes=['st4'])
                            op('dve', lambda e: e.tensor_tensor(out=qnb[j3][:], in0=pB[pb][:].rearrange('p (h d) -> p h d', d=128),
                                                                in1=st4[:, 8:12].unsqueeze(2).broadcast_to([128, 4, 128]), op=ALU.mult),
                               reads=[('pB', pb), 'st4'], writes=[('qnb', j3)])

                            def fin(j3=j3, gcol=gcol, dstT=dstT, half=half, i=i):
                                nonlocal nfin
                                jj = nfin % 2
                                nfin += 1
                                for hh in range(4):
                                    op('pe', lambda e: e.transpose(tpq[:, hh, :], qnb[j3][:, hh, :], ident[:]),
                                       reads=[('qnb', j3), 'ident'], writes=['tpq'])
                                op('act', lambda e: e.activation(out=qTs[jj][:], in_=tpq[:], func=AF.Copy, scale=gqk[:, gcol:gcol + 1]),
                                   reads=['tpq', 'gqk'], writes=[('qTs', jj)])
                                dma('sp', dstT[half * 4:(half + 1) * 4, :, i * 128:(i + 1) * 128].rearrange('h d t -> d h t'),
                                    qTs[jj][:], reads=[('qTs', jj)])
                            pend.append(fin)
                            if len(pend) > 2:
                                pend.pop(0)()
                        if gi == GATE_GROUP:
                            gate_s1(i)
                            if i > 0:
                                gate_s2(i - 1)
                    if gi == GATE_GROUP:
                        gate_s2(NT - 1)
                while pend:
                    pend.pop(0)()
                if GATE_GROUP < 0:
                    for i in range(NT + 1):
                        if i < NT:
                            gate_s1(i)
                        if i > 0:
                            gate_s2(i - 1)

                nf = 0
                nw = 0

                wf_cols = [cc * 128 for cc in range(8)]
                for cc in range(8):
                    wf_cols += [OA + cc * 128, GA + cc * 128, GB + cc * 128]
                wf_issued = [0]

                def issue_wf():
                    idx = wf_issued[0]
                    if idx >= len(wf_cols):
                        return
                    wf_issued[0] += 1
                    c0w = wf_cols[idx]
                    kk = idx % 6
                    with nc.allow_non_contiguous_dma(reason='512B weight rows'):
                        if l == 0:
                            dma('pool', wf[kk][:], win_d[l][:, c0w:c0w + 128].rearrange('(kc p) n -> p kc n', p=128), writes=[('wf', kk)])
                        else:
                            dma('sp', wf[kk][:], winb_d[:, c0w:c0w + 128].rearrange('(kc p) n -> p kc n', p=128),
                                reads=winb_keys, writes=[('wf', kk)])

                def load_wf(c0):
                    nonlocal nw
                    assert wf_cols[nw] == c0
                    k = nw % 6
                    nw += 1
                    while wf_issued[0] < min(nw + 2, len(wf_cols)):
                        issue_wf()
                    return k

                issue_wf()
                issue_wf()

                pFb = [pF[0], pF[1], pI, pI2]
                pFk = [('pF', 0), ('pF', 1), 'pI', 'pI2']

                def fm_matmul(k, tb):
                    nonlocal nf
                    pb = nf % 4
                    nf += 1
                    for kc in range(8):
                        op('pe', lambda e: e.matmul(pFb[pb][:], lhsT=wf[k][:, kc, :], rhs=hT[:, kc, tb * 512:(tb + 1) * 512],
                                                    start=(kc == 0), stop=(kc == 7)),
                           reads=[('hT', tb), ('wf', k)], writes=[pFk[pb]])
                    return pb

                for i in range(2):
                    op('pool', lambda e: e.memset(qlo_s[i][:], 0.0), writes=[('qlo_s', i)])
                    op('pool', lambda e: e.memset(qhi_s[i][:], 0.0), writes=[('qhi_s', i)])
                nq = 0
                nblk = 0
                pendc = []
                for cc in range(8):
                    k = load_wf(cc * 128)
                    for tb in range(NB):
                        pb = fm_matmul(k, tb)
                        sl = nblk % 2
                        nblk += 1
                        P = pre2[sl]
                        Pp = pre2[1 - sl]
                        A = acc2[sl]
                        if tb == 0:
                            op('pool', lambda e: e.memset(P[:, 0:3], 0.0), writes=[('pre', sl)])
                        else:
                            op('pool', lambda e: e.tensor_copy(P[:, 0:3], Pp[:, 512:515]), reads=[('pre', 1 - sl)], writes=[('pre', sl)])
                        op('act', lambda e: e.activation(out=P[:, 3:515], in_=pFb[pb][:], func=AF.Copy),
                           reads=[pFk[pb]], writes=[('pre', sl)])
                        op('act', lambda e: e.activation(out=A[:], in_=pFb[pb][:], func=AF.Copy, scale=cwT[:, cc, 3:4]),
                           reads=[pFk[pb], 'cwT'], writes=[('acc', sl)])
                        for jj in range(0, 3):
                            op('dve', lambda e: e.scalar_tensor_tensor(out=A[:], in0=P[:, jj:jj + 512],
                                                                       scalar=cwT[:, cc, jj:jj + 1], in1=A[:],
                                                                       op0=ALU.mult, op1=ALU.add),
                               reads=[('pre', sl), 'cwT', ('acc', sl)], writes=[('acc', sl)])
                        def fin_conv(cc=cc, tb=tb, A=A, sl=sl):
                            nonlocal nq
                            j = nq % 2
                            nq += 1
                            if cc < 4:
                                op('act', lambda e: e.activation(out=qlo_s[j][0:64, :], in_=A[0:64, :], func=AF.Silu),
                                   reads=[('acc', sl)], writes=[('qlo_s', j)])
                                op('act', lambda e: e.activation(out=qhi_s[j][64:128, :], in_=A[64:128, :], func=AF.Silu),
                                   reads=[('acc', sl)], writes=[('qhi_s', j)])
                                dma('sp', qlo_d[cc, :, tb * 512:(tb + 1) * 512], qlo_s[j][:], reads=[('qlo_s', j)])
                                dma('sp', qhi_d[cc, :, tb * 512:(tb + 1) * 512], qhi_s[j][:], reads=[('qhi_s', j)])
                            else:
                                op('act', lambda e: e.activation(out=gas[j][:], in_=A[:], func=AF.Silu),
                                   reads=[('acc', sl)], writes=[('gas', j)])
                                dma('sp', ka_d[cc - 4, :, tb * 512:(tb + 1) * 512], gas[j][:], reads=[('gas', j)])
                        pendc.append(fin_conv)
                        if len(pendc) > 1:
                            pendc.pop(0)()
                while pendc:
                    pendc.pop(0)()
                for cc in range(8):
                    ko = load_wf(OA + cc * 128)
                    kg = load_wf(GA + cc * 128)
                    kb = load_wf(GB + cc * 128)
                    for tb in range(NB):
                        pb = fm_matmul(ko, tb)
                        op('act', lambda e: e.activation(out=so[:], in_=pFb[pb][:], func=AF.Sigmoid),
                           reads=[pFk[pb]], writes=['so'])
                        pb = fm_matmul(kg, tb)
                        op('act', lambda e: e.activation(out=sg[:], in_=pFb[pb][:], func=AF.Sigmoid, bias=bgT[:, cc:cc + 1]),
                           reads=[pFk[pb], 'bgT'], writes=['sg'])
                        j = nq % 2
                        nq += 1
                        op('dve', lambda e: e.scalar_tensor_tensor(out=gas[j][:], in0=sg[:], scalar=gmT[:, cc:cc + 1], in1=so[:],
                                                                   op0=ALU.mult, op1=ALU.mult),
                           reads=['sg', 'so', 'gmT'], writes=[('gas', j)])
                        dma('sp', gaT_d[cc, :, tb * 512:(tb + 1) * 512], gas[j][:], reads=[('gas', j)])
                        pb = fm_matmul(kb, tb)
                        j = nq % 2
                        nq += 1
                        op('act', lambda e: e.activation(out=gas[j][:], in_=pFb[pb][:], func=AF.Sigmoid, bias=bgT[:, 8 + cc:9 + cc]),
                           reads=[pFk[pb], 'bgT'], writes=[('gas', j)])
                        dma('sp', gbT_d[cc, :, tb * 512:(tb + 1) * 512], gas[j][:], reads=[('gas', j)])
            if stop_after == 'B':
                break

            S_.new_phase()
            for r in range(8):
                S_.dma_bg('pool', woutb_d[r * 128:(r + 1) * 128, :], wout_d[l][r * 128:(r + 1) * 128, :], ('woutb', r))
            for r in range(32):
                S_.dma_bg('pool', wdnb_d[r * 128:(r + 1) * 128, :], wdn_d[l][r * 128:(r + 1) * 128, :], ('wdnb', r))
            for r in range(8):
                S_.dma_bg('pool', wupb_d[r * 128:(r + 1) * 128, :], wup_d[l][r * 128:(r + 1) * 128, :], ('wupb', r))
            if l + 1 < L:
                for r in range(8):
                    S_.dma_bg('pool', winb_d[r * 128:(r + 1) * 128, :], win_d[l + 1][r * 128:(r + 1) * 128, :], ('winb', r))
            with ExitStack() as ph:
                qlo_c = [sb(ph, 'qlo_c%d' % i, [128, 4, 128], BF16) for i in range(4)]
                qhi_c = [sb(ph, 'qhi_c%d' % i, [128, 4, 128], BF16) for i in range(4)]
                k_c = [sb(ph, 'k_c%d' % i, [128, 4, 128], BF16) for i in range(4)]
                v_c = [sb(ph, 'v_c%d' % i, [128, D], BF16) for i in range(4)]
                ga_c = [sb(ph, 'ga_c%d' % i, [128, 8, 128], BF16) for i in range(4)]
                PT = [sb(ph, 'PT%d' % i, [128, 8, 128], BF16) for i in range(2)]
                kw = [sb(ph, 'kw%d' % i, [128, 8, 64], BF16) for i in range(2)]
                C_all = sb(ph, 'C_all', [128, 4, 128], F32)
                n_all = sb(ph, 'n_all', [128, 4], F32)
                Cd = sb(ph, 'Cd', [128, 4, 128], F32)
                nd = sb(ph, 'nd', [128, 4], F32)
                Cd_bf = sb(ph, 'Cd_bf', [128, 4, 128], BF16)
                nd_bf = sb(ph, 'nd_bf', [128, 4], BF16)
                sm = sb(ph, 'sm', [128, 96], F32)
                den_sb = [sb(ph, 'den_sb%d' % i, [128, 8], F32) for i in range(2)]
                sqb = sb(ph, 'sqb', [128, D], F32)
                ya = sb(ph, 'ya', [128, D], BF16)
                yag = [sb(ph, 'yag%d' % i, [128, 8, 128], BF16) for i in range(2)]
                scp = ps(ph, 'scp', [128, 4, 128], F32)
                nump = [[ps(ph, 'nump%d_%d' % (i, j), [128, 4, 128], F32) for j in range(2)] for i in range(2)]
                dCp = ps(ph, 'dCp', [128, 2, 256], F32)
                misc = ps(ph, 'misc', [128, 512], F32)
                tp = ps(ph, 'tpC', [128, 8, 128], BF16)
                op('dve', lambda e: e.memset(C_all[:], 0.0), writes=['C_all'])
                op('dve', lambda e: e.memset(n_all[:], 0.0), writes=['n_all'])

                def qm(c, h):
                    b3 = c % 4
                    return (qlo_c[b3] if h % 2 == 0 else qhi_c[b3])[:, h // 2, :]

                def qmk(c, h):
                    return ('qlo_c', c % 4) if h % 2 == 0 else ('qhi_c', c % 4)

                def loads(c):
                    b3 = c % 4
                    tsl = slice(c * 128, (c + 1) * 128)
                    with nc.allow_non_contiguous_dma(reason='256B rows'):
                        dma('sp', qlo_c[b3][:], qlo_d[:, :, tsl].rearrange('c p t -> p c t'), writes=[('qlo_c', b3)])
                        dma('sp', qhi_c[b3][:], qhi_d[:, :, tsl].rearrange('c p t -> p c t'), writes=[('qhi_c', b3)])
                        dma('sp', k_c[b3][:], ka_d[:, :, tsl].rearrange('c p t -> p c t'), writes=[('k_c', b3)])
                        dma('sp', ga_c[b3][:], gaT_d[:, :, tsl].rearrange('c p t -> p c t'), writes=[('ga_c', b3)])
                    dma('sp', v_c[b3][:], va_d[tsl, :], writes=[('v_c', b3)])

                def front(c):
                    b3 = c % 4
                    p2 = c % 2
                    for hg in range(2):
                        for h in range(4 * hg, 4 * hg + 4):
                            op('pe', lambda e: e.matmul(scp[:, h % 4, :], lhsT=k_c[b3][:, h // 2, :], rhs=qm(c, h), start=True, stop=True),
                               reads=[('k_c', b3), qmk(c, h)], writes=['scp'])
                        for h in range(4 * hg, 4 * hg + 4):
                            op('dve', lambda e: e.scalar_tensor_tensor(out=PT[p2][:, h, :], in0=scp[:, h % 4, :],
                                                                       scalar=gates[:, c, 0, h:h + 1], in1=triU[:],
                                                                       op0=ALU.mult, op1=ALU.mult),
                               reads=['scp', ('gates', c, 0), 'triU'], writes=[('PT', p2, h)])
                        yield
                    for cc in range(4):
                        op('pe', lambda e: e.transpose(tp[:, cc, :], k_c[b3][:, cc, :], ident[:]),
                           reads=[('k_c', b3), 'ident'], writes=['tp'])
                    op('dve', lambda e: e.tensor_tensor(out=kw[p2][:], in0=tp[:, 0:4, :].rearrange('p c (j d) -> p (c j) d', j=2),
                                                        in1=gates[:, c, 0, :].unsqueeze(2).broadcast_to([128, 8, 64]), op=ALU.mult),
                       reads=['tp', ('gates', c, 0)], writes=[('kw', p2)])

                def mid(c):
                    b3 = c % 4
                    p2 = c % 2
                    op('dve', lambda e: e.tensor_tensor(out=Cd[:], in0=C_all[:], in1=decB[:, c, :].unsqueeze(2).broadcast_to([128, 4, 128]),
                                                        op=ALU.mult), reads=['C_all', ('decB', c, 0), ('decB', c, 1)], writes=['Cd'])
                    op('dve', lambda e: e.tensor_tensor(out=nd[:], in0=n_all[:], in1=decB[:, c, :], op=ALU.mult),
                       reads=['n_all', ('decB', c, 0), ('decB', c, 1)], writes=['nd'])
                    op('act', lambda e: e.activation(out=Cd_bf[:], in_=Cd[:], func=AF.Copy), reads=['Cd'], writes=['Cd_bf'])
                    op('act', lambda e: e.activation(out=nd_bf[:], in_=nd[:], func=AF.Copy), reads=['nd'], writes=['nd_bf'])
                    yield
                    for h in range(NH):
                        if h == 4:
                            yield
                        npk = ('nump', p2, h // 4)
                        nt_ = nump[p2][h // 4]
                        op('pe', lambda e: e.matmul(nt_[:, h % 4, :], lhsT=PT[p2][:, h, :], rhs=v_c[b3][:, h * 128:(h + 1) * 128],
                                                    start=True, stop=False),
                           reads=[('PT', p2, h), ('v_c', b3)], writes=[npk])
                        op('pe', lambda e: e.matmul(nt_[:, h % 4, :], lhsT=qm(c, h), rhs=Cd_bf[:, h // 2, :],
                                                    start=False, stop=True),
                           reads=[qmk(c, h), 'Cd_bf'], writes=[npk])
                        dcol = 16 * p2 + h
                        op('pe', lambda e: e.matmul(misc[:, dcol:dcol + 1], lhsT=PT[p2][:, h, :], rhs=ones_bf[:, 0:1], start=True, stop=False),
                           reads=[('PT', p2, h), 'ones_bf'], writes=['misc'])
                        op('pe', lambda e: e.matmul(misc[:, dcol:dcol + 1], lhsT=qm(c, h), rhs=nd_bf[:, h // 2:h // 2 + 1], start=False, stop=True),
                           reads=[qmk(c, h), 'nd_bf'], writes=['misc'])
                    yield
                    op('dve', lambda e: e.tensor_copy(den_sb[p2][:], misc[:, 16 * p2:16 * p2 + 8]), reads=['misc'], writes=[('den_sb', p2)])
                    for g2 in range(2):
                        if g2 == 1:
                            yield
                        for cc in (2 * g2, 2 * g2 + 1):
                            op('pe', lambda e: e.matmul(dCp[:, cc % 2, :], lhsT=kw[p2][:, 2 * cc:2 * cc + 2, :].rearrange('p j d -> p (j d)'),
                                                        rhs=v_c[b3][:, cc * 256:(cc + 1) * 256], start=True, stop=True),
                               reads=[('kw', p2), ('v_c', b3)], writes=['dCp'])
                        op('dve', lambda e: e.tensor_tensor(out=C_all[0:64, 2 * g2:2 * g2 + 2, :], in0=Cd[0:64, 2 * g2:2 * g2 + 2, :],
                                                            in1=dCp[0:64, :, 0:128], op=ALU.add),
                           reads=['Cd', 'dCp'], writes=['C_all'])
                        op('dve', lambda e: e.tensor_tensor(out=C_all[64:128, 2 * g2:2 * g2 + 2, :], in0=Cd[64:128, 2 * g2:2 * g2 + 2, :],
                                                            in1=dCp[64:128, :, 128:256], op=ALU.add),
                           reads=['Cd', 'dCp'], writes=['C_all'])
                    for cc in range(4):
                        op('pe', lambda e: e.matmul(misc[:, 32 + cc:33 + cc], lhsT=kw[p2][:, 2 * cc:2 * cc + 2, :].rearrange('p j d -> p (j d)'),
                                                    rhs=ones_bf[:, 0:1], start=True, stop=True),
                           reads=[('kw', p2), 'ones_bf'], writes=['misc'])
                    op('dve', lambda e: e.tensor_tensor(out=n_all[:], in0=nd[:], in1=misc[:, 32:36], op=ALU.add),
                       reads=['nd', 'misc'], writes=['n_all'])

                def back(c):
                    b3 = c % 4
                    p2 = c % 2
                    tsl = slice(c * 128, (c + 1) * 128)
                    ebp = gates[:, c, 1, :]
                    op('dve', lambda e: e.tensor_tensor(out=sm[:, 0:8], in0=den_sb[p2][:], in1=ebp, op=ALU.mult),
                       reads=[('den_sb', p2), ('gates', c, 1)], writes=['sm0'])
                    op('dve', lambda e: e.scalar_tensor_tensor(out=sm[:, 8:16], in0=sm[:, 0:8], scalar=-1.0, in1=sm[:, 0:8],
                                                               op0=ALU.mult, op1=ALU.max), reads=['sm0'], writes=['sm1'])
                    op('dve', lambda e: e.tensor_scalar(out=sm[:, 16:24], in0=sm[:, 8:16], scalar1=1.0, scalar2=None, op0=ALU.max),
                       reads=['sm1'], writes=['sm2'])
                    op('dve', lambda e: e.reciprocal(sm[:, 24:32], sm[:, 16:24]), reads=['sm2'], writes=['sm3'])
                    op('dve', lambda e: e.tensor_tensor(out=sm[:, 32:40], in0=ebp, in1=sm[:, 24:32], op=ALU.mult),
                       reads=['sm3', ('gates', c, 1)], writes=['sm4'])
                    for g2 in range(2):
                        op('act', lambda e: e.activation(out=sqb[:, g2 * 512:(g2 + 1) * 512],
                                                         in_=nump[p2][g2][:].rearrange('p h d -> p (h d)'), func=AF.Square),
                           reads=[('nump', p2, g2)], writes=[('sqb', g2)])
                    yield
                    op('dve', lambda e: e.tensor_reduce(out=sm[:, 40:48], in_=sqb[:].rearrange('p (h d) -> p h d', d=128),
                                                        axis=AX.X, op=ALU.add), reads=[('sqb', 0), ('sqb', 1)], writes=['sm5'])
                    op('dve', lambda e: e.tensor_tensor(out=sm[:, 48:56], in0=sm[:, 32:40], in1=sm[:, 32:40], op=ALU.mult),
                       reads=['sm4'], writes=['sm6'])
                    op('dve', lambda e: e.tensor_tensor(out=sm[:, 56:64], in0=sm[:, 48:56], in1=sm[:, 40:48], op=ALU.mult),
                       reads=['sm6', 'sm5'], writes=['sm7'])
                    op('act', lambda e: e.activation(out=sm[:, 64:72], in_=sm[:, 56:64], func=AF.Sqrt, scale=1.0 / 128, bias=EPS),
                       reads=['sm7'], writes=['sm8'])
                    op('dve', lambda e: e.reciprocal(sm[:, 72:80], sm[:, 64:72]), reads=['sm8'], writes=['sm9'])
                    op('dve', lambda e: e.tensor_tensor(out=sm[:, 80:88], in0=sm[:, 32:40], in1=sm[:, 72:80], op=ALU.mult),
                       reads=['sm9', 'sm4'], writes=['sm10'])
                    yield
                    for h in range(NH):
                        op('act', lambda e: e.activation(out=ya[:, h * 128:(h + 1) * 128], in_=nump[p2][h // 4][:, h % 4, :],
                                                         func=AF.Copy, scale=sm[:, 80 + h:81 + h]),
                           reads=[('nump', p2, h // 4), 'sm10'], writes=[('ya', h)])
                    yield
                    for h in range(NH):
                        op('pe', lambda e: e.transpose(tp[:, h, :], ya[:, h * 128:(h + 1) * 128], ident[:]),
                           reads=[('ya', h), 'ident'], writes=['tp'])
                    op('dve', lambda e: e.tensor_tensor(out=yag[p2][:], in0=tp[:], in1=ga_c[b3][:], op=ALU.mult),
                       reads=['tp', ('ga_c', b3)], writes=[('yag', p2)])
                    with nc.allow_non_contiguous_dma(reason='256B rows'):
                        dma('sp', yaT_d[:, :, tsl].rearrange('c p t -> p c t'), yag[p2][:], reads=[('yag', p2)])

                loads(0)
                for it in range(NT + 2):
                    if it + 1 < NT:
                        loads(it + 1)
                    gens = []
                    if 0 <= it - 2 < NT:
                        gens.append(back(it - 2))
                    if 0 <= it - 1 < NT:
                        gens.append(mid(it - 1))
                    if it < NT:
                        gens.append(front(it))
                    while gens:
                        for gen in list(gens):
                            try:
                                next(gen)
                            except StopIteration:
                                gens.remove(gen)
            if stop_after == 'C':
                break

            S_.new_phase()
            with ExitStack() as ph:
                knT = [sb(ph, 'knT%d' % i, [128, S], BF16) for i in range(2)]
                qnT = [sb(ph, 'qnT%d' % i, [128, S], BF16) for i in range(2)]
                Vh = [sb(ph, 'Vh%d' % i, [128, NT, 128], BF16) for i in range(2)]
                Eb = [sb(ph, 'Eb', [128, 2, 512], F32)] * 2
                Lp = [sb(ph, 'Lp%d' % i, [128, 2, 512], BF16) for i in range(2)]
                AT = [sb(ph, 'AT%d' % i, [128, 2, 512], BF16) for i in range(2)]
                gb_b = [sb(ph, 'gb_b%d' % i, [128, 512], BF16) for i in range(2)]
                ya_b = [sb(ph, 'ya_b%d' % i, [128, 512], BF16) for i in range(2)]
                c2l = [sb(ph, 'c2_%d' % i, [64, 512], BF16) for i in range(2)]
                ytmp = sb(ph, 'ytmp', [128, 512], F32)
                zA = [ps(ph, 'zA%d' % i, [128, 2, 512], F32) for i in range(3)]
                yacc = ps(ph, 'yacc', [128, 512], F32)
                csp = ps(ph, 'csp', [128, 512], F32)

                def load_head(h):
                    hp = h % 2
                    dma('sp', knT[hp][:], kbT_d[h, :, :], writes=[('knT', hp)])
                    dma('sp', qnT[hp][:], qbT_d[h, :, :], writes=[('qnT', hp)])
                    with nc.allow_non_contiguous_dma(reason='256B rows'):
                        dma('sp', Vh[hp][:], vb_d[:, h * 128:(h + 1) * 128].rearrange('(n p) d -> p n d', p=128),
                            writes=[('Vh', hp)])

                blocks = [(h, qb) for h in range(NH) for qb in range(NB)]

                def load_block(bi):
                    h, qb = blocks[bi]
                    yb = bi % 2
                    qsl = slice(qb * 512, (qb + 1) * 512)
                    dma('sp', gb_b[yb][:], gbT_d[h, :, qsl], writes=[('gb_b', yb)])
                    dma('sp', ya_b[yb][:], yaT_d[h, :, qsl], writes=[('ya_b', yb)])

                tiles = []
                for bi, (h, qb) in enumerate(blocks):
                    ktmax = 4 * qb + 3
                    for n, kt in enumerate(range(ktmax, -1, -2)):
                        tiles.append((bi, h, qb, n, kt))
                G = len(tiles)

                def emit_zA(g):
                    bi, h, qb, n, kt = tiles[g]
                    hp = h % 2
                    k3 = g % 3
                    for u in range(2):
                        ktu = kt - u
                        diag = ktu >= 4 * qb
                        op('pe', lambda e: e.matmul(zA[k3][:, u, :], lhsT=knT[hp][:, ktu * 128:(ktu + 1) * 128],
                                                    rhs=qnT[hp][:, qb * 512:(qb + 1) * 512], start=True, stop=(not diag)),
                           reads=[('knT', hp), ('qnT', hp)], writes=[('zA', k3, u)])
                        if diag:
                            op('pe', lambda e: e.matmul(zA[k3][:, u, :], lhsT=ident[:], rhs=maskneg[:, ktu - 4 * qb, :],
                                                        start=False, stop=True),
                               reads=['ident', 'maskneg'], writes=[('zA', k3, u)])

                def emit_tail(g):
                    bi, h, qb, n, kt = tiles[g]
                    hp = h % 2
                    yb = bi % 2
                    k3 = g % 3
                    op('act', lambda e: e.activation(out=AT[g % 2][:], in_=zA[k3][:], func=AF.Exp),
                       reads=[('zA', k3, 0), ('zA', k3, 1)], writes=[('AT', g % 2)])
                    for u in range(2):
                        op('pe', lambda e: e.matmul(yacc[:], lhsT=Vh[hp][:, kt - u, :], rhs=AT[g % 2][:, u, :],
                                                    start=(n == 0 and u == 0), stop=(kt - u == 0)),
                           reads=[('Vh', hp), ('AT', g % 2)], writes=['yacc'])
                    if kt - 1 == 0:
                        qsl = slice(qb * 512, (qb + 1) * 512)
                        op('dve', lambda e: e.tensor_tensor(out=ytmp[:], in0=yacc[:], in1=gb_b[yb][:], op=ALU.mult),
                           reads=['yacc', ('gb_b', yb)], writes=['ytmp'])
                        op('dve', lambda e: e.tensor_tensor(out=yT[:, h, qsl], in0=ytmp[:], in1=ya_b[yb][:], op=ALU.add),
                           reads=['ytmp', ('ya_b', yb)], writes=[('yT', qb)])

                for i in range(2):
                    op('dve', lambda e: e.memset(c2l[i][:], 0.0), writes=[('c2', i)])
                load_head(0)
                if NH > 1:
                    load_head(1)
                load_block(0)
                emit_zA(0)
                for g in range(G):
                    bi, h, qb, n, kt = tiles[g]
                    k3 = g % 3
                    last = (kt - 1 == 0)
                    if g + 1 < G:
                        emit_zA(g + 1)
                    op('act', lambda e: e.activation(out=Eb[g % 2][:], in_=zA[k3][:], func=AF.Exp),
                       reads=[('zA', k3, 0), ('zA', k3, 1)], writes=['Eb'])
                    op('act', lambda e: e.activation(out=Lp[g % 2][:], in_=Eb[g % 2][:], func=AF.Ln, bias=1.0),
                       reads=['Eb'], writes=[('Lp', g % 2)])
                    Lg = Lp[g % 2]
                    c2p, c2pk = c2l[(g - 1) % 2], ('c2', (g - 1) % 2)
                    c2n, c2nk = c2l[g % 2], ('c2', g % 2)
                    lk = ('Lp', g % 2)
                    if not last:
                        op('pe', lambda e: e.matmul(csp[0:64, :], lhsT=ones_bf[:, 0:64], rhs=Lg[:, 0, :], start=True, stop=False),
                           reads=['ones_bf', lk], writes=['csp'])
                        op('pe', lambda e: e.matmul(csp[0:64, :], lhsT=ones_bf[:, 0:64], rhs=Lg[:, 1, :], start=False, stop=(n == 0)),
                           reads=['ones_bf', lk], writes=['csp'])
                        if n > 0:
                            op('pe', lambda e: e.matmul(csp[0:64, :], lhsT=sel[0:64, 2, 0:64], rhs=c2p[:], start=False, stop=True),
                               reads=['sel', c2pk], writes=['csp'])
                    op('pe', lambda e: e.matmul(zA[k3][:, 0, :], lhsT=triNeg[:], rhs=Lg[:, 0, :], start=False, stop=(n == 0),
                                                skip_group_check=True),
                       reads=['triNeg', lk], writes=[('zA', k3, 0)])
                    op('pe', lambda e: e.matmul(zA[k3][:, 1, :], lhsT=triNeg[:], rhs=Lg[:, 1, :], start=False, stop=False,
                                                skip_group_check=True),
                       reads=['triNeg', lk], writes=[('zA', k3, 1)])
                    op('pe', lambda e: e.matmul(zA[k3][:, 1, :], lhsT=negones[:], rhs=Lg[:, 0, :], start=False, stop=(n == 0),
                                                skip_group_check=True),
                       reads=['negones', lk], writes=[('zA', k3, 1)])
                    if n > 0:
                        for u in range(2):
                            op('pe', lambda e: e.matmul(zA[k3][:, u, :], lhsT=sel[0:64, 0, :], rhs=c2p[:], start=False, stop=True,
                                                        skip_group_check=True),
                               reads=['sel', c2pk], writes=[('zA', k3, u)])
                    if not last:
                        op('dve', lambda e: e.tensor_copy(c2n[:], csp[0:64, :]), reads=['csp'], writes=[c2nk])
                        op('dve', lambda e: e.tensor_tensor(out=c2n[32:64, :], in0=csp[32:64, :], in1=c2n[32:64, :], op=ALU.subtract),
                           reads=['csp', c2nk], writes=[c2nk])
                    if g > 0:
                        emit_tail(g - 1)
                    if n == 0:
                        if bi + 1 < len(blocks):
                            load_block(bi + 1)
                        if qb == 0 and h >= 1 and h + 1 < NH:
                            load_head(h + 1)
                emit_tail(G - 1)
            if stop_after == 'D':
                break

            S_.new_phase()
            with ExitStack() as ph:
                gB = sb(ph, 'gB', [128, D], F32)
                dma('sp', gB[:], g2_d[l:l + 1, :].broadcast_to([128, D]), writes=['gB'])
                wo = sb(ph, 'wo', [128, 8, D], BF16)
                xt = [sb(ph, 'xt%d' % i, [128, D], F32) for i in range(2)]
                x1 = [sb(ph, 'x1%d' % i, [128, D], F32) for i in range(4)]
                junk = sb(ph, 'junk', [128, D], BF16)
                hn = [sb(ph, 'hn%d' % i, [128, D], BF16) for i in range(2)]
                stt = [sb(ph, 'stt%d' % i, [128, 4], F32) for i in range(4)]
                pO = [ps(ph, 'pO%d' % i, [128, 512], F32) for i in range(4)]
                tp = [ps(ph, 'tpE%d' % i, [128, 8, 128], BF16) for i in range(2)]
                dma('sp', wo[:], woutb_d.rearrange('(c p) n -> p c n', p=128), reads=[('woutb', r) for r in range(8)], writes=['wo'])

                def e_tile(i):
                    b = i % 2
                    b4 = i % 4
                    tsl = slice(i * 128, (i + 1) * 128)
                    dma('sp', xt[b][:], src_d[tsl, :], writes=[('xt', b)])
                    for half in range(2):
                        pk = 2 * b + half
                        for cc in range(8):
                            op('pe', lambda e: e.matmul(pO[pk][:], lhsT=yT[:, cc, tsl], rhs=wo[:, cc, half * 512:(half + 1) * 512],
                                                        start=(cc == 0), stop=(cc == 7)),
                               reads=[('yT', i // 4), 'wo'], writes=[('pO', pk)])
                        op('dve', lambda e: e.tensor_tensor(out=x1[b4][:, half * 512:(half + 1) * 512], in0=xt[b][:, half * 512:(half + 1) * 512],
                                                            in1=pO[pk][:], op=ALU.add),
                           reads=[('xt', b), ('pO', pk)], writes=[('x1', b4)])
                    dma('sp', xmid_d[tsl, :], x1[b4][:], reads=[('x1', b4)])
                    op('dve', lambda e: e.scalar_tensor_tensor(out=junk[:], in0=x1[b4][:], scalar=1.0, in1=x1[b4][:],
                                                               op0=ALU.mult, op1=ALU.mult, accum_out=stt[b4][:, 0:1]),
                       reads=[('x1', b4)], writes=['junk', ('stt', b4)])
                    yield
                    op('act', lambda e: e.activation(out=stt[b4][:, 1:2], in_=stt[b4][:, 0:1], func=AF.Sqrt,
                                                     scale=1.0 / D, bias=EPS), reads=[('stt', b4)], writes=[('stt', b4)])
                    yield
                    op('dve', lambda e: e.reciprocal(stt[b4][:, 2:3], stt[b4][:, 1:2]), reads=[('stt', b4)], writes=[('stt', b4)])
                    op('dve', lambda e: e.scalar_tensor_tensor(out=hn[b][:], in0=x1[b4][:], scalar=stt[b4][:, 2:3], in1=gB[:],
                                                               op0=ALU.mult, op1=ALU.mult),
                       reads=[('x1', b4), ('stt', b4), 'gB'], writes=[('hn', b)])
                    yield
                    for k in range(8):
                        op('pe', lambda e: e.transpose(tp[b][:, k, :], hn[b][:, k * 128:(k + 1) * 128], ident[:]),
                           reads=[('hn', b), 'ident'], writes=[('tp', b)])
                    op('act', lambda e: e.activation(out=hT[:, :, tsl], in_=tp[b][:], func=AF.Copy),
                       reads=[('tp', b)], writes=[('hT', i // 4)])

                run_skewed(e_tile, NT)
            if stop_after == 'E1':
                break

            S_.new_phase()
            with ExitStack() as ph:
                wu = [sb(ph, 'wu%d' % i, [128, 8, 512], BF16) for i in range(2)]
                rr = [sb(ph, 'rr%d' % i, [128, 512], F32) for i in range(2)]
                aT = sb(ph, 'aT', [128, 32, 512], BF16)
                x1t = [sb(ph, 'x1t%d' % i, [128, D], F32) for i in range(2)]
                pU = [ps(ph, 'pU%d' % i, [128, 512], F32) for i in range(2)]
                pD = [ps(ph, 'pD%d' % i, [128, 512], F32) for i in range(4)]
                for q4 in range(4):
                    dma('sp', wdn[:, q4 * 8:(q4 + 1) * 8, :], wdnb_d[q4 * 1024:(q4 + 1) * 1024, :].rearrange('(f p) n -> p f n', p=128),
                        reads=[('wdnb', r) for r in range(q4 * 8, q4 * 8 + 8)], writes=[('wdn', q4)])
                nu = 0
                nx = 0
                for tb in range(NB):
                    for g in range(8):
                        wbb = (tb * 8 + g) % 2
                        dma('sp', wu[wbb][:], wupb_d[:, g * 512:(g + 1) * 512].rearrange('(kc p) n -> p kc n', p=128),
                            reads=[('wupb', r) for r in range(8)], writes=[('wu', wbb)])
                        for f in range(4):
                            fc = 4 * g + f
                            pb = nu % 2
                            nu += 1
                            for kc in range(8):
                                op('pe', lambda e: e.matmul(pU[pb][:], lhsT=wu[wbb][:, kc, f * 128:(f + 1) * 128],
                                                            rhs=hT[:, kc, tb * 512:(tb + 1) * 512], start=(kc == 0), stop=(kc == 7)),
                                   reads=[('wu', wbb), ('hT', tb)], writes=[('pU', pb)])
                            op('act', lambda e: e.activation(out=rr[pb][:], in_=pU[pb][:], func=AF.Relu),
                               reads=[('pU', pb)], writes=[('rr', pb)])
                            op('pool', lambda e: e.tensor_tensor(out=aT[:, fc, :], in0=rr[pb][:], in1=rr[pb][:], op=ALU.mult),
                               reads=[('rr', pb)], writes=[('aT', fc)])
                    for ts in range(4):
                        i = tb * 4 + ts
                        xb = nx % 2
                        nx += 1
                        tsl = slice(i * 128, (i + 1) * 128)
                        dma('sp', x1t[xb][:], xmid_d[tsl, :], writes=[('x1t', xb)])
                        for half in range(2):
                            pk = 2 * xb + half
                            for fc in range(32):
                                op('pe', lambda e: e.matmul(pD[pk][:], lhsT=aT[:, fc, ts * 128:(ts + 1) * 128],
                                                            rhs=wdn[:, fc, half * 512:(half + 1) * 512], start=(fc == 0), stop=(fc == 31)),
                                   reads=[('aT', fc), ('wdn', fc // 8)], writes=[('pD', pk)])
                            op('dve', lambda e: e.tensor_tensor(out=x1t[xb][:, half * 512:(half + 1) * 512],
                                                                in0=x1t[xb][:, half * 512:(half + 1) * 512], in1=pD[pk][:], op=ALU.add),
                               reads=[('x1t', xb), ('pD', pk)], writes=[('x1t', xb)])
                        dma('sp', dst_d[tsl, :], x1t[xb][:], reads=[('x1t', xb)])
        S_.finish('sp')
    return nc


_CACHE = {}


def kernel(**inputs):
    x = np.ascontiguousarray(np.asarray(inputs['x'], dtype=np.float32))
    B, S, _ = x.shape
    L = int(np.asarray(inputs['w_in']).shape[0])
    key = (S, L)
    if key not in _CACHE:
        _CACHE[key] = build(S, L)
    nc = _CACHE[key]
    names = ['norm_mix_g', 'w_in', 'b_if', 'b_gate', 'conv_w', 'mlstm_norm_g', 'sb_q_norm_g', 'sb_k_norm_g',
             'w_out', 'norm_mlp_g', 'w_up', 'w_down']
    shared = {k: np.ascontiguousarray(np.asarray(inputs[k], dtype=np.float32)) for k in names}
    n = 8
    in_maps = []
    for c in range(n):
        m = dict(shared)
        m['x'] = x[c % B]
        in_maps.append(m)
    res = run_bass_kernel_spmd(nc, in_maps, core_ids=list(range(n)))
    return np.stack([res.results[b]['out'] for b in range(B)], axis=0).astype(np.float32)
```

```python
import math
from contextlib import ExitStack

import numpy as np
import concourse.bass as bass
import concourse.mybir as mybir
from concourse.bass_utils import run_bass_kernel_spmd

F32 = mybir.dt.float32
BF16 = mybir.dt.bfloat16
AF = mybir.ActivationFunctionType
ALU = mybir.AluOpType
AX = mybir.AxisListType

D = 1024
NH = 8
EPS = 1e-6
QA, KA, VA, OA, IA, FA, QB, KB, VB, GA, GB = 0, 512, 1024, 2048, 3072, 3080, 3088, 4112, 5136, 6160, 7184
INC = 8208
DFF = 4096
NEG = -30000.0
GATE_GROUP = -1
import os
PC_STAGES = os.environ.get('PC_STAGES', 'fmb')


class Sched:
    def __init__(self, nc, st, n_dma_sems=22):
        self.nc = nc
        self.st = st
        self.eng = {'pe': nc.tensor, 'act': nc.scalar, 'dve': nc.vector, 'pool': nc.gpsimd, 'sp': nc.sync}
        self.sem = {}
        self.cnt = {}
        self.nphase = 0
        for e in ['pe', 'act', 'dve', 'pool']:
            self.sem[e] = st.enter_context(nc.semaphore('s_%s_0' % e))
            self.cnt[e] = 0
        self.dsem = [st.enter_context(nc.semaphore('d%d' % i)) for i in range(n_dma_sems)]
        self.bsem = [st.enter_context(nc.semaphore('b%d' % i)) for i in range(6)]
        self.bcnt = [0] * 6
        self.bnext = 0
        self.bg = {}
        self.dcnt = [0] * n_dma_sems
        self.dnext = 0
        self.npool0 = n_dma_sems - 6
        self.dnext_pool = self.npool0
        self.known = {e: {} for e in self.eng}
        self.last_w = {}
        self.readers = {}
        self.log = None
        self._cur = None

    def _wait(self, e, tk):
        key, val, src = tk
        if self.known[e].get(key, 0) >= val:
            return
        if self.log is not None and self._cur is not None:
            self._cur.append((key, val))
        if isinstance(key, str):
            sem = self.sem[key]
        elif isinstance(key, tuple):
            sem = self.bsem[key[1]]
        else:
            sem = self.dsem[key]
        self.eng[e].wait_ge(sem, val)
        self.known[e][key] = val

    def _deps(self, e, reads, writes):
        for r in reads:
            w = self.last_w.get(r)
            if w is not None:
                self._wait(e, w)
            w = self.bg.get(r)
            if w is not None:
                self._wait(e, w)
        for r in writes:
            w = self.last_w.get(r)
            if w is not None and (w[2] != e or e != 'pe'):
                self._wait(e, w)
            for rd in self.readers.get(r, {}).values():
                if rd[2] != e or e != 'pe':
                    self._wait(e, rd)

    def _commit(self, tk, reads, writes):
        for r in reads:
            self.readers.setdefault(r, {})[tk[2]] = tk
        for r in writes:
            self.last_w[r] = tk
            self.readers[r] = {}

    def op(self, e, fn, reads=(), writes=()):
        if self.log is not None:
            self._cur = []
        self._deps(e, reads, writes)
        inst = fn(self.eng[e])
        self.cnt[e] += 1
        if self.log is not None:
            self.log.append((e, self.cnt[e], list(reads), list(writes), self._cur))
            self._cur = None
        inst.then_inc(self.sem[e], 1)
        tk = (e, self.cnt[e], e)
        self._commit(tk, reads, writes)
        return tk

    def dma(self, q, out, in_, reads=(), writes=()):
        if q == 'pool':
            j = self.dnext_pool
            self.dnext_pool = self.npool0 + (self.dnext_pool + 1 - self.npool0) % (len(self.dsem) - self.npool0)
        else:
            j = self.dnext
            self.dnext = (self.dnext + 1) % self.npool0
        if self.dcnt[j] > 0:
            self._wait(q, (j, self.dcnt[j], 'dma'))
        self._deps(q, reads, writes)
        inst = self.eng[q].dma_start(out=out, in_=in_)
        self.dcnt[j] += 16
        inst.then_inc(self.dsem[j], 16)
        tk = (j, self.dcnt[j], 'dma%d' % j)
        self._commit(tk, reads, writes)
        return tk

    def dma_bg(self, q, out, in_, key):
        j = self.bnext
        self.bnext = (self.bnext + 1) % len(self.bsem)
        if self.bcnt[j] > 0:
            self._wait(q, (('b', j), self.bcnt[j], 'bg'))
        inst = self.eng[q].dma_start(out=out, in_=in_)
        self.bcnt[j] += 16
        inst.then_inc(self.bsem[j], 16)
        self.bg[key] = (('b', j), self.bcnt[j], 'bg')

    def _all_tickets(self):
        tks = [(e, self.cnt[e], e) for e in self.cnt if self.cnt[e] > 0]
        tks += [(j, self.dcnt[j], 'dma') for j in range(len(self.dsem)) if self.dcnt[j] > 0]
        return tks

    def barrier(self):
        tks = self._all_tickets()
        for e in self.eng:
            for tk in tks:
                if tk[2] == e:
                    continue
                self._wait(e, tk)
        self.last_w = {}
        self.readers = {}

    def new_phase(self):
        self.barrier()
        self.nphase += 1
        for e in ['pe', 'act', 'dve', 'pool']:
            self.sem[e] = self.st.enter_context(self.nc.semaphore('s_%s_%d' % (e, self.nphase)))
            self.cnt[e] = 0
        for e in self.known:
            self.known[e] = {k: v for k, v in self.known[e].items() if not isinstance(k, str)}

    def finish(self, e='sp'):
        for tk in self._all_tickets():
            self._wait(e, tk)
        for j in range(len(self.bsem)):
            if self.bcnt[j] > 0:
                self._wait(e, (('b', j), self.bcnt[j], 'bg'))


def build(S, L, dbg=False, stop_after=None):
    NT = S // 128
    NB = S // 512
    nc = bass.Bass('TRN2', target_bir_lowering=False)
    dt = nc.dram_tensor
    x_d = dt('x', [S, D], F32, kind='ExternalInput').ap()
    g1_d = dt('norm_mix_g', [L, D], F32, kind='ExternalInput').ap()
    win_d = dt('w_in', [L, D, INC], F32, kind='ExternalInput').ap()
    bif_d = dt('b_if', [L, 16], F32, kind='ExternalInput').ap()
    bg_d = dt('b_gate', [L, 2 * D], F32, kind='ExternalInput').ap()
    cw_d = dt('conv_w', [L, 4, D], F32, kind='ExternalInput').ap()
    gm_d = dt('mlstm_norm_g', [L, D], F32, kind='ExternalInput').ap()
    gq_d = dt('sb_q_norm_g', [L, 128], F32, kind='ExternalInput').ap()
    gk_d = dt('sb_k_norm_g', [L, 128], F32, kind='ExternalInput').ap()
    wout_d = dt('w_out', [L, D, D], F32, kind='ExternalInput').ap()
    g2_d = dt('norm_mlp_g', [L, D], F32, kind='ExternalInput').ap()
    wup_d = dt('w_up', [L, D, DFF], F32, kind='ExternalInput').ap()
    wdn_d = dt('w_down', [L, DFF, D], F32, kind='ExternalInput').ap()
    out_d = dt('out', [S, D], F32, kind='ExternalOutput').ap()
    sk = 'ExternalOutput' if dbg else 'Internal'
    xmid_d = dt('xmid', [S, D], F32, kind=sk).ap()
    xl_d = dt('xl', [S, D], F32, kind=sk).ap()
    va_d = dt('va', [S, D], BF16, kind=sk).ap()
    vb_d = dt('vb', [S, D], BF16, kind=sk).ap()
    qlo_d = dt('qlo', [4, 128, S], BF16, kind=sk).ap()
    qhi_d = dt('qhi', [4, 128, S], BF16, kind=sk).ap()
    ka_d = dt('ka', [4, 128, S], BF16, kind=sk).ap()
    gaT_d = dt('gaT', [8, 128, S], BF16, kind=sk).ap()
    gbT_d = dt('gbT', [8, 128, S], BF16, kind=sk).ap()
    qbT_d = dt('qbT', [8, 128, S], BF16, kind=sk).ap()
    kbT_d = dt('kbT', [8, 128, S], BF16, kind=sk).ap()
    yaT_d = dt('yaT', [8, 128, S], BF16, kind=sk).ap()
    winb_d = dt('winb', [D, INC], BF16, kind='Internal').ap()
    woutb_d = dt('woutb', [D, D], BF16, kind='Internal').ap()
    wupb_d = dt('wupb', [D, DFF], BF16, kind='Internal').ap()
    wdnb_d = dt('wdnb', [DFF, D], BF16, kind='Internal').ap()

    with ExitStack() as st:
        S_ = Sched(nc, st)
        op, dma = S_.op, S_.dma

        uid = [0]

        def sb(stack, name, shape, dtp):
            uid[0] += 1
            return stack.enter_context(nc.sbuf_tensor('%s_%d' % (name, uid[0]), shape, dtp))

        def ps(stack, name, shape, dtp):
            uid[0] += 1
            return stack.enter_context(nc.psum_tensor('%s_%d' % (name, uid[0]), shape, dtp))

        R1 = sb(st, 'R1', [128, 8, S], BF16)
        R2w = max(8 * S, 32 * 1024)
        R2 = sb(st, 'R2', [128, R2w], BF16)
        yT = R2[:, 0:8 * S].rearrange('p (c t) -> p c t', c=8)
        wdn = R2[:, 0:32 * 1024].rearrange('p (f n) -> p f n', f=32)
        hT = R1
        ident = sb(st, 'ident', [128, 128], BF16)
        cf = sb(st, 'cf', [128, 128], F32)
        triU = sb(st, 'triU', [128, 128], F32)
        onesf = sb(st, 'onesf', [128, 128], F32)
        triNeg = sb(st, 'triNeg', [128, 128], BF16)
        ones_bf = sb(st, 'ones_bf', [128, 128], BF16)
        negones = sb(st, 'negones', [128, 128], BF16)
        sel = sb(st, 'sel', [128, 4, 128], BF16)
        maskneg = sb(st, 'maskneg', [128, 4, 512], BF16)
        gqk = sb(st, 'gqk', [128, 2], F32)
        bgT = sb(st, 'bgT', [128, 16], F32)
        gmT = sb(st, 'gmT', [128, 8], F32)
        cwT = sb(st, 'cwT', [128, 8, 4], F32)
        bifB = sb(st, 'bifB', [128, 16], F32)
        gates = sb(st, 'gates', [128, NT, 2, 8], F32)
        decB = sb(st, 'decB', [128, NT, 4], F32)

        op('pool', lambda e: e.memset(cf[:], 1.0), writes=['cf'])
        op('pool', lambda e: e.affine_select(out=cf[:], in_=cf[:], pattern=[[1, 128]], compare_op=ALU.is_equal,
                                            fill=0.0, base=0, channel_multiplier=-1), reads=['cf'], writes=['cf'])
        op('dve', lambda e: e.tensor_copy(ident[:], cf[:]), reads=['cf'], writes=['ident'])
        op('pool', lambda e: e.memset(onesf[:], 1.0), writes=['onesf'])
        op('pool', lambda e: e.affine_select(out=triU[:], in_=onesf[:], pattern=[[1, 128]], compare_op=ALU.is_ge,
                                            fill=0.0, base=0, channel_multiplier=-1), reads=['onesf'], writes=['triU'])
        op('dve', lambda e: e.memset(cf[:], -1.0), reads=['ident'], writes=['cf'])
        op('pool', lambda e: e.affine_select(out=cf[:], in_=cf[:], pattern=[[-1, 128]], compare_op=ALU.is_ge,
                                            fill=0.0, base=0, channel_multiplier=1), reads=['cf'], writes=['cf'])
        op('dve', lambda e: e.tensor_copy(triNeg[:], cf[:]), reads=['cf'], writes=['triNeg'])
        op('dve', lambda e: e.memset(ones_bf[:], 1.0), writes=['ones_bf'])
        op('dve', lambda e: e.memset(negones[:], -1.0), writes=['negones'])
        op('dve', lambda e: e.memset(sel[:], 0.0), writes=['sel'])
        for (k, rows, val) in ((0, (0, 32), -1.0), (1, (64, 96), -1.0), (2, (0, 32), 1.0), (3, (64, 96), 1.0)):
            for r in rows:
                op('dve', lambda e: e.memset(sel[r:r + 1, k, :], val), writes=['sel'])
        mst = ExitStack()
        mtmp = sb(mst, 'mtmp', [128, 512], F32)
        for k in range(4):
            op('pool', lambda e: e.memset(mtmp[:], 0.0), reads=['mtmp'], writes=['mtmp'])
            op('pool', lambda e: e.affine_select(out=mtmp[:], in_=mtmp[:], pattern=[[1, 512]], compare_op=ALU.is_gt,
                                                fill=NEG, base=-128 * k, channel_multiplier=-1),
               reads=['mtmp'], writes=['mtmp'])
            op('dve', lambda e: e.tensor_copy(maskneg[:, k, :], mtmp[:]), reads=['mtmp'], writes=['maskneg'])
        S_.barrier()
        mst.close()

        for l in range(L):
            src_d = x_d if l == 0 else xl_d
            dst_d = out_d if l == L - 1 else xl_d
            S_.new_phase()
            dma('sp', bifB[:], bif_d[l:l + 1, :].broadcast_to([128, 16]), writes=['bifB'])
            with nc.allow_non_contiguous_dma(reason='tiny param transposes'):
                dma('sp', bgT[:], bg_d[l, :].rearrange('(c p) -> p c', p=128), writes=['bgT'])
                dma('sp', gqk[:, 0:1], gq_d[l, :].rearrange('(p o) -> p o', o=1), writes=['gqk'])
                dma('sp', gqk[:, 1:2], gk_d[l, :].rearrange('(p o) -> p o', o=1), writes=['gqk'])
                dma('sp', gmT[:], gm_d[l, :].rearrange('(c p) -> p c', p=128), writes=['gmT'])
                for jj in range(4):
                    dma('sp', cwT[:, :, jj], cw_d[l, jj, :].rearrange('(c p) -> p c', p=128), writes=['cwT'])
            op('dve', lambda e: e.tensor_scalar(out=gqk[:, 0:1], in0=gqk[:, 0:1], scalar1=128.0 ** -0.5, scalar2=None, op0=ALU.mult),
               reads=['gqk'], writes=['gqk'])

            def run_skewed(make_gen, ntiles):
                active = []
                nxt = 0
                while nxt < ntiles or active:
                    for gen in list(active):
                        try:
                            next(gen)
                        except StopIteration:
                            active.remove(gen)
                    if nxt < ntiles:
                        gen = make_gen(nxt)
                        nxt += 1
                        try:
                            next(gen)
                            active.append(gen)
                        except StopIteration:
                            pass

            with ExitStack() as ph:
                gB = sb(ph, 'gB', [128, D], F32)
                dma('sp', gB[:], g1_d[l:l + 1, :].broadcast_to([128, D]), writes=['gB'])
                xt = [sb(ph, 'xt%d' % i, [128, D], F32) for i in range(4)]
                junk = sb(ph, 'junk', [128, D], BF16)
                hn = [sb(ph, 'hn%d' % i, [128, D], BF16) for i in range(2)]
                stt = [sb(ph, 'stt%d' % i, [128, 4], F32) for i in range(4)]
                tp = [ps(ph, 'tpA%d' % i, [128, 8, 128], BF16) for i in range(2)]

                def a_tile(i):
                    b = i % 2
                    b4 = i % 4
                    dma('sp', xt[b4][:], src_d[i * 128:(i + 1) * 128, :], writes=[('xt', b4)])
                    op('dve', lambda e: e.scalar_tensor_tensor(out=junk[:], in0=xt[b4][:], scalar=1.0, in1=xt[b4][:],
                                                               op0=ALU.mult, op1=ALU.mult, accum_out=stt[b4][:, 0:1]),
                       reads=[('xt', b4)], writes=['junk', ('stt', b4)])
                    yield
                    op('act', lambda e: e.activation(out=stt[b4][:, 1:2], in_=stt[b4][:, 0:1], func=AF.Sqrt,
                                                     scale=1.0 / D, bias=EPS), reads=[('stt', b4)], writes=[('stt', b4)])
                    yield
                    op('dve', lambda e: e.reciprocal(stt[b4][:, 2:3], stt[b4][:, 1:2]), reads=[('stt', b4)], writes=[('stt', b4)])
                    op('dve', lambda e: e.scalar_tensor_tensor(out=hn[b][:], in0=xt[b4][:], scalar=stt[b4][:, 2:3], in1=gB[:],
                                                               op0=ALU.mult, op1=ALU.mult),
                       reads=[('xt', b4), ('stt', b4), 'gB'], writes=[('hn', b)])
                    yield
                    for k in range(8):
                        op('pe', lambda e: e.transpose(tp[b][:, k, :], hn[b][:, k * 128:(k + 1) * 128], ident[:]),
                           reads=[('hn', b), 'ident'], writes=[('tp', b)])
                    op('act', lambda e: e.activation(out=hT[:, :, i * 128:(i + 1) * 128], in_=tp[b][:], func=AF.Copy),
                       reads=[('tp', b)], writes=[('hT', i // 4)])

                run_skewed(a_tile, NT)
            if stop_after == 'A':
                break

            S_.new_phase()
            winb_keys = [('winb', r) for r in range(8)]
            with ExitStack() as ph:
                wb = [sb(ph, 'wb%d' % i, [128, 8, 512], BF16) for i in range(2)]
                pB = [ps(ph, 'pB%d' % i, [128, 512], F32) for i in range(3)]
                tpq = ps(ph, 'tpq', [128, 4, 128], BF16)
                pF = [ps(ph, 'pF%d' % i, [128, 512], F32) for i in range(2)]
                pI = ps(ph, 'pI', [128, 512], F32)
                pI2 = ps(ph, 'pI2', [128, 512], F32)
                vst = [sb(ph, 'vst%d' % i, [128, 512], BF16) for i in range(4)]
                sq = sb(ph, 'sq', [128, 512], F32)
                st4 = sb(ph, 'st4', [128, 12], F32)
                qnb = [sb(ph, 'qnb%d' % i, [128, 4, 128], BF16) for i in range(3)]
                qTs = [sb(ph, 'qTs%d' % i, [128, 4, 128], BF16) for i in range(4)]
                wf = [sb(ph, 'wf%d' % i, [128, 8, 128], BF16) for i in range(6)]
                pre2 = [sb(ph, 'pre2_%d' % i, [128, 3 + 512], F32) for i in range(2)]
                acc2 = [sb(ph, 'acc2_%d' % i, [128, 512], F32) for i in range(2)]
                qlo_s = [sb(ph, 'qlo_s%d' % i, [128, 512], BF16) for i in range(2)]
                qhi_s = [sb(ph, 'qhi_s%d' % i, [128, 512], BF16) for i in range(2)]
                so = sb(ph, 'so', [128, 512], F32)
                sg = sb(ph, 'sg', [128, 512], F32)
                gas = [sb(ph, 'gas%d' % i, [128, 512], BF16) for i in range(4)]
                wif = sb(ph, 'wif', [128, 8, 16], BF16)
                pif = [sb(ph, 'pif%d' % i, [128, 16], F32) for i in range(2)]
                t8 = [sb(ph, 't8%d' % i, [128, 40], F32) for i in range(2)]
                cum = [sb(ph, 'cum%d' % i, [128, 16], F32) for i in range(2)]

                if l == 0:
                    with nc.allow_non_contiguous_dma(reason='small gate weight rows'):
                        dma('pool', wif[:], win_d[l][:, IA:IA + 16].rearrange('(kc p) n -> p kc n', p=128), writes=['wif'])
                else:
                    with nc.allow_non_contiguous_dma(reason='small gate weight rows'):
                        dma('sp', wif[:], winb_d[:, IA:IA + 16].rearrange('(kc p) n -> p kc n', p=128), reads=winb_keys, writes=['wif'])

                def gate_s1(i):
                    j = i % 2
                    for kc in range(8):
                        op('pe', lambda e: e.matmul(pI[:, 0:16], lhsT=hT[:, kc, i * 128:(i + 1) * 128], rhs=wif[:, kc, :],
                                                    start=(kc == 0), stop=(kc == 7)),
                           reads=[('hT', i // 4), 'wif'], writes=['pI'])
                    op('dve', lambda e: e.tensor_tensor(out=pif[j][:], in0=pI[:, 0:16], in1=bifB[:], op=ALU.add),
                       reads=['pI', 'bifB'], writes=[('pif', j)])
                    op('act', lambda e: e.activation(out=t8[j][:, 0:8], in_=pif[j][:, 8:16], func=AF.Exp, scale=-1.0),
                       reads=[('pif', j)], writes=[('t8a', j)])
                    op('act', lambda e: e.activation(out=t8[j][:, 8:16], in_=t8[j][:, 0:8], func=AF.Ln, bias=1.0),
                       reads=[('t8a', j)], writes=[('t8b', j)])

                def gate_s2(i):
                    j = i % 2
                    op('pe', lambda e: e.matmul(pI2[:, 0:8], lhsT=triU[:], rhs=t8[j][:, 8:16], start=True, stop=True),
                       reads=[('t8b', j), 'triU'], writes=['pI2'])
                    op('pe', lambda e: e.matmul(pI2[:, 8:16], lhsT=onesf[:], rhs=t8[j][:, 8:16], start=True, stop=True),
                       reads=[('t8b', j), 'onesf'], writes=['pI2'])
                    op('act', lambda e: e.activation(out=cum[j][:], in_=pI2[:, 0:16], func=AF.Copy), reads=['pI2'], writes=[('cum', j)])
                    op('dve', lambda e: e.tensor_tensor(out=t8[j][:, 16:24], in0=cum[j][:, 0:8], in1=cum[j][:, 8:16], op=ALU.subtract),
                       reads=[('cum', j)], writes=[('t8c', j)])
                    op('dve', lambda e: e.tensor_tensor(out=t8[j][:, 24:32], in0=t8[j][:, 16:24], in1=pif[j][:, 0:8], op=ALU.add),
                       reads=[('t8c', j), ('pif', j)], writes=[('t8d', j)])
                    op('act', lambda e: e.activation(out=gates[:, i, 0, :], in_=t8[j][:, 24:32], func=AF.Exp, bias=-math.log(8.0)),
                       reads=[('t8d', j)], writes=[('gates', i, 0)])
                    op('act', lambda e: e.activation(out=gates[:, i, 1, :], in_=t8[j][:, 16:24], func=AF.Exp, scale=-1.0),
                       reads=[('t8c', j)], writes=[('gates', i, 1)])
                    op('act', lambda e: e.activation(out=t8[j][:, 32:40], in_=cum[j][:, 8:16], func=AF.Exp, scale=-1.0),
                       reads=[('cum', j)], writes=[('t8e', j)])
                    op('dve', lambda e: e.tensor_copy(decB[0:64, i, :], t8[j][0:64, 32:40:2]), reads=[('t8e', j)], writes=[('decB', i, 0)])
                    op('dve', lambda e: e.tensor_copy(decB[64:128, i, :], t8[j][64:128, 33:40:2]), reads=[('t8e', j)], writes=[('decB', i, 1)])

                groups = []
                for kind, c0 in (('va', VA), ('qb', QB), ('kb', KB), ('vb', VB)):
                    groups.append((kind, c0, 0))
                    groups.append((kind, c0 + 512, 1))
                n = 0
                nqk = 0
                nfin = 0
                pend = []
                def load_wb(gi):
                    c0g = groups[gi][1]
                    wbb_ = gi % 2
                    if l == 0:
                        dma('pool', wb[wbb_][:], win_d[l][:, c0g:c0g + 512].rearrange('(kc p) n -> p kc n', p=128),
                            writes=[('wb', wbb_)])
                    else:
                        dma('sp', wb[wbb_][:], winb_d[:, c0g:c0g + 512].rearrange('(kc p) n -> p kc n', p=128),
                            reads=winb_keys, writes=[('wb', wbb_)])

                load_wb(0)
                for gi, (kind, c0, half) in enumerate(groups):
                    wbb = gi % 2
                    if gi + 1 < len(groups):
                        load_wb(gi + 1)
                    for i in range(NT):
                        pb = n % 3
                        j = n % 4
                        n += 1
                        for kc in range(8):
                            op('pe', lambda e: e.matmul(pB[pb][:], lhsT=hT[:, kc, i * 128:(i + 1) * 128], rhs=wb[wbb][:, kc, :],
                                                        start=(kc == 0), stop=(kc == 7)),
                               reads=[('hT', i // 4), ('wb', wbb)], writes=[('pB', pb)])
                        if kind in ('va', 'vb'):
                            dstv = va_d if kind == 'va' else vb_d
                            op('act', lambda e: e.activation(out=vst[j][:], in_=pB[pb][:], func=AF.Copy),
                               reads=[('pB', pb)], writes=[('vst', j)])
                            dma('sp', dstv[i * 128:(i + 1) * 128, half * 512:(half + 1) * 512], vst[j][:], reads=[('vst', j)])
                        else:
                            gcol = 0 if kind == 'qb' else 1
                            dstT = qbT_d if kind == 'qb' else kbT_d
                            j3 = nqk % 3
                            nqk += 1
                            op('act', lambda e: e.activation(out=sq[:], in_=pB[pb][:], func=AF.Square),
                               reads=[('pB', pb)], writes=['sq'])
                            op('dve', lambda e: e.tensor_reduce(out=st4[:, 0:4], in_=sq[:].rearrange('p (h d) -> p h d', d=128),
                                                                axis=AX.X, op=ALU.add), reads=['sq'], writes=['st4'])
                            op('act', lambda e: e.activation(out=st4[:, 4:8], in_=st4[:, 0:4], func=AF.Sqrt, scale=1.0 / 128, bias=EPS),
                               reads=['st4'], writes=['st4'])
                            op('dve', lambda e: e.reciprocal(st4[:, 8:12], st4[:, 4:8]), reads=['st4'], writes=['st4'])
                            op('dve', lambda e: e.tensor_tensor(out=qnb[j3][:], in0=pB[pb][:].rearrange('p (h d) -> p h d', d=128),
                                                                in1=st4[:, 8:12].unsqueeze(2).broadcast_to([128, 4, 128]), op=ALU.mult),
                               reads=[('pB', pb), 'st4'], writes=[('qnb', j3)])

                            def fin(j3=j3, gcol=gcol, dstT=dstT, half=half, i=i):
                                nonlocal nfin
                                jj = nfin % 4
                                nfin += 1
                                for hh in range(4):
                                    op('pe', lambda e: e.transpose(tpq[:, hh, :], qnb[j3][:, hh, :], ident[:]),
                                       reads=[('qnb', j3), 'ident'], writes=['tpq'])
                                op('act', lambda e: e.activation(out=qTs[jj][:], in_=tpq[:], func=AF.Copy, scale=gqk[:, gcol:gcol + 1]),
                                   reads=['tpq', 'gqk'], writes=[('qTs', jj)])
                                dma('sp', dstT[half * 4:(half + 1) * 4, :, i * 128:(i + 1) * 128].rearrange('h d t -> d h t'),
                                    qTs[jj][:], reads=[('qTs', jj)])
                            pend.append(fin)
                            if len(pend) > 2:
                                pend.pop(0)()
                        if gi == GATE_GROUP:
                            gate_s1(i)
                            if i > 0:
                                gate_s2(i - 1)
                    if gi == GATE_GROUP:
                        gate_s2(NT - 1)
                while pend:
                    pend.pop(0)()
                if GATE_GROUP < 0:
                    for i in range(NT + 1):
                        if i < NT:
                            gate_s1(i)
                        if i > 0:
                            gate_s2(i - 1)

                nf = 0
                nw = 0

                wf_cols = [cc * 128 for cc in range(8)]
                for cc in range(8):
                    wf_cols += [OA + cc * 128, GA + cc * 128, GB + cc * 128]
                wf_issued = [0]

                def issue_wf():
                    idx = wf_issued[0]
                    if idx >= len(wf_cols):
                        return
                    wf_issued[0] += 1
                    c0w = wf_cols[idx]
                    kk = idx % 6
                    with nc.allow_non_contiguous_dma(reason='512B weight rows'):
                        if l == 0:
                            dma('pool', wf[kk][:], win_d[l][:, c0w:c0w + 128].rearrange('(kc p) n -> p kc n', p=128), writes=[('wf', kk)])
                        else:
                            dma('sp', wf[kk][:], winb_d[:, c0w:c0w + 128].rearrange('(kc p) n -> p kc n', p=128),
                                reads=winb_keys, writes=[('wf', kk)])

                def load_wf(c0):
                    nonlocal nw
                    assert wf_cols[nw] == c0
                    k = nw % 6
                    nw += 1
                    while wf_issued[0] < min(nw + 2, len(wf_cols)):
                        issue_wf()
                    return k

                issue_wf()
                issue_wf()

                def fm_matmul(k, tb):
                    nonlocal nf
                    pb = nf % 2
                    nf += 1
                    for kc in range(8):
                        op('pe', lambda e: e.matmul(pF[pb][:], lhsT=wf[k][:, kc, :], rhs=hT[:, kc, tb * 512:(tb + 1) * 512],
                                                    start=(kc == 0), stop=(kc == 7)),
                           reads=[('hT', tb), ('wf', k)], writes=[('pF', pb)])
                    return pb

                for i in range(2):
                    op('pool', lambda e: e.memset(qlo_s[i][:], 0.0), writes=[('qlo_s', i)])
                    op('pool', lambda e: e.memset(qhi_s[i][:], 0.0), writes=[('qhi_s', i)])
                nq = 0
                nblk = 0
                pendc = []
                for cc in range(8):
                    k = load_wf(cc * 128)
                    for tb in range(NB):
                        pb = fm_matmul(k, tb)
                        sl = nblk % 2
                        nblk += 1
                        P = pre2[sl]
                        Pp = pre2[1 - sl]
                        A = acc2[sl]
                        if tb == 0:
                            op('pool', lambda e: e.memset(P[:, 0:3], 0.0), writes=[('pre', sl)])
                        else:
                            op('pool', lambda e: e.tensor_copy(P[:, 0:3], Pp[:, 512:515]), reads=[('pre', 1 - sl)], writes=[('pre', sl)])
                        op('act', lambda e: e.activation(out=P[:, 3:515], in_=pF[pb][:], func=AF.Copy),
                           reads=[('pF', pb)], writes=[('pre', sl)])
                        op('act', lambda e: e.activation(out=A[:], in_=pF[pb][:], func=AF.Copy, scale=cwT[:, cc, 3:4]),
                           reads=[('pF', pb), 'cwT'], writes=[('acc', sl)])
                        for jj in range(0, 3):
                            op('dve', lambda e: e.scalar_tensor_tensor(out=A[:], in0=P[:, jj:jj + 512],
                                                                       scalar=cwT[:, cc, jj:jj + 1], in1=A[:],
                                                                       op0=ALU.mult, op1=ALU.add),
                               reads=[('pre', sl), 'cwT', ('acc', sl)], writes=[('acc', sl)])
                        def fin_conv(cc=cc, tb=tb, A=A, sl=sl):
                            nonlocal nq
                            j = nq % 2
                            nq += 1
                            if cc < 4:
                                op('act', lambda e: e.activation(out=qlo_s[j][0:64, :], in_=A[0:64, :], func=AF.Silu),
                                   reads=[('acc', sl)], writes=[('qlo_s', j)])
                                op('act', lambda e: e.activation(out=qhi_s[j][64:128, :], in_=A[64:128, :], func=AF.Silu),
                                   reads=[('acc', sl)], writes=[('qhi_s', j)])
                                dma('sp', qlo_d[cc, :, tb * 512:(tb + 1) * 512], qlo_s[j][:], reads=[('qlo_s', j)])
                                dma('sp', qhi_d[cc, :, tb * 512:(tb + 1) * 512], qhi_s[j][:], reads=[('qhi_s', j)])
                            else:
                                op('act', lambda e: e.activation(out=gas[j][:], in_=A[:], func=AF.Silu),
                                   reads=[('acc', sl)], writes=[('gas', j)])
                                dma('sp', ka_d[cc - 4, :, tb * 512:(tb + 1) * 512], gas[j][:], reads=[('gas', j)])
                        pendc.append(fin_conv)
                        if len(pendc) > 1:
                            pendc.pop(0)()
                while pendc:
                    pendc.pop(0)()
                for cc in range(8):
                    ko = load_wf(OA + cc * 128)
                    kg = load_wf(GA + cc * 128)
                    kb = load_wf(GB + cc * 128)
                    for tb in range(NB):
                        pb = fm_matmul(ko, tb)
                        op('act', lambda e: e.activation(out=so[:], in_=pF[pb][:], func=AF.Sigmoid),
                           reads=[('pF', pb)], writes=['so'])
                        pb = fm_matmul(kg, tb)
                        op('act', lambda e: e.activation(out=sg[:], in_=pF[pb][:], func=AF.Sigmoid, bias=bgT[:, cc:cc + 1]),
                           reads=[('pF', pb), 'bgT'], writes=['sg'])
                        j = nq % 4
                        nq += 1
                        op('dve', lambda e: e.scalar_tensor_tensor(out=gas[j][:], in0=sg[:], scalar=gmT[:, cc:cc + 1], in1=so[:],
                                                                   op0=ALU.mult, op1=ALU.mult),
                           reads=['sg', 'so', 'gmT'], writes=[('gas', j)])
                        dma('sp', gaT_d[cc, :, tb * 512:(tb + 1) * 512], gas[j][:], reads=[('gas', j)])
                        pb = fm_matmul(kb, tb)
                        j = nq % 4
                        nq += 1
                        op('act', lambda e: e.activation(out=gas[j][:], in_=pF[pb][:], func=AF.Sigmoid, bias=bgT[:, 8 + cc:9 + cc]),
                           reads=[('pF', pb), 'bgT'], writes=[('gas', j)])
                        dma('sp', gbT_d[cc, :, tb * 512:(tb + 1) * 512], gas[j][:], reads=[('gas', j)])
            if stop_after == 'B':
                break

            S_.new_phase()
            for r in range(8):
                S_.dma_bg('pool', woutb_d[r * 128:(r + 1) * 128, :], wout_d[l][r * 128:(r + 1) * 128, :], ('woutb', r))
            for r in range(32):
                S_.dma_bg('pool', wdnb_d[r * 128:(r + 1) * 128, :], wdn_d[l][r * 128:(r + 1) * 128, :], ('wdnb', r))
            for r in range(8):
                S_.dma_bg('pool', wupb_d[r * 128:(r + 1) * 128, :], wup_d[l][r * 128:(r + 1) * 128, :], ('wupb', r))
            if l + 1 < L:
                for r in range(8):
                    S_.dma_bg('pool', winb_d[r * 128:(r + 1) * 128, :], win_d[l + 1][r * 128:(r + 1) * 128, :], ('winb', r))
            with ExitStack() as ph:
                qlo_c = [sb(ph, 'qlo_c%d' % i, [128, 4, 128], BF16) for i in range(4)]
                qhi_c = [sb(ph, 'qhi_c%d' % i, [128, 4, 128], BF16) for i in range(4)]
                k_c = [sb(ph, 'k_c%d' % i, [128, 4, 128], BF16) for i in range(4)]
                v_c = [sb(ph, 'v_c%d' % i, [128, D], BF16) for i in range(4)]
                ga_c = [sb(ph, 'ga_c%d' % i, [128, 8, 128], BF16) for i in range(4)]
                PT = [sb(ph, 'PT%d' % i, [128, 8, 128], BF16) for i in range(2)]
                kw = [sb(ph, 'kw%d' % i, [128, 8, 64], BF16) for i in range(2)]
                C_all = sb(ph, 'C_all', [128, 4, 128], F32)
                n_all = sb(ph, 'n_all', [128, 4], F32)
                Cd = sb(ph, 'Cd', [128, 4, 128], F32)
                nd = sb(ph, 'nd', [128, 4], F32)
                Cd_bf = sb(ph, 'Cd_bf', [128, 4, 128], BF16)
                nd_bf = sb(ph, 'nd_bf', [128, 4], BF16)
                sm = sb(ph, 'sm', [128, 96], F32)
                den_sb = [sb(ph, 'den_sb%d' % i, [128, 8], F32) for i in range(2)]
                sqb = sb(ph, 'sqb', [128, D], F32)
                ya = sb(ph, 'ya', [128, D], BF16)
                yag = [sb(ph, 'yag%d' % i, [128, 8, 128], BF16) for i in range(2)]
                scp = ps(ph, 'scp', [128, 4, 128], F32)
                nump = [[ps(ph, 'nump%d_%d' % (i, j), [128, 4, 128], F32) for j in range(2)] for i in range(2)]
                dCp = ps(ph, 'dCp', [128, 2, 256], F32)
                misc = ps(ph, 'misc', [128, 512], F32)
                tp = ps(ph, 'tpC', [128, 8, 128], BF16)
                op('dve', lambda e: e.memset(C_all[:], 0.0), writes=['C_all'])
                op('dve', lambda e: e.memset(n_all[:], 0.0), writes=['n_all'])

                def qm(c, h):
                    b3 = c % 4
                    return (qlo_c[b3] if h % 2 == 0 else qhi_c[b3])[:, h // 2, :]

                def qmk(c, h):
                    return ('qlo_c', c % 4) if h % 2 == 0 else ('qhi_c', c % 4)

                def loads(c):
                    b3 = c % 4
                    tsl = slice(c * 128, (c + 1) * 128)
                    with nc.allow_non_contiguous_dma(reason='256B rows'):
                        dma('sp', qlo_c[b3][:], qlo_d[:, :, tsl].rearrange('c p t -> p c t'), writes=[('qlo_c', b3)])
                        dma('sp', qhi_c[b3][:], qhi_d[:, :, tsl].rearrange('c p t -> p c t'), writes=[('qhi_c', b3)])
                        dma('sp', k_c[b3][:], ka_d[:, :, tsl].rearrange('c p t -> p c t'), writes=[('k_c', b3)])
                        dma('sp', ga_c[b3][:], gaT_d[:, :, tsl].rearrange('c p t -> p c t'), writes=[('ga_c', b3)])
                    dma('sp', v_c[b3][:], va_d[tsl, :], writes=[('v_c', b3)])

                def front(c):
                    b3 = c % 4
                    p2 = c % 2
                    for hg in range(2):
                        for h in range(4 * hg, 4 * hg + 4):
                            op('pe', lambda e: e.matmul(scp[:, h % 4, :], lhsT=k_c[b3][:, h // 2, :], rhs=qm(c, h), start=True, stop=True),
                               reads=[('k_c', b3), qmk(c, h)], writes=['scp'])
                        for h in range(4 * hg, 4 * hg + 4):
                            op('dve', lambda e: e.scalar_tensor_tensor(out=PT[p2][:, h, :], in0=scp[:, h % 4, :],
                                                                       scalar=gates[:, c, 0, h:h + 1], in1=triU[:],
                                                                       op0=ALU.mult, op1=ALU.mult),
                               reads=['scp', ('gates', c, 0), 'triU'], writes=[('PT', p2, h)])
                        yield
                    for cc in range(4):
                        op('pe', lambda e: e.transpose(tp[:, cc, :], k_c[b3][:, cc, :], ident[:]),
                           reads=[('k_c', b3), 'ident'], writes=['tp'])
                    op('dve', lambda e: e.tensor_tensor(out=kw[p2][:], in0=tp[:, 0:4, :].rearrange('p c (j d) -> p (c j) d', j=2),
                                                        in1=gates[:, c, 0, :].unsqueeze(2).broadcast_to([128, 8, 64]), op=ALU.mult),
                       reads=['tp', ('gates', c, 0)], writes=[('kw', p2)])

                def mid(c):
                    b3 = c % 4
                    p2 = c % 2
                    op('dve', lambda e: e.tensor_tensor(out=Cd[:], in0=C_all[:], in1=decB[:, c, :].unsqueeze(2).broadcast_to([128, 4, 128]),
                                                        op=ALU.mult), reads=['C_all', ('decB', c, 0), ('decB', c, 1)], writes=['Cd'])
                    op('dve', lambda e: e.tensor_tensor(out=nd[:], in0=n_all[:], in1=decB[:, c, :], op=ALU.mult),
                       reads=['n_all', ('decB', c, 0), ('decB', c, 1)], writes=['nd'])
                    op('act', lambda e: e.activation(out=Cd_bf[:], in_=Cd[:], func=AF.Copy), reads=['Cd'], writes=['Cd_bf'])
                    op('act', lambda e: e.activation(out=nd_bf[:], in_=nd[:], func=AF.Copy), reads=['nd'], writes=['nd_bf'])
                    yield
                    for h in range(NH):
                        if h == 4:
                            yield
                        npk = ('nump', p2, h // 4)
                        nt_ = nump[p2][h // 4]
                        op('pe', lambda e: e.matmul(nt_[:, h % 4, :], lhsT=PT[p2][:, h, :], rhs=v_c[b3][:, h * 128:(h + 1) * 128],
                                                    start=True, stop=False),
                           reads=[('PT', p2, h), ('v_c', b3)], writes=[npk])
                        op('pe', lambda e: e.matmul(nt_[:, h % 4, :], lhsT=qm(c, h), rhs=Cd_bf[:, h // 2, :],
                                                    start=False, stop=True),
                           reads=[qmk(c, h), 'Cd_bf'], writes=[npk])
                        dcol = 16 * p2 + h
                        op('pe', lambda e: e.matmul(misc[:, dcol:dcol + 1], lhsT=PT[p2][:, h, :], rhs=ones_bf[:, 0:1], start=True, stop=False),
                           reads=[('PT', p2, h), 'ones_bf'], writes=['misc'])
                        op('pe', lambda e: e.matmul(misc[:, dcol:dcol + 1], lhsT=qm(c, h), rhs=nd_bf[:, h // 2:h // 2 + 1], start=False, stop=True),
                           reads=[qmk(c, h), 'nd_bf'], writes=['misc'])
                    yield
                    op('dve', lambda e: e.tensor_copy(den_sb[p2][:], misc[:, 16 * p2:16 * p2 + 8]), reads=['misc'], writes=[('den_sb', p2)])
                    for g2 in range(2):
                        if g2 == 1:
                            yield
                        for cc in (2 * g2, 2 * g2 + 1):
                            op('pe', lambda e: e.matmul(dCp[:, cc % 2, :], lhsT=kw[p2][:, 2 * cc:2 * cc + 2, :].rearrange('p j d -> p (j d)'),
                                                        rhs=v_c[b3][:, cc * 256:(cc + 1) * 256], start=True, stop=True),
                               reads=[('kw', p2), ('v_c', b3)], writes=['dCp'])
                        op('dve', lambda e: e.tensor_tensor(out=C_all[0:64, 2 * g2:2 * g2 + 2, :], in0=Cd[0:64, 2 * g2:2 * g2 + 2, :],
                                                            in1=dCp[0:64, :, 0:128], op=ALU.add),
                           reads=['Cd', 'dCp'], writes=['C_all'])
                        op('dve', lambda e: e.tensor_tensor(out=C_all[64:128, 2 * g2:2 * g2 + 2, :], in0=Cd[64:128, 2 * g2:2 * g2 + 2, :],
                                                            in1=dCp[64:128, :, 128:256], op=ALU.add),
                           reads=['Cd', 'dCp'], writes=['C_all'])
                    for cc in range(4):
                        op('pe', lambda e: e.matmul(misc[:, 32 + cc:33 + cc], lhsT=kw[p2][:, 2 * cc:2 * cc + 2, :].rearrange('p j d -> p (j d)'),
                                                    rhs=ones_bf[:, 0:1], start=True, stop=True),
                           reads=[('kw', p2), 'ones_bf'], writes=['misc'])
                    op('dve', lambda e: e.tensor_tensor(out=n_all[:], in0=nd[:], in1=misc[:, 32:36], op=ALU.add),
                       reads=['nd', 'misc'], writes=['n_all'])

                def back(c):
                    b3 = c % 4
                    p2 = c % 2
                    tsl = slice(c * 128, (c + 1) * 128)
                    ebp = gates[:, c, 1, :]
                    op('dve', lambda e: e.tensor_tensor(out=sm[:, 0:8], in0=den_sb[p2][:], in1=ebp, op=ALU.mult),
                       reads=[('den_sb', p2), ('gates', c, 1)], writes=['sm0'])
                    op('dve', lambda e: e.scalar_tensor_tensor(out=sm[:, 8:16], in0=sm[:, 0:8], scalar=-1.0, in1=sm[:, 0:8],
                                                               op0=ALU.mult, op1=ALU.max), reads=['sm0'], writes=['sm1'])
                    op('dve', lambda e: e.tensor_scalar(out=sm[:, 16:24], in0=sm[:, 8:16], scalar1=1.0, scalar2=None, op0=ALU.max),
                       reads=['sm1'], writes=['sm2'])
                    op('dve', lambda e: e.reciprocal(sm[:, 24:32], sm[:, 16:24]), reads=['sm2'], writes=['sm3'])
                    op('dve', lambda e: e.tensor_tensor(out=sm[:, 32:40], in0=ebp, in1=sm[:, 24:32], op=ALU.mult),
                       reads=['sm3', ('gates', c, 1)], writes=['sm4'])
                    for g2 in range(2):
                        op('act', lambda e: e.activation(out=sqb[:, g2 * 512:(g2 + 1) * 512],
                                                         in_=nump[p2][g2][:].rearrange('p h d -> p (h d)'), func=AF.Square),
                           reads=[('nump', p2, g2)], writes=[('sqb', g2)])
                    yield
                    op('dve', lambda e: e.tensor_reduce(out=sm[:, 40:48], in_=sqb[:].rearrange('p (h d) -> p h d', d=128),
                                                        axis=AX.X, op=ALU.add), reads=[('sqb', 0), ('sqb', 1)], writes=['sm5'])
                    op('dve', lambda e: e.tensor_tensor(out=sm[:, 48:56], in0=sm[:, 32:40], in1=sm[:, 32:40], op=ALU.mult),
                       reads=['sm4'], writes=['sm6'])
                    op('dve', lambda e: e.tensor_tensor(out=sm[:, 56:64], in0=sm[:, 48:56], in1=sm[:, 40:48], op=ALU.mult),
                       reads=['sm6', 'sm5'], writes=['sm7'])
                    op('act', lambda e: e.activation(out=sm[:, 64:72], in_=sm[:, 56:64], func=AF.Sqrt, scale=1.0 / 128, bias=EPS),
                       reads=['sm7'], writes=['sm8'])
                    op('dve', lambda e: e.reciprocal(sm[:, 72:80], sm[:, 64:72]), reads=['sm8'], writes=['sm9'])
                    op('dve', lambda e: e.tensor_tensor(out=sm[:, 80:88], in0=sm[:, 32:40], in1=sm[:, 72:80], op=ALU.mult),
                       reads=['sm9', 'sm4'], writes=['sm10'])
                    yield
                    for h in range(NH):
                        op('act', lambda e: e.activation(out=ya[:, h * 128:(h + 1) * 128], in_=nump[p2][h // 4][:, h % 4, :],
                                                         func=AF.Copy, scale=sm[:, 80 + h:81 + h]),
                           reads=[('nump', p2, h // 4), 'sm10'], writes=[('ya', h)])
                    yield
                    for h in range(NH):
                        op('pe', lambda e: e.transpose(tp[:, h, :], ya[:, h * 128:(h + 1) * 128], ident[:]),
                           reads=[('ya', h), 'ident'], writes=['tp'])
                    op('dve', lambda e: e.tensor_tensor(out=yag[p2][:], in0=tp[:], in1=ga_c[b3][:], op=ALU.mult),
                       reads=['tp', ('ga_c', b3)], writes=[('yag', p2)])
                    with nc.allow_non_contiguous_dma(reason='256B rows'):
                        dma('sp', yaT_d[:, :, tsl].rearrange('c p t -> p c t'), yag[p2][:], reads=[('yag', p2)])

                loads(0)
                for it in range(NT + 2):
                    if it + 1 < NT:
                        loads(it + 1)
                    gens = []
                    if 0 <= it - 2 < NT:
                        gens.append(back(it - 2))
                    if 0 <= it - 1 < NT:
                        gens.append(mid(it - 1))
                    if it < NT:
                        gens.append(front(it))
                    while gens:
                        for gen in list(gens):
                            try:
                                next(gen)
                            except StopIteration:
                                gens.remove(gen)
            if stop_after == 'C':
                break

            S_.new_phase()
            with ExitStack() as ph:
                knT = [sb(ph, 'knT%d' % i, [128, S], BF16) for i in range(2)]
                qnT = [sb(ph, 'qnT%d' % i, [128, S], BF16) for i in range(2)]
                Vh = [sb(ph, 'Vh%d' % i, [128, NT, 128], BF16) for i in range(2)]
                Eb = [sb(ph, 'Eb', [128, 2, 512], F32)] * 2
                Lp = [sb(ph, 'Lp%d' % i, [128, 2, 512], BF16) for i in range(2)]
                AT = [sb(ph, 'AT%d' % i, [128, 2, 512], BF16) for i in range(2)]
                gb_b = [sb(ph, 'gb_b%d' % i, [128, 512], BF16) for i in range(2)]
                ya_b = [sb(ph, 'ya_b%d' % i, [128, 512], BF16) for i in range(2)]
                c2l = [sb(ph, 'c2_%d' % i, [64, 512], BF16) for i in range(2)]
                ytmp = sb(ph, 'ytmp', [128, 512], F32)
                zA = [ps(ph, 'zA%d' % i, [128, 2, 512], F32) for i in range(3)]
                yacc = ps(ph, 'yacc', [128, 512], F32)
                csp = ps(ph, 'csp', [128, 512], F32)

                def load_head(h):
                    hp = h % 2
                    dma('sp', knT[hp][:], kbT_d[h, :, :], writes=[('knT', hp)])
                    dma('sp', qnT[hp][:], qbT_d[h, :, :], writes=[('qnT', hp)])
                    with nc.allow_non_contiguous_dma(reason='256B rows'):
                        dma('sp', Vh[hp][:], vb_d[:, h * 128:(h + 1) * 128].rearrange('(n p) d -> p n d', p=128),
                            writes=[('Vh', hp)])

                blocks = [(h, qb) for h in range(NH) for qb in range(NB)]

                def load_block(bi):
                    h, qb = blocks[bi]
                    yb = bi % 2
                    qsl = slice(qb * 512, (qb + 1) * 512)
                    dma('sp', gb_b[yb][:], gbT_d[h, :, qsl], writes=[('gb_b', yb)])
                    dma('sp', ya_b[yb][:], yaT_d[h, :, qsl], writes=[('ya_b', yb)])

                tiles = []
                for bi, (h, qb) in enumerate(blocks):
                    ktmax = 4 * qb + 3
                    for n, kt in enumerate(range(ktmax, -1, -2)):
                        tiles.append((bi, h, qb, n, kt))
                G = len(tiles)

                def emit_zA(g):
                    bi, h, qb, n, kt = tiles[g]
                    hp = h % 2
                    k3 = g % 3
                    for u in range(2):
                        ktu = kt - u
                        diag = ktu >= 4 * qb
                        op('pe', lambda e: e.matmul(zA[k3][:, u, :], lhsT=knT[hp][:, ktu * 128:(ktu + 1) * 128],
                                                    rhs=qnT[hp][:, qb * 512:(qb + 1) * 512], start=True, stop=(not diag)),
                           reads=[('knT', hp), ('qnT', hp)], writes=[('zA', k3, u)])
                        if diag:
                            op('pe', lambda e: e.matmul(zA[k3][:, u, :], lhsT=ident[:], rhs=maskneg[:, ktu - 4 * qb, :],
                                                        start=False, stop=True),
                               reads=['ident', 'maskneg'], writes=[('zA', k3, u)])

                def emit_tail(g):
                    bi, h, qb, n, kt = tiles[g]
                    hp = h % 2
                    yb = bi % 2
                    k3 = g % 3
                    op('act', lambda e: e.activation(out=AT[g % 2][:], in_=zA[k3][:], func=AF.Exp),
                       reads=[('zA', k3, 0), ('zA', k3, 1)], writes=[('AT', g % 2)])
                    for u in range(2):
                        op('pe', lambda e: e.matmul(yacc[:], lhsT=Vh[hp][:, kt - u, :], rhs=AT[g % 2][:, u, :],
                                                    start=(n == 0 and u == 0), stop=(kt - u == 0)),
                           reads=[('Vh', hp), ('AT', g % 2)], writes=['yacc'])
                    if kt - 1 == 0:
                        qsl = slice(qb * 512, (qb + 1) * 512)
                        op('dve', lambda e: e.tensor_tensor(out=ytmp[:], in0=yacc[:], in1=gb_b[yb][:], op=ALU.mult),
                           reads=['yacc', ('gb_b', yb)], writes=['ytmp'])
                        op('dve', lambda e: e.tensor_tensor(out=yT[:, h, qsl], in0=ytmp[:], in1=ya_b[yb][:], op=ALU.add),
                           reads=['ytmp', ('ya_b', yb)], writes=[('yT', qb)])

                for i in range(2):
                    op('dve', lambda e: e.memset(c2l[i][:], 0.0), writes=[('c2', i)])
                load_head(0)
                if NH > 1:
                    load_head(1)
                load_block(0)
                emit_zA(0)
                for g in range(G):
                    bi, h, qb, n, kt = tiles[g]
                    k3 = g % 3
                    last = (kt - 1 == 0)
                    if g + 1 < G:
                        emit_zA(g + 1)
                    op('act', lambda e: e.activation(out=Eb[g % 2][:], in_=zA[k3][:], func=AF.Exp),
                       reads=[('zA', k3, 0), ('zA', k3, 1)], writes=['Eb'])
                    op('act', lambda e: e.activation(out=Lp[g % 2][:], in_=Eb[g % 2][:], func=AF.Ln, bias=1.0),
                       reads=['Eb'], writes=[('Lp', g % 2)])
                    Lg = Lp[g % 2]
                    c2p, c2pk = c2l[(g - 1) % 2], ('c2', (g - 1) % 2)
                    c2n, c2nk = c2l[g % 2], ('c2', g % 2)
                    lk = ('Lp', g % 2)
                    if not last:
                        op('pe', lambda e: e.matmul(csp[0:64, :], lhsT=ones_bf[:, 0:64], rhs=Lg[:, 0, :], start=True, stop=False),
                           reads=['ones_bf', lk], writes=['csp'])
                        op('pe', lambda e: e.matmul(csp[0:64, :], lhsT=ones_bf[:, 0:64], rhs=Lg[:, 1, :], start=False, stop=(n == 0)),
                           reads=['ones_bf', lk], writes=['csp'])
                        if n > 0:
                            op('pe', lambda e: e.matmul(csp[0:64, :], lhsT=sel[0:64, 2, 0:64], rhs=c2p[:], start=False, stop=True),
                               reads=['sel', c2pk], writes=['csp'])
                    op('pe', lambda e: e.matmul(zA[k3][:, 0, :], lhsT=triNeg[:], rhs=Lg[:, 0, :], start=False, stop=(n == 0),
                                                skip_group_check=True),
                       reads=['triNeg', lk], writes=[('zA', k3, 0)])
                    op('pe', lambda e: e.matmul(zA[k3][:, 1, :], lhsT=triNeg[:], rhs=Lg[:, 1, :], start=False, stop=False,
                                                skip_group_check=True),
                       reads=['triNeg', lk], writes=[('zA', k3, 1)])
                    op('pe', lambda e: e.matmul(zA[k3][:, 1, :], lhsT=negones[:], rhs=Lg[:, 0, :], start=False, stop=(n == 0),
                                                skip_group_check=True),
                       reads=['negones', lk], writes=[('zA', k3, 1)])
                    if n > 0:
                        for u in range(2):
                            op('pe', lambda e: e.matmul(zA[k3][:, u, :], lhsT=sel[0:64, 0, :], rhs=c2p[:], start=False, stop=True,
                                                        skip_group_check=True),
                               reads=['sel', c2pk], writes=[('zA', k3, u)])
                    if not last:
                        op('dve', lambda e: e.tensor_copy(c2n[:], csp[0:64, :]), reads=['csp'], writes=[c2nk])
                        op('dve', lambda e: e.tensor_tensor(out=c2n[32:64, :], in0=csp[32:64, :], in1=c2n[32:64, :], op=ALU.subtract),
                           reads=['csp', c2nk], writes=[c2nk])
                    if g > 0:
                        emit_tail(g - 1)
                    if n == 0:
                        if bi + 1 < len(blocks):
                            load_block(bi + 1)
                        if qb == 0 and h >= 1 and h + 1 < NH:
                            load_head(h + 1)
                emit_tail(G - 1)
            if stop_after == 'D':
                break

            S_.new_phase()
            with ExitStack() as ph:
                gB = sb(ph, 'gB', [128, D], F32)
                dma('sp', gB[:], g2_d[l:l + 1, :].broadcast_to([128, D]), writes=['gB'])
                wo = sb(ph, 'wo', [128, 8, D], BF16)
                xt = [sb(ph, 'xt%d' % i, [128, D], F32) for i in range(2)]
                x1 = [sb(ph, 'x1%d' % i, [128, D], F32) for i in range(4)]
                junk = sb(ph, 'junk', [128, D], BF16)
                hn = [sb(ph, 'hn%d' % i, [128, D], BF16) for i in range(2)]
                stt = [sb(ph, 'stt%d' % i, [128, 4], F32) for i in range(4)]
                pO = [ps(ph, 'pO%d' % i, [128, 512], F32) for i in range(4)]
                tp = [ps(ph, 'tpE%d' % i, [128, 8, 128], BF16) for i in range(2)]
                dma('sp', wo[:], woutb_d.rearrange('(c p) n -> p c n', p=128), reads=[('woutb', r) for r in range(8)], writes=['wo'])

                def e_tile(i):
                    b = i % 2
                    b4 = i % 4
                    tsl = slice(i * 128, (i + 1) * 128)
                    dma('sp', xt[b][:], src_d[tsl, :], writes=[('xt', b)])
                    for half in range(2):
                        pk = 2 * b + half
                        for cc in range(8):
                            op('pe', lambda e: e.matmul(pO[pk][:], lhsT=yT[:, cc, tsl], rhs=wo[:, cc, half * 512:(half + 1) * 512],
                                                        start=(cc == 0), stop=(cc == 7)),
                               reads=[('yT', i // 4), 'wo'], writes=[('pO', pk)])
                        op('dve', lambda e: e.tensor_tensor(out=x1[b4][:, half * 512:(half + 1) * 512], in0=xt[b][:, half * 512:(half + 1) * 512],
                                                            in1=pO[pk][:], op=ALU.add),
                           reads=[('xt', b), ('pO', pk)], writes=[('x1', b4)])
                    dma('sp', xmid_d[tsl, :], x1[b4][:], reads=[('x1', b4)])
                    op('dve', lambda e: e.scalar_tensor_tensor(out=junk[:], in0=x1[b4][:], scalar=1.0, in1=x1[b4][:],
                                                               op0=ALU.mult, op1=ALU.mult, accum_out=stt[b4][:, 0:1]),
                       reads=[('x1', b4)], writes=['junk', ('stt', b4)])
                    yield
                    op('act', lambda e: e.activation(out=stt[b4][:, 1:2], in_=stt[b4][:, 0:1], func=AF.Sqrt,
                                                     scale=1.0 / D, bias=EPS), reads=[('stt', b4)], writes=[('stt', b4)])
                    yield
                    op('dve', lambda e: e.reciprocal(stt[b4][:, 2:3], stt[b4][:, 1:2]), reads=[('stt', b4)], writes=[('stt', b4)])
                    op('dve', lambda e: e.scalar_tensor_tensor(out=hn[b][:], in0=x1[b4][:], scalar=stt[b4][:, 2:3], in1=gB[:],
                                                               op0=ALU.mult, op1=ALU.mult),
                       reads=[('x1', b4), ('stt', b4), 'gB'], writes=[('hn', b)])
                    yield
                    for k in range(8):
                        op('pe', lambda e: e.transpose(tp[b][:, k, :], hn[b][:, k * 128:(k + 1) * 128], ident[:]),
                           reads=[('hn', b), 'ident'], writes=[('tp', b)])
                    op('act', lambda e: e.activation(out=hT[:, :, tsl], in_=tp[b][:], func=AF.Copy),
                       reads=[('tp', b)], writes=[('hT', i // 4)])

                run_skewed(e_tile, NT)
            if stop_after == 'E1':
                break

            S_.new_phase()
            with ExitStack() as ph:
                wu = [sb(ph, 'wu%d' % i, [128, 8, 512], BF16) for i in range(2)]
                rr = [sb(ph, 'rr%d' % i, [128, 512], F32) for i in range(2)]
                aT = sb(ph, 'aT', [128, 32, 512], BF16)
                x1t = [sb(ph, 'x1t%d' % i, [128, D], F32) for i in range(2)]
                pU = [ps(ph, 'pU%d' % i, [128, 512], F32) for i in range(2)]
                pD = [ps(ph, 'pD%d' % i, [128, 512], F32) for i in range(4)]
                for q4 in range(4):
                    dma('sp', wdn[:, q4 * 8:(q4 + 1) * 8, :], wdnb_d[q4 * 1024:(q4 + 1) * 1024, :].rearrange('(f p) n -> p f n', p=128),
                        reads=[('wdnb', r) for r in range(q4 * 8, q4 * 8 + 8)], writes=[('wdn', q4)])
                nu = 0
                nx = 0
                for tb in range(NB):
                    for g in range(8):
                        wbb = (tb * 8 + g) % 2
                        dma('sp', wu[wbb][:], wupb_d[:, g * 512:(g + 1) * 512].rearrange('(kc p) n -> p kc n', p=128),
                            reads=[('wupb', r) for r in range(8)], writes=[('wu', wbb)])
                        for f in range(4):
                            fc = 4 * g + f
                            pb = nu % 2
                            nu += 1
                            for kc in range(8):
                                op('pe', lambda e: e.matmul(pU[pb][:], lhsT=wu[wbb][:, kc, f * 128:(f + 1) * 128],
                                                            rhs=hT[:, kc, tb * 512:(tb + 1) * 512], start=(kc == 0), stop=(kc == 7)),
                                   reads=[('wu', wbb), ('hT', tb)], writes=[('pU', pb)])
                            op('act', lambda e: e.activation(out=rr[pb][:], in_=pU[pb][:], func=AF.Relu),
                               reads=[('pU', pb)], writes=[('rr', pb)])
                            op('pool', lambda e: e.tensor_tensor(out=aT[:, fc, :], in0=rr[pb][:], in1=rr[pb][:], op=ALU.mult),
                               reads=[('rr', pb)], writes=[('aT', fc)])
                    for ts in range(4):
                        i = tb * 4 + ts
                        xb = nx % 2
                        nx += 1
                        tsl = slice(i * 128, (i + 1) * 128)
                        dma('sp', x1t[xb][:], xmid_d[tsl, :], writes=[('x1t', xb)])
                        for half in range(2):
                            pk = 2 * xb + half
                            for fc in range(32):
                                op('pe', lambda e: e.matmul(pD[pk][:], lhsT=aT[:, fc, ts * 128:(ts + 1) * 128],
                                                            rhs=wdn[:, fc, half * 512:(half + 1) * 512], start=(fc == 0), stop=(fc == 31)),
                                   reads=[('aT', fc), ('wdn', fc // 8)], writes=[('pD', pk)])
                            op('dve', lambda e: e.tensor_tensor(out=x1t[xb][:, half * 512:(half + 1) * 512],
                                                                in0=x1t[xb][:, half * 512:(half + 1) * 512], in1=pD[pk][:], op=ALU.add),
                               reads=[('x1t', xb), ('pD', pk)], writes=[('x1t', xb)])
                        dma('sp', dst_d[tsl, :], x1t[xb][:], reads=[('x1t', xb)])
        S_.finish('sp')
    return nc


_CACHE = {}


def kernel(**inputs):
    x = np.ascontiguousarray(np.asarray(inputs['x'], dtype=np.float32))
    B, S, _ = x.shape
    L = int(np.asarray(inputs['w_in']).shape[0])
    key = (S, L)
    if key not in _CACHE:
        _CACHE[key] = build(S, L)
    nc = _CACHE[key]
    names = ['norm_mix_g', 'w_in', 'b_if', 'b_gate', 'conv_w', 'mlstm_norm_g', 'sb_q_norm_g', 'sb_k_norm_g',
             'w_out', 'norm_mlp_g', 'w_up', 'w_down']
    shared = {k: np.ascontiguousarray(np.asarray(inputs[k], dtype=np.float32)) for k in names}
    n = 8
    in_maps = []
    for c in range(n):
        m = dict(shared)
        m['x'] = x[c % B]
        in_maps.append(m)
    res = run_bass_kernel_spmd(nc, in_maps, core_ids=list(range(n)))
    return np.stack([res.results[b]['out'] for b in range(B)], axis=0).astype(np.float32)
```

```python
import math
from contextlib import ExitStack

import numpy as np
import concourse.bass as bass
import concourse.mybir as mybir
from concourse.bass_utils import run_bass_kernel_spmd

F32 = mybir.dt.float32
BF16 = mybir.dt.bfloat16
AF = mybir.ActivationFunctionType
ALU = mybir.AluOpType
AX = mybir.AxisListType

D = 1024
NH = 8
EPS = 1e-6
QA, KA, VA, OA, IA, FA, QB, KB, VB, GA, GB = 0, 512, 1024, 2048, 3072, 3080, 3088, 4112, 5136, 6160, 7184
INC = 8208
DFF = 4096
NEG = -30000.0
GATE_GROUP = -1
import os
PC_STAGES = os.environ.get('PC_STAGES', 'fmb')


class Sched:
    def __init__(self, nc, st, n_dma_sems=22):
        self.nc = nc
        self.st = st
        self.eng = {'pe': nc.tensor, 'act': nc.scalar, 'dve': nc.vector, 'pool': nc.gpsimd, 'sp': nc.sync}
        self.sem = {}
        self.cnt = {}
        self.nphase = 0
        for e in ['pe', 'act', 'dve', 'pool']:
            self.sem[e] = st.enter_context(nc.semaphore('s_%s_0' % e))
            self.cnt[e] = 0
        self.dsem = [st.enter_context(nc.semaphore('d%d' % i)) for i in range(n_dma_sems)]
        self.bsem = [st.enter_context(nc.semaphore('b%d' % i)) for i in range(6)]
        self.bcnt = [0] * 6
        self.bnext = 0
        self.bg = {}
        self.dcnt = [0] * n_dma_sems
        self.dnext = 0
        self.npool0 = n_dma_sems - 6
        self.dnext_pool = self.npool0
        self.known = {e: {} for e in self.eng}
        self.last_w = {}
        self.readers = {}
        self.log = None
        self._cur = None

    def _wait(self, e, tk):
        key, val, src = tk
        if self.known[e].get(key, 0) >= val:
            return
        if self.log is not None and self._cur is not None:
            self._cur.append((key, val))
        if isinstance(key, str):
            sem = self.sem[key]
        elif isinstance(key, tuple):
            sem = self.bsem[key[1]]
        else:
            sem = self.dsem[key]
        self.eng[e].wait_ge(sem, val)
        self.known[e][key] = val

    def _deps(self, e, reads, writes):
        for r in reads:
            w = self.last_w.get(r)
            if w is not None:
                self._wait(e, w)
            w = self.bg.get(r)
            if w is not None:
                self._wait(e, w)
        for r in writes:
            w = self.last_w.get(r)
            if w is not None and (w[2] != e or e != 'pe'):
                self._wait(e, w)
            for rd in self.readers.get(r, {}).values():
                if rd[2] != e or e != 'pe':
                    self._wait(e, rd)

    def _commit(self, tk, reads, writes):
        for r in reads:
            self.readers.setdefault(r, {})[tk[2]] = tk
        for r in writes:
            self.last_w[r] = tk
            self.readers[r] = {}

    def op(self, e, fn, reads=(), writes=()):
        if self.log is not None:
            self._cur = []
        self._deps(e, reads, writes)
        inst = fn(self.eng[e])
        self.cnt[e] += 1
        if self.log is not None:
            self.log.append((e, self.cnt[e], list(reads), list(writes), self._cur))
            self._cur = None
        inst.then_inc(self.sem[e], 1)
        tk = (e, self.cnt[e], e)
        self._commit(tk, reads, writes)
        return tk

    def dma(self, q, out, in_, reads=(), writes=()):
        if q == 'pool':
            j = self.dnext_pool
            self.dnext_pool = self.npool0 + (self.dnext_pool + 1 - self.npool0) % (len(self.dsem) - self.npool0)
        else:
            j = self.dnext
            self.dnext = (self.dnext + 1) % self.npool0
        if self.dcnt[j] > 0:
            self._wait(q, (j, self.dcnt[j], 'dma'))
        self._deps(q, reads, writes)
        inst = self.eng[q].dma_start(out=out, in_=in_)
        self.dcnt[j] += 16
        inst.then_inc(self.dsem[j], 16)
        tk = (j, self.dcnt[j], 'dma%d' % j)
        self._commit(tk, reads, writes)
        return tk

    def dma_bg(self, q, out, in_, key):
        j = self.bnext
        self.bnext = (self.bnext + 1) % len(self.bsem)
        if self.bcnt[j] > 0:
            self._wait(q, (('b', j), self.bcnt[j], 'bg'))
        inst = self.eng[q].dma_start(out=out, in_=in_)
        self.bcnt[j] += 16
        inst.then_inc(self.bsem[j], 16)
        self.bg[key] = (('b', j), self.bcnt[j], 'bg')

    def _all_tickets(self):
        tks = [(e, self.cnt[e], e) for e in self.cnt if self.cnt[e] > 0]
        tks += [(j, self.dcnt[j], 'dma') for j in range(len(self.dsem)) if self.dcnt[j] > 0]
        return tks

    def barrier(self):
        tks = self._all_tickets()
        for e in self.eng:
            for tk in tks:
                if tk[2] == e:
                    continue
                self._wait(e, tk)
        self.last_w = {}
        self.readers = {}

    def new_phase(self):
        self.barrier()
        self.nphase += 1
        for e in ['pe', 'act', 'dve', 'pool']:
            self.sem[e] = self.st.enter_context(self.nc.semaphore('s_%s_%d' % (e, self.nphase)))
            self.cnt[e] = 0
        for e in self.known:
            self.known[e] = {k: v for k, v in self.known[e].items() if not isinstance(k, str)}

    def finish(self, e='sp'):
        for tk in self._all_tickets():
            self._wait(e, tk)
        for j in range(len(self.bsem)):
            if self.bcnt[j] > 0:
                self._wait(e, (('b', j), self.bcnt[j], 'bg'))


def build(S, L, dbg=False, stop_after=None):
    NT = S // 128
    NB = S // 512
    nc = bass.Bass('TRN2', target_bir_lowering=False)
    dt = nc.dram_tensor
    x_d = dt('x', [S, D], F32, kind='ExternalInput').ap()
    g1_d = dt('norm_mix_g', [L, D], F32, kind='ExternalInput').ap()
    win_d = dt('w_in', [L, D, INC], F32, kind='ExternalInput').ap()
    bif_d = dt('b_if', [L, 16], F32, kind='ExternalInput').ap()
    bg_d = dt('b_gate', [L, 2 * D], F32, kind='ExternalInput').ap()
    cw_d = dt('conv_w', [L, 4, D], F32, kind='ExternalInput').ap()
    gm_d = dt('mlstm_norm_g', [L, D], F32, kind='ExternalInput').ap()
    gq_d = dt('sb_q_norm_g', [L, 128], F32, kind='ExternalInput').ap()
    gk_d = dt('sb_k_norm_g', [L, 128], F32, kind='ExternalInput').ap()
    wout_d = dt('w_out', [L, D, D], F32, kind='ExternalInput').ap()
    g2_d = dt('norm_mlp_g', [L, D], F32, kind='ExternalInput').ap()
    wup_d = dt('w_up', [L, D, DFF], F32, kind='ExternalInput').ap()
    wdn_d = dt('w_down', [L, DFF, D], F32, kind='ExternalInput').ap()
    out_d = dt('out', [S, D], F32, kind='ExternalOutput').ap()
    sk = 'ExternalOutput' if dbg else 'Internal'
    xmid_d = dt('xmid', [S, D], F32, kind=sk).ap()
    xl_d = dt('xl', [S, D], F32, kind=sk).ap()
    va_d = dt('va', [S, D], BF16, kind=sk).ap()
    vb_d = dt('vb', [S, D], BF16, kind=sk).ap()
    qlo_d = dt('qlo', [4, 128, S], BF16, kind=sk).ap()
    qhi_d = dt('qhi', [4, 128, S], BF16, kind=sk).ap()
    ka_d = dt('ka', [4, 128, S], BF16, kind=sk).ap()
    gaT_d = dt('gaT', [8, 128, S], BF16, kind=sk).ap()
    gbT_d = dt('gbT', [8, 128, S], BF16, kind=sk).ap()
    qbT_d = dt('qbT', [8, 128, S], BF16, kind=sk).ap()
    kbT_d = dt('kbT', [8, 128, S], BF16, kind=sk).ap()
    yaT_d = dt('yaT', [8, 128, S], BF16, kind=sk).ap()
    winb_d = dt('winb', [D, INC], BF16, kind='Internal').ap()
    woutb_d = dt('woutb', [D, D], BF16, kind='Internal').ap()
    wupb_d = dt('wupb', [D, DFF], BF16, kind='Internal').ap()
    wdnb_d = dt('wdnb', [DFF, D], BF16, kind='Internal').ap()

    with ExitStack() as st:
        S_ = Sched(nc, st)
        op, dma = S_.op, S_.dma

        uid = [0]

        def sb(stack, name, shape, dtp):
            uid[0] += 1
            return stack.enter_context(nc.sbuf_tensor('%s_%d' % (name, uid[0]), shape, dtp))

        def ps(stack, name, shape, dtp):
            uid[0] += 1
            return stack.enter_context(nc.psum_tensor('%s_%d' % (name, uid[0]), shape, dtp))

        R1 = sb(st, 'R1', [128, 8, S], BF16)
        R2w = max(8 * S, 32 * 1024)
        R2 = sb(st, 'R2', [128, R2w], BF16)
        yT = R2[:, 0:8 * S].rearrange('p (c t) -> p c t', c=8)
        wdn = R2[:, 0:32 * 1024].rearrange('p (f n) -> p f n', f=32)
        hT = R1
        ident = sb(st, 'ident', [128, 128], BF16)
        cf = sb(st, 'cf', [128, 128], F32)
        triU = sb(st, 'triU', [128, 128], F32)
        onesf = sb(st, 'onesf', [128, 128], F32)
        triNeg = sb(st, 'triNeg', [128, 128], BF16)
        ones_bf = sb(st, 'ones_bf', [128, 128], BF16)
        negones = sb(st, 'negones', [128, 128], BF16)
        sel = sb(st, 'sel', [128, 4, 128], BF16)
        maskneg = sb(st, 'maskneg', [128, 4, 512], BF16)
        gqk = sb(st, 'gqk', [128, 2], F32)
        bgT = sb(st, 'bgT', [128, 16], F32)
        gmT = sb(st, 'gmT', [128, 8], F32)
        cwT = sb(st, 'cwT', [128, 8, 4], F32)
        bifB = sb(st, 'bifB', [128, 16], F32)
        gates = sb(st, 'gates', [128, NT, 2, 8], F32)
        decB = sb(st, 'decB', [128, NT, 4], F32)

        op('pool', lambda e: e.memset(cf[:], 1.0), writes=['cf'])
        op('pool', lambda e: e.affine_select(out=cf[:], in_=cf[:], pattern=[[1, 128]], compare_op=ALU.is_equal,
                                            fill=0.0, base=0, channel_multiplier=-1), reads=['cf'], writes=['cf'])
        op('dve', lambda e: e.tensor_copy(ident[:], cf[:]), reads=['cf'], writes=['ident'])
        op('pool', lambda e: e.memset(onesf[:], 1.0), writes=['onesf'])
        op('pool', lambda e: e.affine_select(out=triU[:], in_=onesf[:], pattern=[[1, 128]], compare_op=ALU.is_ge,
                                            fill=0.0, base=0, channel_multiplier=-1), reads=['onesf'], writes=['triU'])
        op('dve', lambda e: e.memset(cf[:], -1.0), reads=['ident'], writes=['cf'])
        op('pool', lambda e: e.affine_select(out=cf[:], in_=cf[:], pattern=[[-1, 128]], compare_op=ALU.is_ge,
                                            fill=0.0, base=0, channel_multiplier=1), reads=['cf'], writes=['cf'])
        op('dve', lambda e: e.tensor_copy(triNeg[:], cf[:]), reads=['cf'], writes=['triNeg'])
        op('dve', lambda e: e.memset(ones_bf[:], 1.0), writes=['ones_bf'])
        op('dve', lambda e: e.memset(negones[:], -1.0), writes=['negones'])
        op('dve', lambda e: e.memset(sel[:], 0.0), writes=['sel'])
        for (k, rows, val) in ((0, (0, 32), -1.0), (1, (64, 96), -1.0), (2, (0, 32), 1.0), (3, (64, 96), 1.0)):
            for r in rows:
                op('dve', lambda e: e.memset(sel[r:r + 1, k, :], val), writes=['sel'])
        mst = ExitStack()
        mtmp = sb(mst, 'mtmp', [128, 512], F32)
        for k in range(4):
            op('pool', lambda e: e.memset(mtmp[:], 0.0), reads=['mtmp'], writes=['mtmp'])
            op('pool', lambda e: e.affine_select(out=mtmp[:], in_=mtmp[:], pattern=[[1, 512]], compare_op=ALU.is_gt,
                                                fill=NEG, base=-128 * k, channel_multiplier=-1),
               reads=['mtmp'], writes=['mtmp'])
            op('dve', lambda e: e.tensor_copy(maskneg[:, k, :], mtmp[:]), reads=['mtmp'], writes=['maskneg'])
        S_.barrier()
        mst.close()

        for l in range(L):
            src_d = x_d if l == 0 else xl_d
            dst_d = out_d if l == L - 1 else xl_d
            S_.new_phase()
            dma('sp', bifB[:], bif_d[l:l + 1, :].broadcast_to([128, 16]), writes=['bifB'])
            with nc.allow_non_contiguous_dma(reason='tiny param transposes'):
                dma('sp', bgT[:], bg_d[l, :].rearrange('(c p) -> p c', p=128), writes=['bgT'])
                dma('sp', gqk[:, 0:1], gq_d[l, :].rearrange('(p o) -> p o', o=1), writes=['gqk'])
                dma('sp', gqk[:, 1:2], gk_d[l, :].rearrange('(p o) -> p o', o=1), writes=['gqk'])
                dma('sp', gmT[:], gm_d[l, :].rearrange('(c p) -> p c', p=128), writes=['gmT'])
                for jj in range(4):
                    dma('sp', cwT[:, :, jj], cw_d[l, jj, :].rearrange('(c p) -> p c', p=128), writes=['cwT'])
            op('dve', lambda e: e.tensor_scalar(out=gqk[:, 0:1], in0=gqk[:, 0:1], scalar1=128.0 ** -0.5, scalar2=None, op0=ALU.mult),
               reads=['gqk'], writes=['gqk'])

            def run_skewed(make_gen, ntiles):
                active = []
                nxt = 0
                while nxt < ntiles or active:
                    for gen in list(active):
                        try:
                            next(gen)
                        except StopIteration:
                            active.remove(gen)
                    if nxt < ntiles:
                        gen = make_gen(nxt)
                        nxt += 1
                        try:
                            next(gen)
                            active.append(gen)
                        except StopIteration:
                            pass

            with ExitStack() as ph:
                gB = sb(ph, 'gB', [128, D], F32)
                dma('sp', gB[:], g1_d[l:l + 1, :].broadcast_to([128, D]), writes=['gB'])
                xt = [sb(ph, 'xt%d' % i, [128, D], F32) for i in range(4)]
                junk = sb(ph, 'junk', [128, D], BF16)
                hn = [sb(ph, 'hn%d' % i, [128, D], BF16) for i in range(2)]
                stt = [sb(ph, 'stt%d' % i, [128, 4], F32) for i in range(4)]
                tp = [ps(ph, 'tpA%d' % i, [128, 8, 128], BF16) for i in range(2)]

                def a_tile(i):
                    b = i % 2
                    b4 = i % 4
                    dma('sp', xt[b4][:], src_d[i * 128:(i + 1) * 128, :], writes=[('xt', b4)])
                    op('dve', lambda e: e.scalar_tensor_tensor(out=junk[:], in0=xt[b4][:], scalar=1.0, in1=xt[b4][:],
                                                               op0=ALU.mult, op1=ALU.mult, accum_out=stt[b4][:, 0:1]),
                       reads=[('xt', b4)], writes=['junk', ('stt', b4)])
                    yield
                    op('act', lambda e: e.activation(out=stt[b4][:, 1:2], in_=stt[b4][:, 0:1], func=AF.Sqrt,
                                                     scale=1.0 / D, bias=EPS), reads=[('stt', b4)], writes=[('stt', b4)])
                    yield
                    op('dve', lambda e: e.reciprocal(stt[b4][:, 2:3], stt[b4][:, 1:2]), reads=[('stt', b4)], writes=[('stt', b4)])
                    op('dve', lambda e: e.scalar_tensor_tensor(out=hn[b][:], in0=xt[b4][:], scalar=stt[b4][:, 2:3], in1=gB[:],
                                                               op0=ALU.mult, op1=ALU.mult),
                       reads=[('xt', b4), ('stt', b4), 'gB'], writes=[('hn', b)])
                    yield
                    for k in range(8):
                        op('pe', lambda e: e.transpose(tp[b][:, k, :], hn[b][:, k * 128:(k + 1) * 128], ident[:]),
                           reads=[('hn', b), 'ident'], writes=[('tp', b)])
                    op('act', lambda e: e.activation(out=hT[:, :, i * 128:(i + 1) * 128], in_=tp[b][:], func=AF.Copy),
                       reads=[('tp', b)], writes=[('hT', i // 4)])

                run_skewed(a_tile, NT)
            if stop_after == 'A':
                break

            S_.new_phase()
            winb_keys = [('winb', r) for r in range(8)]
            with ExitStack() as ph:
                wb = [sb(ph, 'wb%d' % i, [128, 8, 512], BF16) for i in range(2)]
                pB = [ps(ph, 'pB%d' % i, [128, 512], F32) for i in range(3)]
                tpq = ps(ph, 'tpq', [128, 4, 128], BF16)
                pF = [ps(ph, 'pF%d' % i, [128, 512], F32) for i in range(2)]
                pI = ps(ph, 'pI', [128, 512], F32)
                pI2 = ps(ph, 'pI2', [128, 512], F32)
                vst = [sb(ph, 'vst%d' % i, [128, 512], BF16) for i in range(4)]
                sq = sb(ph, 'sq', [128, 512], F32)
                st4 = sb(ph, 'st4', [128, 12], F32)
                qnb = [sb(ph, 'qnb%d' % i, [128, 4, 128], BF16) for i in range(3)]
                qTs = [sb(ph, 'qTs%d' % i, [128, 4, 128], BF16) for i in range(4)]
                wf = [sb(ph, 'wf%d' % i, [128, 8, 128], BF16) for i in range(6)]
                pre2 = [sb(ph, 'pre2_%d' % i, [128, 3 + 512], F32) for i in range(2)]
                acc2 = [sb(ph, 'acc2_%d' % i, [128, 512], F32) for i in range(2)]
                qlo_s = [sb(ph, 'qlo_s%d' % i, [128, 512], BF16) for i in range(4)]
                qhi_s = [sb(ph, 'qhi_s%d' % i, [128, 512], BF16) for i in range(4)]
                so = sb(ph, 'so', [128, 512], F32)
                sg = sb(ph, 'sg', [128, 512], F32)
                gas = [sb(ph, 'gas%d' % i, [128, 512], BF16) for i in range(4)]
                wif = sb(ph, 'wif', [128, 8, 16], BF16)
                pif = [sb(ph, 'pif%d' % i, [128, 16], F32) for i in range(2)]
                t8 = [sb(ph, 't8%d' % i, [128, 40], F32) for i in range(2)]
                cum = [sb(ph, 'cum%d' % i, [128, 16], F32) for i in range(2)]

                if l == 0:
                    with nc.allow_non_contiguous_dma(reason='small gate weight rows'):
                        dma('pool', wif[:], win_d[l][:, IA:IA + 16].rearrange('(kc p) n -> p kc n', p=128), writes=['wif'])
                else:
                    with nc.allow_non_contiguous_dma(reason='small gate weight rows'):
                        dma('sp', wif[:], winb_d[:, IA:IA + 16].rearrange('(kc p) n -> p kc n', p=128), reads=winb_keys, writes=['wif'])

                def gate_s1(i):
                    j = i % 2
                    for kc in range(8):
                        op('pe', lambda e: e.matmul(pI[:, 0:16], lhsT=hT[:, kc, i * 128:(i + 1) * 128], rhs=wif[:, kc, :],
                                                    start=(kc == 0), stop=(kc == 7)),
                           reads=[('hT', i // 4), 'wif'], writes=['pI'])
                    op('dve', lambda e: e.tensor_tensor(out=pif[j][:], in0=pI[:, 0:16], in1=bifB[:], op=ALU.add),
                       reads=['pI', 'bifB'], writes=[('pif', j)])
                    op('act', lambda e: e.activation(out=t8[j][:, 0:8], in_=pif[j][:, 8:16], func=AF.Exp, scale=-1.0),
                       reads=[('pif', j)], writes=[('t8a', j)])
                    op('act', lambda e: e.activation(out=t8[j][:, 8:16], in_=t8[j][:, 0:8], func=AF.Ln, bias=1.0),
                       reads=[('t8a', j)], writes=[('t8b', j)])

                def gate_s2(i):
                    j = i % 2
                    op('pe', lambda e: e.matmul(pI2[:, 0:8], lhsT=triU[:], rhs=t8[j][:, 8:16], start=True, stop=True),
                       reads=[('t8b', j), 'triU'], writes=['pI2'])
                    op('pe', lambda e: e.matmul(pI2[:, 8:16], lhsT=onesf[:], rhs=t8[j][:, 8:16], start=True, stop=True),
                       reads=[('t8b', j), 'onesf'], writes=['pI2'])
                    op('act', lambda e: e.activation(out=cum[j][:], in_=pI2[:, 0:16], func=AF.Copy), reads=['pI2'], writes=[('cum', j)])
                    op('dve', lambda e: e.tensor_tensor(out=t8[j][:, 16:24], in0=cum[j][:, 0:8], in1=cum[j][:, 8:16], op=ALU.subtract),
                       reads=[('cum', j)], writes=[('t8c', j)])
                    op('dve', lambda e: e.tensor_tensor(out=t8[j][:, 24:32], in0=t8[j][:, 16:24], in1=pif[j][:, 0:8], op=ALU.add),
                       reads=[('t8c', j), ('pif', j)], writes=[('t8d', j)])
                    op('act', lambda e: e.activation(out=gates[:, i, 0, :], in_=t8[j][:, 24:32], func=AF.Exp, bias=-math.log(8.0)),
                       reads=[('t8d', j)], writes=[('gates', i, 0)])
                    op('act', lambda e: e.activation(out=gates[:, i, 1, :], in_=t8[j][:, 16:24], func=AF.Exp, scale=-1.0),
                       reads=[('t8c', j)], writes=[('gates', i, 1)])
                    op('act', lambda e: e.activation(out=t8[j][:, 32:40], in_=cum[j][:, 8:16], func=AF.Exp, scale=-1.0),
                       reads=[('cum', j)], writes=[('t8e', j)])
                    op('dve', lambda e: e.tensor_copy(decB[0:64, i, :], t8[j][0:64, 32:40:2]), reads=[('t8e', j)], writes=[('decB', i, 0)])
                    op('dve', lambda e: e.tensor_copy(decB[64:128, i, :], t8[j][64:128, 33:40:2]), reads=[('t8e', j)], writes=[('decB', i, 1)])

                groups = []
                for kind, c0 in (('va', VA), ('qb', QB), ('kb', KB), ('vb', VB)):
                    groups.append((kind, c0, 0))
                    groups.append((kind, c0 + 512, 1))
                n = 0
                nqk = 0
                nfin = 0
                pend = []
                def load_wb(gi):
                    c0g = groups[gi][1]
                    wbb_ = gi % 2
                    if l == 0:
                        dma('pool', wb[wbb_][:], win_d[l][:, c0g:c0g + 512].rearrange('(kc p) n -> p kc n', p=128),
                            writes=[('wb', wbb_)])
                    else:
                        dma('sp', wb[wbb_][:], winb_d[:, c0g:c0g + 512].rearrange('(kc p) n -> p kc n', p=128),
                            reads=winb_keys, writes=[('wb', wbb_)])

                load_wb(0)
                for gi, (kind, c0, half) in enumerate(groups):
                    wbb = gi % 2
                    if gi + 1 < len(groups):
                        load_wb(gi + 1)
                    for i in range(NT):
                        pb = n % 3
                        j = n % 4
                        n += 1
                        for kc in range(8):
                            op('pe', lambda e: e.matmul(pB[pb][:], lhsT=hT[:, kc, i * 128:(i + 1) * 128], rhs=wb[wbb][:, kc, :],
                                                        start=(kc == 0), stop=(kc == 7)),
                               reads=[('hT', i // 4), ('wb', wbb)], writes=[('pB', pb)])
                        if kind in ('va', 'vb'):
                            dstv = va_d if kind == 'va' else vb_d
                            op('act', lambda e: e.activation(out=vst[j][:], in_=pB[pb][:], func=AF.Copy),
                               reads=[('pB', pb)], writes=[('vst', j)])
                            dma('sp', dstv[i * 128:(i + 1) * 128, half * 512:(half + 1) * 512], vst[j][:], reads=[('vst', j)])
                        else:
                            gcol = 0 if kind == 'qb' else 1
                            dstT = qbT_d if kind == 'qb' else kbT_d
                            j3 = nqk % 3
                            nqk += 1
                            op('act', lambda e: e.activation(out=sq[:], in_=pB[pb][:], func=AF.Square),
                               reads=[('pB', pb)], writes=['sq'])
                            op('dve', lambda e: e.tensor_reduce(out=st4[:, 0:4], in_=sq[:].rearrange('p (h d) -> p h d', d=128),
                                                                axis=AX.X, op=ALU.add), reads=['sq'], writes=['st4'])
                            op('act', lambda e: e.activation(out=st4[:, 4:8], in_=st4[:, 0:4], func=AF.Sqrt, scale=1.0 / 128, bias=EPS),
                               reads=['st4'], writes=['st4'])
                            op('dve', lambda e: e.reciprocal(st4[:, 8:12], st4[:, 4:8]), reads=['st4'], writes=['st4'])
                            op('dve', lambda e: e.tensor_tensor(out=qnb[j3][:], in0=pB[pb][:].rearrange('p (h d) -> p h d', d=128),
                                                                in1=st4[:, 8:12].unsqueeze(2).broadcast_to([128, 4, 128]), op=ALU.mult),
                               reads=[('pB', pb), 'st4'], writes=[('qnb', j3)])

                            def fin(j3=j3, gcol=gcol, dstT=dstT, half=half, i=i):
                                nonlocal nfin
                                jj = nfin % 4
                                nfin += 1
                                for hh in range(4):
                                    op('pe', lambda e: e.transpose(tpq[:, hh, :], qnb[j3][:, hh, :], ident[:]),
                                       reads=[('qnb', j3), 'ident'], writes=['tpq'])
                                op('act', lambda e: e.activation(out=qTs[jj][:], in_=tpq[:], func=AF.Copy, scale=gqk[:, gcol:gcol + 1]),
                                   reads=['tpq', 'gqk'], writes=[('qTs', jj)])
                                dma('sp', dstT[half * 4:(half + 1) * 4, :, i * 128:(i + 1) * 128].rearrange('h d t -> d h t'),
                                    qTs[jj][:], reads=[('qTs', jj)])
                            pend.append(fin)
                            if len(pend) > 2:
                                pend.pop(0)()
                        if gi == GATE_GROUP:
                            gate_s1(i)
                            if i > 0:
                                gate_s2(i - 1)
                    if gi == GATE_GROUP:
                        gate_s2(NT - 1)
                while pend:
                    pend.pop(0)()
                if GATE_GROUP < 0:
                    for i in range(NT + 1):
                        if i < NT:
                            gate_s1(i)
                        if i > 0:
                            gate_s2(i - 1)

                nf = 0
                nw = 0

                wf_cols = [cc * 128 for cc in range(8)]
                for cc in range(8):
                    wf_cols += [OA + cc * 128, GA + cc * 128, GB + cc * 128]
                wf_issued = [0]

                def issue_wf():
                    idx = wf_issued[0]
                    if idx >= len(wf_cols):
                        return
                    wf_issued[0] += 1
                    c0w = wf_cols[idx]
                    kk = idx % 6
                    with nc.allow_non_contiguous_dma(reason='512B weight rows'):
                        if l == 0:
                            dma('pool', wf[kk][:], win_d[l][:, c0w:c0w + 128].rearrange('(kc p) n -> p kc n', p=128), writes=[('wf', kk)])
                        else:
                            dma('sp', wf[kk][:], winb_d[:, c0w:c0w + 128].rearrange('(kc p) n -> p kc n', p=128),
                                reads=winb_keys, writes=[('wf', kk)])

                def load_wf(c0):
                    nonlocal nw
                    assert wf_cols[nw] == c0
                    k = nw % 6
                    nw += 1
                    while wf_issued[0] < min(nw + 2, len(wf_cols)):
                        issue_wf()
                    return k

                issue_wf()
                issue_wf()

                def fm_matmul(k, tb):
                    nonlocal nf
                    pb = nf % 2
                    nf += 1
                    for kc in range(8):
                        op('pe', lambda e: e.matmul(pF[pb][:], lhsT=wf[k][:, kc, :], rhs=hT[:, kc, tb * 512:(tb + 1) * 512],
                                                    start=(kc == 0), stop=(kc == 7)),
                           reads=[('hT', tb), ('wf', k)], writes=[('pF', pb)])
                    return pb

                for i in range(4):
                    op('pool', lambda e: e.memset(qlo_s[i][:], 0.0), writes=[('qlo_s', i)])
                    op('pool', lambda e: e.memset(qhi_s[i][:], 0.0), writes=[('qhi_s', i)])
                nq = 0
                nblk = 0
                pendc = []
                for cc in range(8):
                    k = load_wf(cc * 128)
                    for tb in range(NB):
                        pb = fm_matmul(k, tb)
                        sl = nblk % 2
                        nblk += 1
                        P = pre2[sl]
                        Pp = pre2[1 - sl]
                        A = acc2[sl]
                        if tb == 0:
                            op('pool', lambda e: e.memset(P[:, 0:3], 0.0), writes=[('pre', sl)])
                        else:
                            op('pool', lambda e: e.tensor_copy(P[:, 0:3], Pp[:, 512:515]), reads=[('pre', 1 - sl)], writes=[('pre', sl)])
                        op('act', lambda e: e.activation(out=P[:, 3:515], in_=pF[pb][:], func=AF.Copy),
                           reads=[('pF', pb)], writes=[('pre', sl)])
                        op('act', lambda e: e.activation(out=A[:], in_=pF[pb][:], func=AF.Copy, scale=cwT[:, cc, 3:4]),
                           reads=[('pF', pb), 'cwT'], writes=[('acc', sl)])
                        for jj in range(0, 3):
                            op('dve', lambda e: e.scalar_tensor_tensor(out=A[:], in0=P[:, jj:jj + 512],
                                                                       scalar=cwT[:, cc, jj:jj + 1], in1=A[:],
                                                                       op0=ALU.mult, op1=ALU.add),
                               reads=[('pre', sl), 'cwT', ('acc', sl)], writes=[('acc', sl)])
                        def fin_conv(cc=cc, tb=tb, A=A, sl=sl):
                            nonlocal nq
                            j = nq % 4
                            nq += 1
                            if cc < 4:
                                op('act', lambda e: e.activation(out=qlo_s[j][0:64, :], in_=A[0:64, :], func=AF.Silu),
                                   reads=[('acc', sl)], writes=[('qlo_s', j)])
                                op('act', lambda e: e.activation(out=qhi_s[j][64:128, :], in_=A[64:128, :], func=AF.Silu),
                                   reads=[('acc', sl)], writes=[('qhi_s', j)])
                                dma('sp', qlo_d[cc, :, tb * 512:(tb + 1) * 512], qlo_s[j][:], reads=[('qlo_s', j)])
                                dma('sp', qhi_d[cc, :, tb * 512:(tb + 1) * 512], qhi_s[j][:], reads=[('qhi_s', j)])
                            else:
                                op('act', lambda e: e.activation(out=gas[j][:], in_=A[:], func=AF.Silu),
                                   reads=[('acc', sl)], writes=[('gas', j)])
                                dma('sp', ka_d[cc - 4, :, tb * 512:(tb + 1) * 512], gas[j][:], reads=[('gas', j)])
                        pendc.append(fin_conv)
                        if len(pendc) > 1:
                            pendc.pop(0)()
                while pendc:
                    pendc.pop(0)()
                for cc in range(8):
                    ko = load_wf(OA + cc * 128)
                    kg = load_wf(GA + cc * 128)
                    kb = load_wf(GB + cc * 128)
                    for tb in range(NB):
                        pb = fm_matmul(ko, tb)
                        op('act', lambda e: e.activation(out=so[:], in_=pF[pb][:], func=AF.Sigmoid),
                           reads=[('pF', pb)], writes=['so'])
                        pb = fm_matmul(kg, tb)
                        op('act', lambda e: e.activation(out=sg[:], in_=pF[pb][:], func=AF.Sigmoid, bias=bgT[:, cc:cc + 1]),
                           reads=[('pF', pb), 'bgT'], writes=['sg'])
                        j = nq % 4
                        nq += 1
                        op('dve', lambda e: e.scalar_tensor_tensor(out=gas[j][:], in0=sg[:], scalar=gmT[:, cc:cc + 1], in1=so[:],
                                                                   op0=ALU.mult, op1=ALU.mult),
                           reads=['sg', 'so', 'gmT'], writes=[('gas', j)])
                        dma('sp', gaT_d[cc, :, tb * 512:(tb + 1) * 512], gas[j][:], reads=[('gas', j)])
                        pb = fm_matmul(kb, tb)
                        j = nq % 4
                        nq += 1
                        op('act', lambda e: e.activation(out=gas[j][:], in_=pF[pb][:], func=AF.Sigmoid, bias=bgT[:, 8 + cc:9 + cc]),
                           reads=[('pF', pb), 'bgT'], writes=[('gas', j)])
                        dma('sp', gbT_d[cc, :, tb * 512:(tb + 1) * 512], gas[j][:], reads=[('gas', j)])
            if stop_after == 'B':
                break

            S_.new_phase()
            for r in range(8):
                S_.dma_bg('pool', woutb_d[r * 128:(r + 1) * 128, :], wout_d[l][r * 128:(r + 1) * 128, :], ('woutb', r))
            for r in range(32):
                S_.dma_bg('pool', wdnb_d[r * 128:(r + 1) * 128, :], wdn_d[l][r * 128:(r + 1) * 128, :], ('wdnb', r))
            for r in range(8):
                S_.dma_bg('pool', wupb_d[r * 128:(r + 1) * 128, :], wup_d[l][r * 128:(r + 1) * 128, :], ('wupb', r))
            if l + 1 < L:
                for r in range(8):
                    S_.dma_bg('pool', winb_d[r * 128:(r + 1) * 128, :], win_d[l + 1][r * 128:(r + 1) * 128, :], ('winb', r))
            with ExitStack() as ph:
                qlo_c = [sb(ph, 'qlo_c%d' % i, [128, 4, 128], BF16) for i in range(4)]
                qhi_c = [sb(ph, 'qhi_c%d' % i, [128, 4, 128], BF16) for i in range(4)]
                k_c = [sb(ph, 'k_c%d' % i, [128, 4, 128], BF16) for i in range(4)]
                v_c = [sb(ph, 'v_c%d' % i, [128, D], BF16) for i in range(4)]
                ga_c = [sb(ph, 'ga_c%d' % i, [128, 8, 128], BF16) for i in range(4)]
                PT = [sb(ph, 'PT%d' % i, [128, 8, 128], BF16) for i in range(2)]
                kw = [sb(ph, 'kw%d' % i, [128, 8, 64], BF16) for i in range(2)]
                C_all = sb(ph, 'C_all', [128, 4, 128], F32)
                n_all = sb(ph, 'n_all', [128, 4], F32)
                Cd = sb(ph, 'Cd', [128, 4, 128], F32)
                nd = sb(ph, 'nd', [128, 4], F32)
                Cd_bf = sb(ph, 'Cd_bf', [128, 4, 128], BF16)
                nd_bf = sb(ph, 'nd_bf', [128, 4], BF16)
                sm = sb(ph, 'sm', [128, 96], F32)
                den_sb = [sb(ph, 'den_sb%d' % i, [128, 8], F32) for i in range(2)]
                sqb = sb(ph, 'sqb', [128, D], F32)
                ya = sb(ph, 'ya', [128, D], BF16)
                yag = [sb(ph, 'yag%d' % i, [128, 8, 128], BF16) for i in range(4)]
                scp = ps(ph, 'scp', [128, 4, 128], F32)
                nump = [[ps(ph, 'nump%d_%d' % (i, j), [128, 4, 128], F32) for j in range(2)] for i in range(2)]
                dCp = ps(ph, 'dCp', [128, 2, 256], F32)
                misc = ps(ph, 'misc', [128, 512], F32)
                tp = ps(ph, 'tpC', [128, 8, 128], BF16)
                op('dve', lambda e: e.memset(C_all[:], 0.0), writes=['C_all'])
                op('dve', lambda e: e.memset(n_all[:], 0.0), writes=['n_all'])

                def qm(c, h):
                    b3 = c % 4
                    return (qlo_c[b3] if h % 2 == 0 else qhi_c[b3])[:, h // 2, :]

                def qmk(c, h):
                    return ('qlo_c', c % 4) if h % 2 == 0 else ('qhi_c', c % 4)

                def loads(c):
                    b3 = c % 4
                    tsl = slice(c * 128, (c + 1) * 128)
                    with nc.allow_non_contiguous_dma(reason='256B rows'):
                        dma('sp', qlo_c[b3][:], qlo_d[:, :, tsl].rearrange('c p t -> p c t'), writes=[('qlo_c', b3)])
                        dma('sp', qhi_c[b3][:], qhi_d[:, :, tsl].rearrange('c p t -> p c t'), writes=[('qhi_c', b3)])
                        dma('sp', k_c[b3][:], ka_d[:, :, tsl].rearrange('c p t -> p c t'), writes=[('k_c', b3)])
                        dma('sp', ga_c[b3][:], gaT_d[:, :, tsl].rearrange('c p t -> p c t'), writes=[('ga_c', b3)])
                    dma('sp', v_c[b3][:], va_d[tsl, :], writes=[('v_c', b3)])

                def front(c):
                    b3 = c % 4
                    p2 = c % 2
                    for hg in range(2):
                        for h in range(4 * hg, 4 * hg + 4):
                            op('pe', lambda e: e.matmul(scp[:, h % 4, :], lhsT=k_c[b3][:, h // 2, :], rhs=qm(c, h), start=True, stop=True),
                               reads=[('k_c', b3), qmk(c, h)], writes=['scp'])
                        for h in range(4 * hg, 4 * hg + 4):
                            op('dve', lambda e: e.scalar_tensor_tensor(out=PT[p2][:, h, :], in0=scp[:, h % 4, :],
                                                                       scalar=gates[:, c, 0, h:h + 1], in1=triU[:],
                                                                       op0=ALU.mult, op1=ALU.mult),
                               reads=['scp', ('gates', c, 0), 'triU'], writes=[('PT', p2, h)])
                        yield
                    for cc in range(4):
                        op('pe', lambda e: e.transpose(tp[:, cc, :], k_c[b3][:, cc, :], ident[:]),
                           reads=[('k_c', b3), 'ident'], writes=['tp'])
                    op('dve', lambda e: e.tensor_tensor(out=kw[p2][:], in0=tp[:, 0:4, :].rearrange('p c (j d) -> p (c j) d', j=2),
                                                        in1=gates[:, c, 0, :].unsqueeze(2).broadcast_to([128, 8, 64]), op=ALU.mult),
                       reads=['tp', ('gates', c, 0)], writes=[('kw', p2)])

                def mid(c):
                    b3 = c % 4
                    p2 = c % 2
                    op('dve', lambda e: e.tensor_tensor(out=Cd[:], in0=C_all[:], in1=decB[:, c, :].unsqueeze(2).broadcast_to([128, 4, 128]),
                                                        op=ALU.mult), reads=['C_all', ('decB', c, 0), ('decB', c, 1)], writes=['Cd'])
                    op('dve', lambda e: e.tensor_tensor(out=nd[:], in0=n_all[:], in1=decB[:, c, :], op=ALU.mult),
                       reads=['n_all', ('decB', c, 0), ('decB', c, 1)], writes=['nd'])
                    op('act', lambda e: e.activation(out=Cd_bf[:], in_=Cd[:], func=AF.Copy), reads=['Cd'], writes=['Cd_bf'])
                    op('act', lambda e: e.activation(out=nd_bf[:], in_=nd[:], func=AF.Copy), reads=['nd'], writes=['nd_bf'])
                    yield
                    for h in range(NH):
                        if h == 4:
                            yield
                        npk = ('nump', p2, h // 4)
                        nt_ = nump[p2][h // 4]
                        op('pe', lambda e: e.matmul(nt_[:, h % 4, :], lhsT=PT[p2][:, h, :], rhs=v_c[b3][:, h * 128:(h + 1) * 128],
                                                    start=True, stop=False),
                           reads=[('PT', p2, h), ('v_c', b3)], writes=[npk])
                        op('pe', lambda e: e.matmul(nt_[:, h % 4, :], lhsT=qm(c, h), rhs=Cd_bf[:, h // 2, :],
                                                    start=False, stop=True),
                           reads=[qmk(c, h), 'Cd_bf'], writes=[npk])
                        dcol = 16 * p2 + h
                        op('pe', lambda e: e.matmul(misc[:, dcol:dcol + 1], lhsT=PT[p2][:, h, :], rhs=ones_bf[:, 0:1], start=True, stop=False),
                           reads=[('PT', p2, h), 'ones_bf'], writes=['misc'])
                        op('pe', lambda e: e.matmul(misc[:, dcol:dcol + 1], lhsT=qm(c, h), rhs=nd_bf[:, h // 2:h // 2 + 1], start=False, stop=True),
                           reads=[qmk(c, h), 'nd_bf'], writes=['misc'])
                    yield
                    op('dve', lambda e: e.tensor_copy(den_sb[p2][:], misc[:, 16 * p2:16 * p2 + 8]), reads=['misc'], writes=[('den_sb', p2)])
                    for g2 in range(2):
                        if g2 == 1:
                            yield
                        for cc in (2 * g2, 2 * g2 + 1):
                            op('pe', lambda e: e.matmul(dCp[:, cc % 2, :], lhsT=kw[p2][:, 2 * cc:2 * cc + 2, :].rearrange('p j d -> p (j d)'),
                                                        rhs=v_c[b3][:, cc * 256:(cc + 1) * 256], start=True, stop=True),
                               reads=[('kw', p2), ('v_c', b3)], writes=['dCp'])
                        op('dve', lambda e: e.tensor_tensor(out=C_all[0:64, 2 * g2:2 * g2 + 2, :], in0=Cd[0:64, 2 * g2:2 * g2 + 2, :],
                                                            in1=dCp[0:64, :, 0:128], op=ALU.add),
                           reads=['Cd', 'dCp'], writes=['C_all'])
                        op('dve', lambda e: e.tensor_tensor(out=C_all[64:128, 2 * g2:2 * g2 + 2, :], in0=Cd[64:128, 2 * g2:2 * g2 + 2, :],
                                                            in1=dCp[64:128, :, 128:256], op=ALU.add),
                           reads=['Cd', 'dCp'], writes=['C_all'])
                    for cc in range(4):
                        op('pe', lambda e: e.matmul(misc[:, 32 + cc:33 + cc], lhsT=kw[p2][:, 2 * cc:2 * cc + 2, :].rearrange('p j d -> p (j d)'),
                                                    rhs=ones_bf[:, 0:1], start=True, stop=True),
                           reads=[('kw', p2), 'ones_bf'], writes=['misc'])
                    op('dve', lambda e: e.tensor_tensor(out=n_all[:], in0=nd[:], in1=misc[:, 32:36], op=ALU.add),
                       reads=['nd', 'misc'], writes=['n_all'])

                def back(c):
                    b3 = c % 4
                    p2 = c % 2
                    tsl = slice(c * 128, (c + 1) * 128)
                    ebp = gates[:, c, 1, :]
                    op('dve', lambda e: e.tensor_tensor(out=sm[:, 0:8], in0=den_sb[p2][:], in1=ebp, op=ALU.mult),
                       reads=[('den_sb', p2), ('gates', c, 1)], writes=['sm0'])
                    op('dve', lambda e: e.scalar_tensor_tensor(out=sm[:, 8:16], in0=sm[:, 0:8], scalar=-1.0, in1=sm[:, 0:8],
                                                               op0=ALU.mult, op1=ALU.max), reads=['sm0'], writes=['sm1'])
                    op('dve', lambda e: e.tensor_scalar(out=sm[:, 16:24], in0=sm[:, 8:16], scalar1=1.0, scalar2=None, op0=ALU.max),
                       reads=['sm1'], writes=['sm2'])
                    op('dve', lambda e: e.reciprocal(sm[:, 24:32], sm[:, 16:24]), reads=['sm2'], writes=['sm3'])
                    op('dve', lambda e: e.tensor_tensor(out=sm[:, 32:40], in0=ebp, in1=sm[:, 24:32], op=ALU.mult),
                       reads=['sm3', ('gates', c, 1)], writes=['sm4'])
                    for g2 in range(2):
                        op('act', lambda e: e.activation(out=sqb[:, g2 * 512:(g2 + 1) * 512],
                                                         in_=nump[p2][g2][:].rearrange('p h d -> p (h d)'), func=AF.Square),
                           reads=[('nump', p2, g2)], writes=[('sqb', g2)])
                    yield
                    op('dve', lambda e: e.tensor_reduce(out=sm[:, 40:48], in_=sqb[:].rearrange('p (h d) -> p h d', d=128),
                                                        axis=AX.X, op=ALU.add), reads=[('sqb', 0), ('sqb', 1)], writes=['sm5'])
                    op('dve', lambda e: e.tensor_tensor(out=sm[:, 48:56], in0=sm[:, 32:40], in1=sm[:, 32:40], op=ALU.mult),
                       reads=['sm4'], writes=['sm6'])
                    op('dve', lambda e: e.tensor_tensor(out=sm[:, 56:64], in0=sm[:, 48:56], in1=sm[:, 40:48], op=ALU.mult),
                       reads=['sm6', 'sm5'], writes=['sm7'])
                    op('act', lambda e: e.activation(out=sm[:, 64:72], in_=sm[:, 56:64], func=AF.Sqrt, scale=1.0 / 128, bias=EPS),
                       reads=['sm7'], writes=['sm8'])
                    op('dve', lambda e: e.reciprocal(sm[:, 72:80], sm[:, 64:72]), reads=['sm8'], writes=['sm9'])
                    op('dve', lambda e: e.tensor_tensor(out=sm[:, 80:88], in0=sm[:, 32:40], in1=sm[:, 72:80], op=ALU.mult),
                       reads=['sm9', 'sm4'], writes=['sm10'])
                    yield
                    for h in range(NH):
                        op('act', lambda e: e.activation(out=ya[:, h * 128:(h + 1) * 128], in_=nump[p2][h // 4][:, h % 4, :],
                                                         func=AF.Copy, scale=sm[:, 80 + h:81 + h]),
                           reads=[('nump', p2, h // 4), 'sm10'], writes=[('ya', h)])
                    yield
                    for h in range(NH):
                        op('pe', lambda e: e.transpose(tp[:, h, :], ya[:, h * 128:(h + 1) * 128], ident[:]),
                           reads=[('ya', h), 'ident'], writes=['tp'])
                    op('dve', lambda e: e.tensor_tensor(out=yag[b3][:], in0=tp[:], in1=ga_c[b3][:], op=ALU.mult),
                       reads=['tp', ('ga_c', b3)], writes=[('yag', b3)])
                    with nc.allow_non_contiguous_dma(reason='256B rows'):
                        dma('sp', yaT_d[:, :, tsl].rearrange('c p t -> p c t'), yag[b3][:], reads=[('yag', b3)])

                loads(0)
                for it in range(NT + 2):
                    if it + 1 < NT:
                        loads(it + 1)
                    gens = []
                    if 0 <= it - 2 < NT:
                        gens.append(back(it - 2))
                    if 0 <= it - 1 < NT:
                        gens.append(mid(it - 1))
                    if it < NT:
                        gens.append(front(it))
                    while gens:
                        for gen in list(gens):
                            try:
                                next(gen)
                            except StopIteration:
                                gens.remove(gen)
            if stop_after == 'C':
                break

            S_.new_phase()
            with ExitStack() as ph:
                knT = [sb(ph, 'knT%d' % i, [128, S], BF16) for i in range(2)]
                qnT = [sb(ph, 'qnT%d' % i, [128, S], BF16) for i in range(2)]
                Vh = [sb(ph, 'Vh%d' % i, [128, NT, 128], BF16) for i in range(2)]
                Eb = [sb(ph, 'Eb', [128, 2, 512], F32)] * 2
                Lp = [sb(ph, 'Lp%d' % i, [128, 2, 512], BF16) for i in range(2)]
                AT = [sb(ph, 'AT%d' % i, [128, 2, 512], BF16) for i in range(2)]
                gb_b = [sb(ph, 'gb_b%d' % i, [128, 512], BF16) for i in range(2)]
                ya_b = [sb(ph, 'ya_b%d' % i, [128, 512], BF16) for i in range(2)]
                c2l = [sb(ph, 'c2_%d' % i, [64, 512], BF16) for i in range(2)]
                ytmp = sb(ph, 'ytmp', [128, 512], F32)
                zA = [ps(ph, 'zA%d' % i, [128, 2, 512], F32) for i in range(3)]
                yacc = ps(ph, 'yacc', [128, 512], F32)
                csp = ps(ph, 'csp', [128, 512], F32)

                def load_head(h):
                    hp = h % 2
                    dma('sp', knT[hp][:], kbT_d[h, :, :], writes=[('knT', hp)])
                    dma('sp', qnT[hp][:], qbT_d[h, :, :], writes=[('qnT', hp)])
                    with nc.allow_non_contiguous_dma(reason='256B rows'):
                        dma('sp', Vh[hp][:], vb_d[:, h * 128:(h + 1) * 128].rearrange('(n p) d -> p n d', p=128),
                            writes=[('Vh', hp)])

                blocks = [(h, qb) for h in range(NH) for qb in range(NB)]

                def load_block(bi):
                    h, qb = blocks[bi]
                    yb = bi % 2
                    qsl = slice(qb * 512, (qb + 1) * 512)
                    dma('sp', gb_b[yb][:], gbT_d[h, :, qsl], writes=[('gb_b', yb)])
                    dma('sp', ya_b[yb][:], yaT_d[h, :, qsl], writes=[('ya_b', yb)])

                tiles = []
                for bi, (h, qb) in enumerate(blocks):
                    ktmax = 4 * qb + 3
                    for n, kt in enumerate(range(ktmax, -1, -2)):
                        tiles.append((bi, h, qb, n, kt))
                G = len(tiles)

                def emit_zA(g):
                    bi, h, qb, n, kt = tiles[g]
                    hp = h % 2
                    k3 = g % 3
                    for u in range(2):
                        ktu = kt - u
                        diag = ktu >= 4 * qb
                        op('pe', lambda e: e.matmul(zA[k3][:, u, :], lhsT=knT[hp][:, ktu * 128:(ktu + 1) * 128],
                                                    rhs=qnT[hp][:, qb * 512:(qb + 1) * 512], start=True, stop=(not diag)),
                           reads=[('knT', hp), ('qnT', hp)], writes=[('zA', k3, u)])
                        if diag:
                            op('pe', lambda e: e.matmul(zA[k3][:, u, :], lhsT=ident[:], rhs=maskneg[:, ktu - 4 * qb, :],
                                                        start=False, stop=True),
                               reads=['ident', 'maskneg'], writes=[('zA', k3, u)])

                def emit_tail(g):
                    bi, h, qb, n, kt = tiles[g]
                    hp = h % 2
                    yb = bi % 2
                    k3 = g % 3
                    op('act', lambda e: e.activation(out=AT[g % 2][:], in_=zA[k3][:], func=AF.Exp),
                       reads=[('zA', k3, 0), ('zA', k3, 1)], writes=[('AT', g % 2)])
                    for u in range(2):
                        op('pe', lambda e: e.matmul(yacc[:], lhsT=Vh[hp][:, kt - u, :], rhs=AT[g % 2][:, u, :],
                                                    start=(n == 0 and u == 0), stop=(kt - u == 0)),
                           reads=[('Vh', hp), ('AT', g % 2)], writes=['yacc'])
                    if kt - 1 == 0:
                        qsl = slice(qb * 512, (qb + 1) * 512)
                        op('dve', lambda e: e.tensor_tensor(out=ytmp[:], in0=yacc[:], in1=gb_b[yb][:], op=ALU.mult),
                           reads=['yacc', ('gb_b', yb)], writes=['ytmp'])
                        op('dve', lambda e: e.tensor_tensor(out=yT[:, h, qsl], in0=ytmp[:], in1=ya_b[yb][:], op=ALU.add),
                           reads=['ytmp', ('ya_b', yb)], writes=[('yT', qb)])

                for i in range(2):
                    op('dve', lambda e: e.memset(c2l[i][:], 0.0), writes=[('c2', i)])
                load_head(0)
                if NH > 1:
                    load_head(1)
                load_block(0)
                emit_zA(0)
                for g in range(G):
                    bi, h, qb, n, kt = tiles[g]
                    k3 = g % 3
                    last = (kt - 1 == 0)
                    if g + 1 < G:
                        emit_zA(g + 1)
                    op('act', lambda e: e.activation(out=Eb[g % 2][:], in_=zA[k3][:], func=AF.Exp),
                       reads=[('zA', k3, 0), ('zA', k3, 1)], writes=['Eb'])
                    op('act', lambda e: e.activation(out=Lp[g % 2][:], in_=Eb[g % 2][:], func=AF.Ln, bias=1.0),
                       reads=['Eb'], writes=[('Lp', g % 2)])
                    Lg = Lp[g % 2]
                    c2p, c2pk = c2l[(g - 1) % 2], ('c2', (g - 1) % 2)
                    c2n, c2nk = c2l[g % 2], ('c2', g % 2)
                    lk = ('Lp', g % 2)
                    if not last:
                        op('pe', lambda e: e.matmul(csp[0:64, :], lhsT=ones_bf[:, 0:64], rhs=Lg[:, 0, :], start=True, stop=False),
                           reads=['ones_bf', lk], writes=['csp'])
                        op('pe', lambda e: e.matmul(csp[0:64, :], lhsT=ones_bf[:, 0:64], rhs=Lg[:, 1, :], start=False, stop=(n == 0)),
                           reads=['ones_bf', lk], writes=['csp'])
                        if n > 0:
                            op('pe', lambda e: e.matmul(csp[0:64, :], lhsT=sel[0:64, 2, 0:64], rhs=c2p[:], start=False, stop=True),
                               reads=['sel', c2pk], writes=['csp'])
                    op('pe', lambda e: e.matmul(zA[k3][:, 0, :], lhsT=triNeg[:], rhs=Lg[:, 0, :], start=False, stop=(n == 0),
                                                skip_group_check=True),
                       reads=['triNeg', lk], writes=[('zA', k3, 0)])
                    op('pe', lambda e: e.matmul(zA[k3][:, 1, :], lhsT=triNeg[:], rhs=Lg[:, 1, :], start=False, stop=False,
                                                skip_group_check=True),
                       reads=['triNeg', lk], writes=[('zA', k3, 1)])
                    op('pe', lambda e: e.matmul(zA[k3][:, 1, :], lhsT=negones[:], rhs=Lg[:, 0, :], start=False, stop=(n == 0),
                                                skip_group_check=True),
                       reads=['negones', lk], writes=[('zA', k3, 1)])
                    if n > 0:
                        for u in range(2):
                            op('pe', lambda e: e.matmul(zA[k3][:, u, :], lhsT=sel[0:64, 0, :], rhs=c2p[:], start=False, stop=True,
                                                        skip_group_check=True),
                               reads=['sel', c2pk], writes=[('zA', k3, u)])
                    if not last:
                        op('dve', lambda e: e.tensor_copy(c2n[:], csp[0:64, :]), reads=['csp'], writes=[c2nk])
                        op('dve', lambda e: e.tensor_tensor(out=c2n[32:64, :], in0=csp[32:64, :], in1=c2n[32:64, :], op=ALU.subtract),
                           reads=['csp', c2nk], writes=[c2nk])
                    if g > 0:
                        emit_tail(g - 1)
                    if n == 0:
                        if bi + 1 < len(blocks):
                            load_block(bi + 1)
                        if qb == 0 and h >= 1 and h + 1 < NH:
                            load_head(h + 1)
                emit_tail(G - 1)
            if stop_after == 'D':
                break

            S_.new_phase()
            with ExitStack() as ph:
                gB = sb(ph, 'gB', [128, D], F32)
                dma('sp', gB[:], g2_d[l:l + 1, :].broadcast_to([128, D]), writes=['gB'])
                wo = sb(ph, 'wo', [128, 8, D], BF16)
                xt = [sb(ph, 'xt%d' % i, [128, D], F32) for i in range(2)]
                x1 = [sb(ph, 'x1%d' % i, [128, D], F32) for i in range(4)]
                junk = sb(ph, 'junk', [128, D], BF16)
                hn = [sb(ph, 'hn%d' % i, [128, D], BF16) for i in range(2)]
                stt = [sb(ph, 'stt%d' % i, [128, 4], F32) for i in range(4)]
                pO = [ps(ph, 'pO%d' % i, [128, 512], F32) for i in range(4)]
                tp = [ps(ph, 'tpE%d' % i, [128, 8, 128], BF16) for i in range(2)]
                dma('sp', wo[:], woutb_d.rearrange('(c p) n -> p c n', p=128), reads=[('woutb', r) for r in range(8)], writes=['wo'])

                def e_tile(i):
                    b = i % 2
                    b4 = i % 4
                    tsl = slice(i * 128, (i + 1) * 128)
                    dma('sp', xt[b][:], src_d[tsl, :], writes=[('xt', b)])
                    for half in range(2):
                        pk = 2 * b + half
                        for cc in range(8):
                            op('pe', lambda e: e.matmul(pO[pk][:], lhsT=yT[:, cc, tsl], rhs=wo[:, cc, half * 512:(half + 1) * 512],
                                                        start=(cc == 0), stop=(cc == 7)),
                               reads=[('yT', i // 4), 'wo'], writes=[('pO', pk)])
                        op('dve', lambda e: e.tensor_tensor(out=x1[b4][:, half * 512:(half + 1) * 512], in0=xt[b][:, half * 512:(half + 1) * 512],
                                                            in1=pO[pk][:], op=ALU.add),
                           reads=[('xt', b), ('pO', pk)], writes=[('x1', b4)])
                    dma('sp', xmid_d[tsl, :], x1[b4][:], reads=[('x1', b4)])
                    op('dve', lambda e: e.scalar_tensor_tensor(out=junk[:], in0=x1[b4][:], scalar=1.0, in1=x1[b4][:],
                                                               op0=ALU.mult, op1=ALU.mult, accum_out=stt[b4][:, 0:1]),
                       reads=[('x1', b4)], writes=['junk', ('stt', b4)])
                    yield
                    op('act', lambda e: e.activation(out=stt[b4][:, 1:2], in_=stt[b4][:, 0:1], func=AF.Sqrt,
                                                     scale=1.0 / D, bias=EPS), reads=[('stt', b4)], writes=[('stt', b4)])
                    yield
                    op('dve', lambda e: e.reciprocal(stt[b4][:, 2:3], stt[b4][:, 1:2]), reads=[('stt', b4)], writes=[('stt', b4)])
                    op('dve', lambda e: e.scalar_tensor_tensor(out=hn[b][:], in0=x1[b4][:], scalar=stt[b4][:, 2:3], in1=gB[:],
                                                               op0=ALU.mult, op1=ALU.mult),
                       reads=[('x1', b4), ('stt', b4), 'gB'], writes=[('hn', b)])
                    yield
                    for k in range(8):
                        op('pe', lambda e: e.transpose(tp[b][:, k, :], hn[b][:, k * 128:(k + 1) * 128], ident[:]),
                           reads=[('hn', b), 'ident'], writes=[('tp', b)])
                    op('act', lambda e: e.activation(out=hT[:, :, tsl], in_=tp[b][:], func=AF.Copy),
                       reads=[('tp', b)], writes=[('hT', i // 4)])

                run_skewed(e_tile, NT)
            if stop_after == 'E1':
                break

            S_.new_phase()
            with ExitStack() as ph:
                wu = [sb(ph, 'wu%d' % i, [128, 8, 512], BF16) for i in range(2)]
                rr = [sb(ph, 'rr%d' % i, [128, 512], F32) for i in range(2)]
                aT = sb(ph, 'aT', [128, 32, 512], BF16)
                x1t = [sb(ph, 'x1t%d' % i, [128, D], F32) for i in range(2)]
                pU = [ps(ph, 'pU%d' % i, [128, 512], F32) for i in range(2)]
                pD = [ps(ph, 'pD%d' % i, [128, 512], F32) for i in range(4)]
                for q4 in range(4):
                    dma('sp', wdn[:, q4 * 8:(q4 + 1) * 8, :], wdnb_d[q4 * 1024:(q4 + 1) * 1024, :].rearrange('(f p) n -> p f n', p=128),
                        reads=[('wdnb', r) for r in range(q4 * 8, q4 * 8 + 8)], writes=[('wdn', q4)])
                nu = 0
                nx = 0
                for tb in range(NB):
                    for g in range(8):
                        wbb = (tb * 8 + g) % 2
                        dma('sp', wu[wbb][:], wupb_d[:, g * 512:(g + 1) * 512].rearrange('(kc p) n -> p kc n', p=128),
                            reads=[('wupb', r) for r in range(8)], writes=[('wu', wbb)])
                        for f in range(4):
                            fc = 4 * g + f
                            pb = nu % 2
                            nu += 1
                            for kc in range(8):
                                op('pe', lambda e: e.matmul(pU[pb][:], lhsT=wu[wbb][:, kc, f * 128:(f + 1) * 128],
                                                            rhs=hT[:, kc, tb * 512:(tb + 1) * 512], start=(kc == 0), stop=(kc == 7)),
                                   reads=[('wu', wbb), ('hT', tb)], writes=[('pU', pb)])
                            op('act', lambda e: e.activation(out=rr[pb][:], in_=pU[pb][:], func=AF.Relu),
                               reads=[('pU', pb)], writes=[('rr', pb)])
                            op('pool', lambda e: e.tensor_tensor(out=aT[:, fc, :], in0=rr[pb][:], in1=rr[pb][:], op=ALU.mult),
                               reads=[('rr', pb)], writes=[('aT', fc)])
                    for ts in range(4):
                        i = tb * 4 + ts
                        xb = nx % 2
                        nx += 1
                        tsl = slice(i * 128, (i + 1) * 128)
                        dma('sp', x1t[xb][:], xmid_d[tsl, :], writes=[('x1t', xb)])
                        for half in range(2):
                            pk = 2 * xb + half
                            for fc in range(32):
                                op('pe', lambda e: e.matmul(pD[pk][:], lhsT=aT[:, fc, ts * 128:(ts + 1) * 128],
                                                            rhs=wdn[:, fc, half * 512:(half + 1) * 512], start=(fc == 0), stop=(fc == 31)),
                                   reads=[('aT', fc), ('wdn', fc // 8)], writes=[('pD', pk)])
                            op('dve', lambda e: e.tensor_tensor(out=x1t[xb][:, half * 512:(half + 1) * 512],
                                                                in0=x1t[xb][:, half * 512:(half + 1) * 512], in1=pD[pk][:], op=ALU.add),
                               reads=[('x1t', xb), ('pD', pk)], writes=[('x1t', xb)])
                        dma('sp', dst_d[tsl, :], x1t[xb][:], reads=[('x1t', xb)])
        S_.finish('sp')
    return nc


_CACHE = {}


def kernel(**inputs):
    x = np.ascontiguousarray(np.asarray(inputs['x'], dtype=np.float32))
    B, S, _ = x.shape
    L = int(np.asarray(inputs['w_in']).shape[0])
    key = (S, L)
    if key not in _CACHE:
        _CACHE[key] = build(S, L)
    nc = _CACHE[key]
    names = ['norm_mix_g', 'w_in', 'b_if', 'b_gate', 'conv_w', 'mlstm_norm_g', 'sb_q_norm_g', 'sb_k_norm_g',
             'w_out', 'norm_mlp_g', 'w_up', 'w_down']
    shared = {k: np.ascontiguousarray(np.asarray(inputs[k], dtype=np.float32)) for k in names}
    n = 8
    in_maps = []
    for c in range(n):
        m = dict(shared)
        m['x'] = x[c % B]
        in_maps.append(m)
    res = run_bass_kernel_spmd(nc, in_maps, core_ids=list(range(n)))
    return np.stack([res.results[b]['out'] for b in range(B)], axis=0).astype(np.float32)
```
